# Optimizing a Trainium2 kernel written in Bass

```python
import math
import jax, jax.numpy as jnp
from jax import lax
import numpy as np

D_MODEL = 1024
BATCH = 8
SEQ = 4096
DEPTH = 4

N_MIXERS = 2
SSM_EXPAND = 2
D_INNER = SSM_EXPAND * D_MODEL
HEAD_DIM = 64
N_SSM_HEADS = D_INNER // HEAD_DIM
N_GROUPS = 4
HEADS_PER_GROUP = N_SSM_HEADS // N_GROUPS
D_STATE = 128
SSM_CONV = 4
CONV_DIM = D_INNER + 2 * N_GROUPS * D_STATE
IN_PROJ_DIM = 2 * D_INNER + 2 * N_GROUPS * D_STATE + N_SSM_HEADS
CHUNK = 128
CF_KERNEL = 31
N_MEM = 256
XA_HEADS = 4
XA_HEAD_DIM = D_MODEL // XA_HEADS
D_FF = 2816
FFN_CONV = 3
EPS = 1e-6

kernel_name = "hybrid_ssd_conformer_memxattn_convffn"


def rmsnorm(x, g):
    xf = x.astype(jnp.float32)
    y = xf * lax.rsqrt(jnp.mean(xf * xf, axis=-1, keepdims=True) + EPS)
    return (y * g.astype(jnp.float32)).astype(x.dtype)


def layernorm(x, g, b):
    xf = x.astype(jnp.float32)
    mu = jnp.mean(xf, axis=-1, keepdims=True)
    var = jnp.mean(jnp.square(xf - mu), axis=-1, keepdims=True)
    y = (xf - mu) * lax.rsqrt(var + EPS)
    return (y * g.astype(jnp.float32) + b.astype(jnp.float32)).astype(x.dtype)


def causal_dwconv(x, w, b):
    K, C = w.shape
    y = lax.conv_general_dilated(
        x, w[:, None, :].astype(x.dtype), window_strides=(1,), padding=[(K - 1, 0)],
        dimension_numbers=("NWC", "WIO", "NWC"), feature_group_count=C)
    return y + b.astype(x.dtype)


def ssd_mixer(h, in_w, conv_w, conv_b, dt_bias, A_log, D_skip, norm_g, out_w):
    Bsz, L, _ = h.shape
    nc = L // CHUNK
    G, Hg, P, N, Q = N_GROUPS, HEADS_PER_GROUP, HEAD_DIM, D_STATE, CHUNK
    f32 = jnp.float32
    proj = h @ in_w
    z, xBC, dt_raw = jnp.split(proj, [D_INNER, D_INNER + CONV_DIM], axis=-1)
    xBC = jax.nn.silu(causal_dwconv(xBC, conv_w, conv_b))
    xs, Bm, Cm = jnp.split(xBC, [D_INNER, D_INNER + G * N], axis=-1)
    xs = xs.astype(f32).reshape(Bsz, nc, Q, G, Hg, P)
    Bm = Bm.astype(f32).reshape(Bsz, nc, Q, G, N)
    Cm = Cm.astype(f32).reshape(Bsz, nc, Q, G, N)
    dt = jax.nn.softplus(dt_raw.astype(f32) + dt_bias.astype(f32)).reshape(Bsz, nc, Q, G, Hg)
    A = -jnp.exp(A_log.astype(f32)).reshape(G, Hg)
    cs = jnp.cumsum(dt * A, axis=2)
    xdt = xs * dt[..., None]
    tri = jnp.tril(jnp.ones((Q, Q), dtype=bool))
    seg = cs[:, :, :, None] - cs[:, :, None, :]
    decay = jnp.exp(jnp.where(tri[None, None, :, :, None, None], seg, -jnp.inf))
    CB = jnp.einsum("bcign,bcjgn->bcijg", Cm, Bm)
    y_diag = jnp.einsum("bcijgh,bcjghp->bcighp", CB[..., None] * decay, xdt)
    decay_to_end = jnp.exp(cs[:, :, -1:] - cs)
    states = jnp.einsum("bcjgn,bcjgh,bcjghp->bcghpn", Bm, decay_to_end, xdt)
    chunk_decay = jnp.exp(cs[:, :, -1])

    def step(carry, inp):
        st, dec = inp
        return carry * dec[..., None, None] + st, carry

    h0 = jnp.zeros((Bsz, G, Hg, P, N), f32)
    _, prev = lax.scan(step, h0, (jnp.swapaxes(states, 0, 1), jnp.swapaxes(chunk_decay, 0, 1)))
    prev = jnp.swapaxes(prev, 0, 1)
    y_off = jnp.einsum("bcign,bcghpn,bcigh->bcighp", Cm, prev, jnp.exp(cs))
    y = y_diag + y_off + xs * D_skip.astype(f32).reshape(G, Hg)[:, :, None]
    y = y.reshape(Bsz, L, D_INNER).astype(h.dtype)
    y = rmsnorm(y * jax.nn.silu(z), norm_g)
    return y @ out_w


def conformer_conv(h, pw1_w, pw1_b, dw_w, dw_b, ln_g, ln_b, pw2_w, pw2_b):
    u = h @ pw1_w + pw1_b
    a, gt = jnp.split(u, 2, axis=-1)
    c = causal_dwconv(a * jax.nn.sigmoid(gt), dw_w, dw_b)
    c = jax.nn.silu(layernorm(c, ln_g, ln_b))
    return c @ pw2_w + pw2_b


def mem_xattn(h, mem, mem_g, q_w, kv_w, o_w):
    Bsz, L, _ = h.shape
    m = rmsnorm(mem, mem_g)
    q = (h @ q_w).reshape(Bsz, L, XA_HEADS, XA_HEAD_DIM)
    k, v = jnp.split(m @ kv_w, 2, axis=-1)
    k = k.reshape(Bsz, N_MEM, XA_HEADS, XA_HEAD_DIM)
    v = v.reshape(Bsz, N_MEM, XA_HEADS, XA_HEAD_DIM)
    s = jnp.einsum("blhd,bmhd->bhlm", q, k).astype(jnp.float32) * (XA_HEAD_DIM ** -0.5)
    p = jax.nn.softmax(s, axis=-1).astype(v.dtype)
    o = jnp.einsum("bhlm,bmhd->blhd", p, v).reshape(Bsz, L, D_MODEL)
    return o @ o_w


def conv_ffn(h, in_w, conv_w, conv_b, out_w):
    u = causal_dwconv(h @ in_w, conv_w, conv_b)
    g, v = jnp.split(u, 2, axis=-1)
    return (jax.nn.silu(g) * v) @ out_w


def setup_inputs(seed: int = 0) -> dict:
    key = jax.random.key(seed)
    ks = jax.random.split(key, 32)
    nA = (DEPTH + 1) // 2
    nB = DEPTH // 2
    D = D_MODEL

    def w(k, shape, fan_in):
        return jax.random.normal(k, shape, jnp.float32) * (fan_in ** -0.5)

    def gain(k, shape):
        return 1.0 + 0.02 * jax.random.normal(k, shape, jnp.float32)

    def bias(k, shape):
        return 0.02 * jax.random.normal(k, shape, jnp.float32)

    log_dt = jax.random.uniform(ks[5], (nA, N_SSM_HEADS), jnp.float32, math.log(1e-3), math.log(1e-1))
    dt0 = jnp.exp(log_dt)
    return {
        "x": jax.random.normal(ks[0], (BATCH, SEQ, D), jnp.float32),
        "mem": jax.random.normal(ks[1], (BATCH, N_MEM, D), jnp.float32),
        "norm_g": gain(ks[2], (DEPTH, 6, D)),
        "ssm_in_w": w(ks[3], (nA, D, IN_PROJ_DIM), D),
        "ssm_conv_w": w(ks[4], (nA, SSM_CONV, CONV_DIM), SSM_CONV),
        "ssm_conv_b": bias(ks[6], (nA, CONV_DIM)),
        "ssm_dt_bias": dt0 + jnp.log(-jnp.expm1(-dt0)),
        "ssm_A_log": jnp.log(jax.random.uniform(ks[7], (nA, N_SSM_HEADS), jnp.float32, 1.0, 16.0)),
        "ssm_D": 1.0 + 0.1 * jax.random.normal(ks[8], (nA, N_SSM_HEADS), jnp.float32),
        "ssm_norm_g": gain(ks[9], (nA, D_INNER)),
        "ssm_out_w": w(ks[10], (nA, D_INNER, D), D_INNER),
        "cf_pw1_w": w(ks[11], (nB, D, 2 * D), D),
        "cf_pw1_b": bias(ks[12], (nB, 2 * D)),
        "cf_dw_w": w(ks[13], (nB, CF_KERNEL, D), CF_KERNEL),
        "cf_dw_b": bias(ks[14], (nB, D)),
        "cf_ln_g": gain(ks[15], (nB, D)),
        "cf_ln_b": bias(ks[16], (nB, D)),
        "cf_pw2_w": w(ks[17], (nB, D, D), D),
        "cf_pw2_b": bias(ks[18], (nB, D)),
        "xa_mem_g": gain(ks[19], (DEPTH, D)),
        "xa_q_w": w(ks[20], (DEPTH, D, D), D),
        "xa_kv_w": w(ks[21], (DEPTH, D, 2 * D), D),
        "xa_o_w": w(ks[22], (DEPTH, D, D), D),
        "ffn_in_w": w(ks[23], (DEPTH, D, 2 * D_FF), D),
        "ffn_conv_w": w(ks[24], (DEPTH, FFN_CONV, 2 * D_FF), FFN_CONV),
        "ffn_conv_b": bias(ks[25], (DEPTH, 2 * D_FF)),
        "ffn_out_w": w(ks[26], (DEPTH, D_FF, D), D_FF),
    }


def reference(x, mem, norm_g,
              ssm_in_w, ssm_conv_w, ssm_conv_b, ssm_dt_bias, ssm_A_log, ssm_D, ssm_norm_g, ssm_out_w,
              cf_pw1_w, cf_pw1_b, cf_dw_w, cf_dw_b, cf_ln_g, cf_ln_b, cf_pw2_w, cf_pw2_b,
              xa_mem_g, xa_q_w, xa_kv_w, xa_o_w,
              ffn_in_w, ffn_conv_w, ffn_conv_b, ffn_out_w):
    for i in range(DEPTH):
        g = norm_g[i]
        j = i // N_MIXERS
        h = rmsnorm(x, g[0])
        if i % N_MIXERS == 0:
            mix = ssd_mixer(h, ssm_in_w[j], ssm_conv_w[j], ssm_conv_b[j], ssm_dt_bias[j],
                            ssm_A_log[j], ssm_D[j], ssm_norm_g[j], ssm_out_w[j])
        else:
            mix = conformer_conv(h, cf_pw1_w[j], cf_pw1_b[j], cf_dw_w[j], cf_dw_b[j],
                                 cf_ln_g[j], cf_ln_b[j], cf_pw2_w[j], cf_pw2_b[j])
        x = x + rmsnorm(mix, g[1])
        a = mem_xattn(rmsnorm(x, g[2]), mem, xa_mem_g[i], xa_q_w[i], xa_kv_w[i], xa_o_w[i])
        x = x + rmsnorm(a, g[3])
        f = conv_ffn(rmsnorm(x, g[4]), ffn_in_w[i], ffn_conv_w[i], ffn_conv_b[i], ffn_out_w[i])
        x = x + rmsnorm(f, g[5])
    return x
```

```python
from contextlib import ExitStack
import numpy as np
import concourse.bass as bass
import concourse.mybir as mybir
from concourse.bass_utils import run_bass_kernel_spmd

F32 = mybir.dt.float32
BF16 = mybir.dt.bfloat16
AF = mybir.ActivationFunctionType
ALU = mybir.AluOpType

D = 1024
L = 4096
NL = 4
DI = 2048
NH = 32
NG = 4
CONVD = 3072
INP = 5152
DFF = 2816
NM = 256
CFK = 31
EPS = 1e-6
ENGS = ("pe", "act", "dve", "pool", "sp")
SYNC_SMALL_ONLY = False


class Sched:
    def __init__(self, nc, same_engine_sync=True, n_dma_sems=28, n_pool_dma_sems=12):
        self.nc = nc
        self.same = same_engine_sync
        self.prog = {e: [] for e in ENGS}
        self.cnt = {e: 0 for e in ENGS}
        self.known = {e: {} for e in ENGS}
        self.state = {}
        self.sems = {}
        self._ctx = []
        for e in ("pe", "act", "dve", "pool"):
            self.sems[e] = self._sem("s_" + e)
        self.dma_pool = {"sp": [], "pool": []}
        for i in range(n_dma_sems):
            k = ("dsp", i); self.sems[k] = self._sem("d_sp%d" % i); self.dma_pool["sp"].append(k)
        for i in range(n_pool_dma_sems):
            k = ("dpl", i); self.sems[k] = self._sem("d_pl%d" % i); self.dma_pool["pool"].append(k)
        self.dma_cnt = {k: 0 for q in self.dma_pool.values() for k in q}
        self.dma_rr = {"sp": 0, "pool": 0}

    def _sem(self, name):
        cm = self.nc.semaphore(name)
        h = cm.__enter__()
        self._ctx.append(cm)
        return h

    def _need(self, eng, needs, key, val, small=True):
        if key == eng and (eng == "pe" or not self.same or (SYNC_SMALL_ONLY and not small)):
            return
        if self.known[eng].get(key, 0) >= val:
            return
        if needs.get(key, 0) < val:
            needs[key] = val

    def _deps(self, eng, r, w):
        needs = {}
        for t in r:
            st = self.state.get(t)
            if st and st[0]:
                self._need(eng, needs, st[0][0], st[0][1], st[2])
        for t in w:
            st = self.state.get(t)
            if st:
                if st[0]:
                    self._need(eng, needs, st[0][0], st[0][1], st[2])
                for k, v in st[1].items():
                    self._need(eng, needs, k, v)
        return needs

    def _emit_waits(self, eng, needs):
        for k, v in needs.items():
            self.known[eng][k] = v
            sem = self.sems[k]
            self.prog[eng].append(lambda e, sem=sem, v=v: e.wait_ge(sem, v))

    def _commit(self, r, w, key, val, small=True):
        for t in r:
            st = self.state.setdefault(t, [None, {}, True])
            if st[1].get(key, 0) < val:
                st[1][key] = val
        for t in w:
            self.state[t] = [(key, val), {}, small]

    def op(self, eng, fn, r=(), w=(), small=False):
        needs = self._deps(eng, r, w)
        self._emit_waits(eng, needs)
        self.cnt[eng] += 1
        val = self.cnt[eng]
        sem = self.sems[eng]
        self.prog[eng].append(lambda e, fn=fn, sem=sem: fn(e).then_inc(sem, 1))
        self._commit(r, w, eng, val, small)

    def dma(self, q, out, in_, r=(), w=()):
        pool = self.dma_pool[q]
        k = pool[self.dma_rr[q] % len(pool)]
        self.dma_rr[q] += 1
        needs = self._deps(q, r, w)
        prev = 16 * self.dma_cnt[k]
        if prev:
            self._need(q, needs, k, prev)
        self._emit_waits(q, needs)
        self.dma_cnt[k] += 1
        val = 16 * self.dma_cnt[k]
        sem = self.sems[k]
        self.prog[q].append(lambda e, out=out, in_=in_, sem=sem: e.dma_start(out=out, in_=in_).then_inc(sem, 16))
        self._commit(r, w, k, val)

    def barrier(self):
        cur = {e: self.cnt[e] for e in ("pe", "act", "dve", "pool")}
        for k, c in self.dma_cnt.items():
            cur[k] = 16 * c
        for eng in ENGS:
            needs = {}
            for k, v in cur.items():
                if v and k != eng:
                    self._need(eng, needs, k, v)
            self._emit_waits(eng, needs)
        self.state = {}

    def finish(self):
        self.barrier()
        prog = self.prog
        with self.nc.Block() as block:
            @block.tensor
            def _(e):
                for f in prog["pe"]:
                    f(e)

            @block.scalar
            def _(e):
                for f in prog["act"]:
                    f(e)

            @block.vector
            def _(e):
                for f in prog["dve"]:
                    f(e)

            @block.gpsimd
            def _(e):
                for f in prog["pool"]:
                    f(e)

            @block.sync
            def _(e):
                for f in prog["sp"]:
                    f(e)
        for cm in reversed(self._ctx):
            cm.__exit__(None, None, None)


def vec_layout():
    off = {}
    n = 0

    def add(name, cols):
        nonlocal n
        off[name] = n
        n += cols

    for l in range(NL):
        for j in range(6):
            add(("ng", l, j), 8)
        add(("mg", l), 8)
        add(("fcw", l), 3 * 44)
        add(("fcb", l), 44)
    for j in range(2):
        add(("scw", j), 4 * 24)
        add(("scb", j), 24)
        add(("sng", j), 16)
        add(("pw1b", j), 16)
        add(("dww", j), CFK * 8)
        add(("dwb", j), 8)
        add(("lng", j), 8)
        add(("lnb", j), 8)
        add(("pw2b", j), 8)
    return off, n


def _cols(v):
    v = np.asarray(v, dtype=np.float32)
    if v.ndim == 1:
        return np.ascontiguousarray(v.reshape(-1, 128).T)
    k, c = v.shape
    return np.ascontiguousarray(v.reshape(k, c // 128, 128).transpose(2, 0, 1).reshape(128, k * (c // 128)))


def build_vecs(inp):
    off, n = vec_layout()
    vecs = np.zeros((128, n), np.float32)

    def put(name, v):
        c = _cols(v)
        vecs[:, off[name]:off[name] + c.shape[1]] = c

    for l in range(NL):
        for j in range(6):
            put(("ng", l, j), inp["norm_g"][l, j])
        put(("mg", l), inp["xa_mem_g"][l])
        put(("fcw", l), inp["ffn_conv_w"][l])
        put(("fcb", l), inp["ffn_conv_b"][l])
    for j in range(2):
        put(("scw", j), inp["ssm_conv_w"][j])
        put(("scb", j), inp["ssm_conv_b"][j])
        put(("sng", j), inp["ssm_norm_g"][j])
        put(("pw1b", j), inp["cf_pw1_b"][j])
        put(("dww", j), inp["cf_dw_w"][j])
        put(("dwb", j), inp["cf_dw_b"][j])
        put(("lng", j), inp["cf_ln_g"][j])
        put(("lnb", j), inp["cf_ln_b"][j])
        put(("pw2b", j), inp["cf_pw2_b"][j])
    bv = np.zeros((128, 2 * 96), np.float32)
    for j in range(2):
        bv[:, j * 96 + 0:j * 96 + 32] = np.asarray(inp["ssm_dt_bias"][j], np.float32)[None, :]
        bv[:, j * 96 + 32:j * 96 + 64] = np.asarray(inp["ssm_A_log"][j], np.float32)[None, :]
        bv[:, j * 96 + 64:j * 96 + 96] = np.asarray(inp["ssm_D"][j], np.float32)[None, :]
    consts = np.zeros((128, 512), np.float32)
    consts[:, 0:128] = np.eye(128, dtype=np.float32)
    consts[:, 128:256] = np.triu(np.ones((128, 128), np.float32))
    consts[:, 256:384] = np.tril(np.ones((128, 128), np.float32), -1)
    consts[:, 384:512] = 1.0
    return vecs, bv, consts


WEIGHT_SHAPES = {
    "ssm_in_w": [2, D, INP], "ssm_out_w": [2, DI, D],
    "cf_pw1_w": [2, D, 2 * D], "cf_pw2_w": [2, D, D],
    "xa_q_w": [NL, D, D], "xa_kv_w": [NL, D, 2 * D], "xa_o_w": [NL, D, D],
    "ffn_in_w": [NL, D, 2 * DFF], "ffn_out_w": [NL, DFF, D],
}

ALL_STAGES = [(l, s) for l in range(NL) for s in ("mix", "xa", "ffn")]


def build_program(stages=None, same_engine_sync=True, L=L):
    stages = ALL_STAGES if stages is None else stages
    nc = bass.Bass("TRN2", target_bir_lowering=False)
    S = Sched(nc, same_engine_sync=same_engine_sync)
    voff, NV = vec_layout()

    xT = nc.dram_tensor("xT", [D, L], F32, kind="ExternalInput").ap()
    memT = nc.dram_tensor("memT", [D, NM], F32, kind="ExternalInput").ap()
    vecs_d = nc.dram_tensor("vecs", [128, NV], F32, kind="ExternalInput").ap()
    bvecs_d = nc.dram_tensor("bvecs", [128, 192], F32, kind="ExternalInput").ap()
    consts_d = nc.dram_tensor("consts", [128, 512], F32, kind="ExternalInput").ap()
    W = {k: nc.dram_tensor(k, shp, F32, kind="ExternalInput").ap() for k, shp in WEIGHT_SHAPES.items()}
    y = nc.dram_tensor("y", [D, L], F32, kind="ExternalOutput").ap()
    ZS = nc.dram_tensor("zs_scr", [L, DI], BF16).ap()
    XBC = nc.dram_tensor("xbc_scr", [CONVD, L], BF16).ap()
    DTA = nc.dram_tensor("dta_scr", [L, 64], F32).ap()

    vecs = nc.alloc_sbuf_tensor("vecs_sb", [128, NV], F32)
    cst = nc.alloc_sbuf_tensor("cst_sb", [128, 512], F32)
    ident_bf = nc.alloc_sbuf_tensor("ident_bf", [128, 128], BF16)
    ones_bf = nc.alloc_sbuf_tensor("ones_bf", [128, 128], BF16)
    onesD_bf = nc.alloc_sbuf_tensor("onesD_bf", [128, 128], BF16)
    epsc = nc.alloc_sbuf_tensor("epsc", [128, 1], F32)
    onec = nc.alloc_sbuf_tensor("onec", [128, 1], F32)
    ident_f = cst[:, 0:128]
    U_f = cst[:, 128:256]
    Lm_f = cst[:, 256:384]
    ones_f = cst[:, 384:512]

    def _small(ap):
        try:
            return ap.free_size() < 128
        except Exception:
            return True

    def mm(out, lhsT, rhs, start, stop, r, w):
        S.op("pe", lambda e: e.matmul(out, lhsT=lhsT, rhs=rhs, start=start, stop=stop), r, w, small=_small(out))

    def tr(out, in_, r, w):
        S.op("pe", lambda e: e.transpose(out, in_, ident_bf[:]), r, w, small=_small(out))

    def act(out, in_, func, r, w, bias=None, scale=None, accum=None):
        kw = {}
        if bias is not None:
            kw["bias"] = bias
        if scale is not None:
            kw["scale"] = scale
        if accum is not None:
            kw["accum_out"] = accum
        S.op("act", lambda e: e.activation(out=out, in_=in_, func=func, **kw), r, w,
             small=(_small(out) or accum is not None))

    def tt(eng, out, in0, in1, op, r, w):
        S.op(eng, lambda e: e.tensor_tensor(out=out, in0=in0, in1=in1, op=op), r, w, small=_small(out))

    def ts(eng, out, in0, s1, s2, op0, op1, r, w):
        if op1 is None:
            S.op(eng, lambda e: e.tensor_scalar(out=out, in0=in0, scalar1=s1, scalar2=None, op0=op0), r, w, small=_small(out))
        else:
            S.op(eng, lambda e: e.tensor_scalar(out=out, in0=in0, scalar1=s1, scalar2=s2, op0=op0, op1=op1), r, w, small=_small(out))

    def stt(eng, out, in0, scalar, in1, op0, op1, r, w):
        S.op(eng, lambda e: e.scalar_tensor_tensor(out=out, in0=in0, scalar=scalar, in1=in1, op0=op0, op1=op1), r, w, small=_small(out))

    def rsqrt_act(out, in_, scale, r, w):
        act(out, in_, AF.Ln, r=list(r) + ["cbf"], w=w, bias=epsc[:], scale=scale)
        act(out, out, AF.Exp, r=w, w=w, scale=-0.5)

    def cp(eng, out, in_, r, w):
        S.op(eng, lambda e: e.tensor_copy(out=out, in_=in_), r, w, small=_small(out))

    def mset(eng, ap, val, w):
        S.op(eng, lambda e: e.memset(ap, val), (), w, small=_small(ap))

    def vcol(name, c):
        o = voff[name] + c
        return vecs[:, o:o + 1]

    def xr_ap(dram, t0, tl):
        return dram.rearrange("(kt p) l -> p kt l", p=128)[:, :, t0:t0 + tl]

    def xr_tok(t0, tl):
        return [("xr", i) for i in range(t0 // 256, (t0 + tl) // 256)]

    S.dma("sp", vecs[:], vecs_d, w=["vecs"])
    S.dma("sp", cst[:], consts_d, w=["cst"])
    cp("dve", ident_bf[:], ident_f, r=["cst"], w=["cbf"])
    cp("dve", ones_bf[:], ones_f, r=["cst"], w=["cbf"])
    ts("dve", onesD_bf[:], ones_f, 1.0 / D, None, ALU.mult, None, r=["cst"], w=["cbf"])
    mset("dve", epsc[:], EPS, w=["cbf"])
    mset("dve", onec[:], 1.0, w=["cbf"])
    S.barrier()

    class Stage:
        n = 0

        def __init__(self):
            self.es = ExitStack()
            Stage.n += 1
            self.sfx = "_s%d" % Stage.n

        def sb(self, name, shape, dt):
            return self.es.enter_context(nc.sbuf_tensor(name + self.sfx, shape, dt))

        def ps(self, name, shape, dt=F32):
            return self.es.enter_context(nc.psum_tensor(name + self.sfx, shape, dt))

        def close(self):
            S.barrier()
            self.es.close()

    def prenorm(xa, xtok, gname, h, TT, sq, ssps, rstd, htok="h"):
        for kt in range(8):
            act(sq[:, kt, :TT], xa[:, kt, :TT], AF.Square, r=[xtok], w=[("sq", kt)])
        for kt in range(8):
            mm(ssps[:, :TT], onesD_bf[:], sq[:, kt, :TT], kt == 0, kt == 7, r=[("sq", kt)], w=["ssps"])
        rsqrt_act(rstd[:, :TT], ssps[:, :TT], 1.0, r=["ssps"], w=["rstd"])
        for kt in range(8):
            stt("dve", h[:, kt, :TT], xa[:, kt, :TT], vcol(gname, kt), rstd[:, :TT],
                ALU.mult, ALU.mult, r=[xtok, "rstd"], w=[(htok, kt)])

    def prenorm_sq(xa, xtok, TT, sq):
        for kt in range(8):
            act(sq[:, kt, :TT], xa[:, kt, :TT], AF.Square, r=[xtok], w=[("sq", kt)])

    def prenorm_ss(TT, sq, ssps, rstd):
        for kt in range(8):
            mm(ssps[:, :TT], onesD_bf[:], sq[:, kt, :TT], kt == 0, kt == 7, r=[("sq", kt)], w=["ssps"])
        rsqrt_act(rstd[:, :TT], ssps[:, :TT], 1.0, r=["ssps"], w=["rstd"])

    def prenorm_h(xa, xtok, gname, h, TT, rstd, htok="h"):
        for kt in range(8):
            stt("dve", h[:, kt, :TT], xa[:, kt, :TT], vcol(gname, kt), rstd[:, :TT],
                ALU.mult, ALU.mult, r=[xtok, "rstd"], w=[(htok, kt)])

    def post_evac(dt, fps, ftok, TT, fsb, sq, bias=None, pp=""):
        act(fsb[:, dt, :TT], fps, AF.Identity, r=[ftok], w=[("fsb", dt)], bias=bias)
        act(sq[:, dt, :TT], fps, AF.Square, r=[ftok], w=[(pp + "sq", dt)], bias=bias)

    def post_finish(gname, xb, TT, t0, fsb, sq, ssps, rstd, pp="", sstok=None, add_eng="pool"):
        sstok = sstok or (pp + "ssps")
        for kt in range(8):
            mm(ssps[:, :TT], onesD_bf[:], sq[:, kt, :TT], kt == 0, kt == 7, r=[(pp + "sq", kt)], w=[sstok])
        rsqrt_act(rstd[:, :TT], ssps[:, :TT], 1.0, r=[sstok], w=[pp + "rstd"])
        for kt in range(8):
            stt("dve", fsb[:, kt, :TT], fsb[:, kt, :TT], vcol(gname, kt), rstd[:, :TT], ALU.mult, ALU.mult,
                r=[("fsb", kt), pp + "rstd"], w=[("fsb", kt)])
            if add_eng == "pool":
                tt("pool", fsb[:, kt, :TT], fsb[:, kt, :TT], xb[:, kt, :TT], ALU.add, r=[("fsb", kt), "xb"], w=[("fsb", kt)])
        if add_eng != "pool":
            for kt in range(8):
                tt(add_eng, fsb[:, kt, :TT], fsb[:, kt, :TT], xb[:, kt, :TT], ALU.add, r=[("fsb", kt), "xb"], w=[("fsb", kt)])
        S.dma("sp", xr_ap(y, t0, TT), fsb[:, :, :TT], r=[("fsb", kt) for kt in range(8)], w=xr_tok(t0, TT))

    def load_w(dst, src2d, nk, tokname, per=1):
        for k0 in range(0, nk, per):
            k1 = min(nk, k0 + per)
            S.dma("pool", dst[:, k0:k1, :], src2d[k0 * 128:k1 * 128, :].rearrange("(k p) f -> p k f", p=128),
                  w=[(tokname, k) for k in range(k0, k1)])

    def load_wc(dst, src2d, nk, tokname, groups):
        for gi, (c0, c1) in groups:
            S.dma("pool", dst[:, :, c0:c1], src2d[:, c0:c1].rearrange("(k p) f -> p k f", p=128), w=[(tokname, gi)])

    def gtok(tokname, groups, col):
        for gi, (c0, c1) in groups:
            if c0 <= col < c1:
                return (tokname, gi)
        raise ValueError(col)

    def stage_ffn(l, xsrc):
        TT = 256
        st = Stage()
        w1 = st.sb("w1", [128, 8, 2 * DFF], BF16)
        w2 = st.sb("w2", [128, 22, D], BF16)
        xa = st.sb("xa", [128, 8, TT], F32)
        xb = st.sb("xb", [128, 8, TT], F32)
        sq = st.sb("sq", [128, 8, TT], BF16)
        sq2 = st.sb("sq2", [128, 8, TT], BF16)
        h = st.sb("h", [128, 8, TT + 2], BF16)
        hv = h[:, :, 2:2 + TT]
        rstd = st.sb("rstd", [128, TT], F32)
        rstd2 = st.sb("rstd2", [128, TT], F32)
        ga = st.sb("ga", [128, 22, TT], BF16)
        fsb = st.sb("fsb", [128, 8, TT], F32)
        acc = [[st.sb("acc%d%d" % (a, b), [128, TT], F32) for b in range(2)] for a in range(2)]
        sg = [st.sb("sg%d" % b, [128, TT], F32) for b in range(2)]
        ups = [[st.ps("ups%d%d" % (a, b), [128, 512]) for b in range(2)] for a in range(2)]
        fps = [st.ps("fps%d" % b, [128, 512]) for b in range(2)]
        ssps = st.ps("ssps", [128, 512])
        ssps2 = st.ps("ssps2", [128, 512])
        w1g = []
        for (f0, f1) in ((0, 3), (3, 9), (9, 15), (15, 22)):
            for a in range(2):
                w1g.append((len(w1g), (a * DFF + f0 * 128, a * DFF + f1 * 128)))
        load_wc(w1, W["ffn_in_w"][l], 8, "w1", w1g)
        load_w(w2, W["ffn_out_w"][l], 22, "w2", per=11)
        NT = L // TT

        def gate(fc):
            b = fc % 2
            act(sg[b][:], acc[0][b][:], AF.Silu, r=[("acc", 0, b)], w=[("sg", b)])
            tt("pool", ga[:, fc, :], sg[b][:], acc[1][b][:], ALU.mult, r=[("sg", b), ("acc", 1, b)], w=[("ga", fc)])

        mset("pool", h[:, :, 0:2], 0.0, w=["hh"])
        S.dma("sp", xa[:], xr_ap(xsrc, 0, TT), r=xr_tok(0, TT), w=["xa"])
        prenorm(xa, "xa", ("ng", l, 4), hv, TT, sq, ssps, rstd)
        hall = [("h", kt) for kt in range(8)]
        for ti in range(NT):
            t0 = ti * TT
            S.dma("sp", xb[:], xr_ap(xsrc, t0, TT), r=xr_tok(t0, TT), w=["xb"])
            for fc in range(22):
                b = fc % 2
                for a in range(2):
                    col0 = a * DFF + fc * 128
                    cidx = a * 22 + fc
                    up, ac = ups[a][b], acc[a][b]
                    for kt in range(8):
                        mm(up[:, :TT + 2], w1[:, kt, col0:col0 + 128], h[:, kt, :], kt == 0, kt == 7,
                           r=[gtok("w1", w1g, col0), ("h", kt), "hh"], w=[("ups", a, b)])
                    act(ac[:], up[:, 2:2 + TT], AF.Identity, r=[("ups", a, b)], w=[("acc", a, b)],
                        scale=vcol(("fcw", l), 2 * 44 + cidx), bias=vcol(("fcb", l), cidx))
                if fc > 0:
                    gate(fc - 1)
                if ti + 1 < NT:
                    if fc == 12:
                        S.dma("sp", xa[:], xr_ap(xsrc, t0 + TT, TT), r=xr_tok(t0 + TT, TT), w=["xa"])
                        prenorm_sq(xa, "xa", TT, sq)
                    elif fc == 17:
                        prenorm_ss(TT, sq, ssps, rstd)
                for k in (1, 0):
                    for a in range(2):
                        cidx = a * 22 + fc
                        up, ac = ups[a][b], acc[a][b]
                        stt("dve", ac[:], up[:, k:k + TT], vcol(("fcw", l), k * 44 + cidx), ac[:], ALU.mult, ALU.add,
                            r=[("ups", a, b), ("acc", a, b)], w=[("acc", a, b)])
            gate(21)
            if ti + 1 < NT:
                cp("pool", h[:, :, 0:2], h[:, :, TT:TT + 2], r=hall, w=["hh"])
                prenorm_h(xa, "xa", ("ng", l, 4), hv, TT, rstd)
            for dt in range(8):
                fp = fps[dt % 2]
                for fc in range(22):
                    mm(fp[:, :TT], w2[:, fc, dt * 128:(dt + 1) * 128], ga[:, fc, :], fc == 0, fc == 21,
                       r=[("w2", fc), ("ga", fc)], w=[("fps", dt % 2)])
                post_evac(dt, fp[:, :TT], ("fps", dt % 2), TT, fsb, sq2, pp="p")
            post_finish(("ng", l, 5), xb, TT, t0, fsb, sq2, ssps2, rstd2, pp="p")
        st.close()

    def stage_xattn(l, xsrc):
        TT = 512
        st = Stage()
        wq = st.sb("wq", [128, 8, D], BF16)
        wkv = st.sb("wkv", [128, 8, 2 * D], BF16)
        wo = st.sb("wo", [128, 8, D], BF16)
        memf = st.sb("memf", [128, 8, NM], F32)
        memn = st.sb("memn", [128, 8, NM], BF16)
        KT = st.sb("KT", [128, 8, NM], BF16)
        V = st.sb("V", [128, 2, D], BF16)
        xa = st.sb("xa", [128, 8, TT], F32)
        xb = st.sb("xb", [128, 8, TT], F32)
        sq = st.sb("sq", [128, 8, TT], BF16)
        h = st.sb("h", [128, 8, TT], BF16)
        rstd = st.sb("rstd", [128, TT], F32)
        qT = st.sb("qT", [128, 8, TT], BF16)
        eT = [st.sb("eT%d" % b, [128, 2, TT], BF16) for b in range(2)]
        rden = [st.sb("rden%d" % b, [128, TT], F32) for b in range(2)]
        oT = st.sb("oT", [128, 8, TT], BF16)
        fsb = st.sb("fsb", [128, 8, TT], F32)
        ssps = st.ps("ssps", [128, 512])
        qps = [st.ps("qps%d" % b, [128, 512]) for b in range(2)]
        sps = [st.ps("sps%d" % b, [128, 512]) for b in range(2)]
        dps = st.ps("dps", [128, 512])
        ops_ = [st.ps("ops%d" % b, [128, 512]) for b in range(2)]
        g512 = lambda n: [(i, (i * 512, (i + 1) * 512)) for i in range(n)]
        wkvg, wqg, wog = g512(4), g512(2), g512(2)
        load_wc(wkv, W["xa_kv_w"][l], 8, "wkv", wkvg)
        load_wc(wq, W["xa_q_w"][l], 8, "wq", wqg)
        load_wc(wo, W["xa_o_w"][l], 8, "wo", wog)
        S.dma("sp", memf[:], memT.rearrange("(kt p) m -> p kt m", p=128), w=["memf"])
        prenorm(memf, "memf", ("mg", l), memn, NM, sq, ssps, rstd)
        for dt in range(8):
            qp = qps[dt % 2]
            for kt in range(8):
                mm(qp[:, :NM], wkv[:, kt, dt * 128:(dt + 1) * 128], memn[:, kt, :], kt == 0, kt == 7,
                   r=[gtok("wkv", wkvg, dt * 128), ("h", kt)], w=[("qps", dt % 2)])
            act(KT[:, dt, :], qp[:, :NM], AF.Copy, r=[("qps", dt % 2)], w=["KT"])
        for mt in range(2):
            for nb in range(2):
                qp = qps[nb]
                for kt in range(8):
                    mm(qp[:], memn[:, kt, mt * 128:(mt + 1) * 128], wkv[:, kt, D + nb * 512:D + (nb + 1) * 512],
                       kt == 0, kt == 7, r=[gtok("wkv", wkvg, D + nb * 512), ("h", kt)], w=[("qps", nb)])
                act(V[:, mt, nb * 512:(nb + 1) * 512], qp[:], AF.Copy, r=[("qps", nb)], w=["V"])
        sq2 = st.sb("sq2", [128, 8, TT], BF16)
        rstd2 = st.sb("rstd2", [128, TT], F32)
        NT = L // TT

        def pre(ti):
            S.dma("sp", xa[:], xr_ap(xsrc, ti * TT, TT), r=xr_tok(ti * TT, TT), w=["xa"])
            prenorm(xa, "xa", ("ng", l, 2), h, TT, sq, ssps, rstd)

        pre(0)
        S.dma("sp", xb[:], xr_ap(xsrc, 0, TT), r=xr_tok(0, TT), w=["xb"])
        for ti in range(NT):
            t0 = ti * TT
            for dt in range(8):
                qp = qps[dt % 2]
                for kt in range(8):
                    mm(qp[:], wq[:, kt, dt * 128:(dt + 1) * 128], h[:, kt, :], kt == 0, kt == 7,
                       r=[gtok("wq", wqg, dt * 128), ("h", kt)], w=[("qps", dt % 2)])
                act(qT[:, dt, :], qp[:], AF.Identity, r=[("qps", dt % 2)], w=[("qT", dt)], scale=1.0 / 16.0)
            for hd in range(4):
                b = hd % 2
                for mt in range(2):
                    for j in range(2):
                        mm(sps[mt][:], KT[:, 2 * hd + j, mt * 128:(mt + 1) * 128], qT[:, 2 * hd + j, :], j == 0, j == 1,
                           r=["KT", ("qT", 2 * hd + j)], w=[("sps", mt)])
                    act(eT[b][:, mt, :], sps[mt][:], AF.Exp, r=[("sps", mt)], w=[("eT", b, mt)])
                for mt in range(2):
                    mm(dps[:], ones_bf[:], eT[b][:, mt, :], mt == 0, mt == 1, r=[("eT", b, mt)], w=["dps"])
                act(rden[b][:], dps[:], AF.Ln, r=["dps"], w=[("rden", b)])
                act(rden[b][:], rden[b][:], AF.Exp, r=[("rden", b)], w=[("rden", b)], scale=-1.0)
                for j in range(2):
                    op_ = ops_[j]
                    for mt in range(2):
                        mm(op_[:], V[:, mt, (2 * hd + j) * 128:(2 * hd + j + 1) * 128], eT[b][:, mt, :], mt == 0, mt == 1,
                           r=["V", ("eT", b, mt)], w=[("ops", j)])
                    tt("dve", oT[:, 2 * hd + j, :], op_[:], rden[b][:], ALU.mult, r=[("ops", j), ("rden", b)],
                       w=[("oT", 2 * hd + j)])
            if ti + 1 < NT:
                pre(ti + 1)
            for dt in range(8):
                qp = qps[dt % 2]
                for kt in range(8):
                    mm(qp[:], wo[:, kt, dt * 128:(dt + 1) * 128], oT[:, kt, :], kt == 0, kt == 7,
                       r=[gtok("wo", wog, dt * 128), ("oT", kt)], w=[("qps", dt % 2)])
                post_evac(dt, qp[:], ("qps", dt % 2), TT, fsb, sq2, pp="p")
            post_finish(("ng", l, 3), xb, TT, t0, fsb, sq2, dps, rstd2, pp="p", sstok="dps")
            if ti + 1 < NT:
                S.dma("sp", xb[:], xr_ap(xsrc, t0 + TT, TT), r=xr_tok(t0 + TT, TT), w=["xb"])
        st.close()

    def stage_conf(l, xsrc):
        j = l // 2
        TT = 256
        HL = CFK - 1
        NT = L // TT
        st = Stage()
        pw1 = st.sb("pw1", [128, 8, 2 * D], BF16)
        pw2 = st.sb("pw2", [128, 8, D], BF16)
        dg = st.sb("dg", [128, CFK * 8, 128], BF16)
        xa = st.sb("xa", [128, 8, TT], F32)
        xb = st.sb("xb", [128, 8, TT], F32)
        sq = st.sb("sq", [128, 8, TT], BF16)
        sqc = st.sb("sqc", [128, 8, TT], BF16)
        sq2 = st.sb("sq2", [128, 8, TT], BF16)
        h = st.sb("h", [128, 8, TT], BF16)
        rstd = st.sb("rstd", [128, TT], F32)
        rstdc = st.sb("rstdc", [128, TT], F32)
        rstd2 = st.sb("rstd2", [128, TT], F32)
        sig = [st.sb("sig%d" % b, [128, TT], F32) for b in range(2)]
        glu = [st.sb("glu%d" % b, [128, 8, HL + TT], BF16) for b in range(2)]
        cs = st.sb("cs", [128, 8, TT], F32)
        cb = st.sb("cb", [128, 8, TT], BF16)
        accd = [st.sb("accd%d" % b, [128, TT], F32) for b in range(2)]
        mean = st.sb("mean", [128, TT], F32)
        var = st.sb("var", [128, TT], F32)
        t1 = [st.sb("t1%d" % b, [128, TT], F32) for b in range(2)]
        sn = st.sb("sn", [128, 8, TT], BF16)
        fsb = st.sb("fsb", [128, 8, TT], F32)
        ssps = st.ps("ssps", [128, 512])
        aps = [st.ps("aps%d" % b, [128, 512]) for b in range(2)]
        gps = [st.ps("gps%d" % b, [128, 512]) for b in range(2)]
        cps = [st.ps("cps%d" % b, [128, 512]) for b in range(2)]
        mps = st.ps("mps", [128, 512])
        pw1g = [(0, (0, 512)), (1, (D, D + 512)), (2, (512, D)), (3, (D + 512, 2 * D))]
        load_wc(pw1, W["cf_pw1_w"][j], 8, "pw1", pw1g)
        load_w(pw2, W["cf_pw2_w"][j], 8, "pw2", per=4)
        dwo = voff[("dww", j)]
        for c in range(8):
            tt("dve", dg[:, c * CFK:(c + 1) * CFK, :],
               ident_f.unsqueeze(1).to_broadcast([128, CFK, 128]),
               vecs[:, dwo + c:dwo + c + CFK * 8:8].unsqueeze(2).to_broadcast([128, CFK, 128]), ALU.mult,
               r=["cst"], w=[("dg", c)])
        for c in range(8):
            mset("pool", glu[1][:, c, TT:TT + HL], 0.0, w=[("glu", 1, c)])

        def pre(ti):
            t0 = ti * TT
            S.dma("sp", xa[:], xr_ap(xsrc, t0, TT), r=xr_tok(t0, TT), w=["xa"])
            prenorm(xa, "xa", ("ng", l, 0), h, TT, sq, ssps, rstd)

        def pw1glu(ti):
            gb = ti % 2
            G_, Gp = glu[gb], glu[1 - gb]
            for c in range(8):
                b = c % 2
                for kt in range(8):
                    mm(aps[b][:, :TT], pw1[:, kt, c * 128:(c + 1) * 128], h[:, kt, :], kt == 0, kt == 7,
                       r=[gtok("pw1", pw1g, c * 128), ("h", kt)], w=[("aps", b)])
                for kt in range(8):
                    mm(gps[b][:, :TT], pw1[:, kt, D + c * 128:D + (c + 1) * 128], h[:, kt, :], kt == 0, kt == 7,
                       r=[gtok("pw1", pw1g, D + c * 128), ("h", kt)], w=[("gps", b)])
                act(sig[b][:], gps[b][:, :TT], AF.Sigmoid, r=[("gps", b)], w=[("sig", b)], bias=vcol(("pw1b", j), 8 + c))
                cp("pool", G_[:, c, 0:HL], Gp[:, c, TT:TT + HL], r=[("glu", 1 - gb, c)], w=[("gluh", gb, c)])
                stt("dve", G_[:, c, HL:HL + TT], aps[b][:, :TT], vcol(("pw1b", j), c), sig[b][:], ALU.add, ALU.mult,
                    r=[("aps", b), ("sig", b)], w=[("glu", gb, c)])

        KD = 6

        def conv(ti):
            gb = ti % 2
            G_ = glu[gb]
            for c0 in range(0, 8, 2):
                for k in range(KD):
                    for c in (c0, c0 + 1):
                        gr = [("glu", gb, c), ("gluh", gb, c)]
                        if k == 0:
                            ts("dve", accd[c % 2][:], G_[:, c, 0:TT], vcol(("dww", j), c), None, ALU.mult, None,
                               r=gr, w=[("accd", c % 2)])
                        else:
                            stt("dve", accd[c % 2][:], G_[:, c, k:k + TT], vcol(("dww", j), k * 8 + c), accd[c % 2][:],
                                ALU.mult, ALU.add, r=gr + [("accd", c % 2)], w=[("accd", c % 2)])
                for c in (c0, c0 + 1):
                    b = c % 2
                    for k in range(KD, CFK):
                        mm(cps[b][:, :TT], dg[:, c * CFK + k, :], G_[:, c, k:k + TT], k == KD, k == CFK - 1,
                           r=[("dg", c), ("glu", gb, c), ("gluh", gb, c)], w=[("cps", b)])
                    stt("dve", cs[:, c, :], cps[b][:, :TT], vcol(("dwb", j), c), accd[b][:], ALU.add, ALU.add,
                        r=[("cps", b), ("accd", b)], w=[("cs", c)])
                    act(sqc[:, c, :], cs[:, c, :], AF.Square, r=[("cs", c)], w=[("sqc", c)])
                    act(cb[:, c, :], cs[:, c, :], AF.Copy, r=[("cs", c)], w=[("cb", c)])

        def lnorm(ti):
            for c in range(8):
                mm(mps[:, 0:TT], onesD_bf[:], cb[:, c, :], c == 0, c == 7, r=[("cb", c)], w=["mps"])
            for c in range(8):
                mm(mps[:, TT:2 * TT], onesD_bf[:], sqc[:, c, :], c == 0, c == 7, r=[("sqc", c)], w=["mps"])
            cp("dve", mean[:], mps[:, 0:TT], r=["mps"], w=["mean"])
            tt("dve", var[:], mean[:], mean[:], ALU.mult, r=["mean"], w=["var"])
            tt("dve", var[:], mps[:, TT:2 * TT], var[:], ALU.subtract, r=["mps", "var"], w=["var"])
            rsqrt_act(rstdc[:], var[:], 1.0, r=["var"], w=["rstdc"])
            for c in range(8):
                b = c % 2
                tt("pool", t1[b][:], cs[:, c, :], mean[:], ALU.subtract, r=[("cs", c), "mean"], w=[("t1", b)])
                stt("dve", t1[b][:], t1[b][:], vcol(("lng", j), c), rstdc[:], ALU.mult, ALU.mult,
                    r=[("t1", b), "rstdc"], w=[("t1", b)])
                act(sn[:, c, :], t1[b][:], AF.Silu, r=[("t1", b)], w=[("sn", c)], bias=vcol(("lnb", j), c))

        def pw2post(ti):
            t0 = ti * TT
            for dt in range(8):
                b = dt % 2
                for c in range(8):
                    mm(cps[b][:, :TT], pw2[:, c, dt * 128:(dt + 1) * 128], sn[:, c, :], c == 0, c == 7,
                       r=[("pw2", c), ("sn", c)], w=[("cps", b)])
                post_evac(dt, cps[b][:, :TT], ("cps", b), TT, fsb, sq2, bias=vcol(("pw2b", j), dt), pp="p")
            post_finish(("ng", l, 1), xb, TT, t0, fsb, sq2, ssps, rstd2, pp="p", sstok="ssps")

        pre(0)
        S.dma("sp", xb[:], xr_ap(xsrc, 0, TT), r=xr_tok(0, TT), w=["xb"])
        pw1glu(0)
        for ti in range(NT):
            conv(ti)
            if ti + 1 < NT:
                pre(ti + 1)
                pw1glu(ti + 1)
            lnorm(ti)
            pw2post(ti)
            if ti + 1 < NT:
                S.dma("sp", xb[:], xr_ap(xsrc, (ti + 1) * TT, TT), r=xr_tok((ti + 1) * TT, TT), w=["xb"])
        st.close()

    def stage_ssd_a(l, xsrc, wout=None):
        j = l // 2
        TT = 512
        st = Stage()
        win = st.sb("win", [128, 8, INP], BF16)
        xa = st.sb("xa", [128, 8, TT], F32)
        sq = st.sb("sq", [128, 8, TT], BF16)
        hh_ = [st.sb("h%d" % b, [128, 8, TT], BF16) for b in range(2)]
        rstd = st.sb("rstd", [128, TT], F32)
        zs = [st.sb("zs%d" % b, [128, DI], BF16) for b in range(2)]
        ub = [st.sb("ub%d" % b, [128, TT + 3], F32) for b in range(2)]
        acc = [st.sb("acc%d" % b, [128, TT], F32) for b in range(2)]
        xbs = [st.sb("xbs%d" % b, [128, TT], BF16) for b in range(4)]
        halo = st.sb("halo", [128, 24, 3], F32)
        bv = st.sb("bv", [128, 96], F32)
        Ab = st.sb("Ab", [128, 32], F32)
        dtt = [st.sb("dtt%d" % b, [128, 32], F32) for b in range(2)]
        dta = [st.sb("dta%d" % b, [128, 64], F32) for b in range(2)]
        ssps = st.ps("ssps", [128, 512])
        zps = [st.ps("zps%d" % b, [128, 512]) for b in range(2)]
        ups = [st.ps("ups%d" % b, [128, 512]) for b in range(2)]
        dps = st.ps("dps", [128, 512])
        wing = [(i, (i * 512, (i + 1) * 512)) for i in range(4)] + [(4, (DI + CONVD, INP))] + \
               [(5 + i, (DI + i * 512, DI + (i + 1) * 512)) for i in range(6)]
        load_wc(win, W["ssm_in_w"][j], 8, "win", wing)
        if wout is not None:
            load_w(wout, W["ssm_out_w"][j], 16, "wout", per=4)
        S.dma("sp", bv[:], bvecs_d[:, j * 96:(j + 1) * 96], w=["bv"])
        act(Ab[:], bv[:, 32:64], AF.Exp, r=["bv"], w=["Ab"])
        ts("dve", Ab[:], Ab[:], -1.0, None, ALU.mult, None, r=["Ab"], w=["Ab"])
        mset("pool", halo[:], 0.0, w=[("halo", c) for c in range(24)])
        NT = L // TT

        def pre(ti):
            S.dma("sp", xa[:], xr_ap(xsrc, ti * TT, TT), r=xr_tok(ti * TT, TT), w=["xa"])
            prenorm(xa, "xa", ("ng", l, 0), hh_[ti % 2], TT, sq, ssps, rstd, htok="h%d" % (ti % 2))

        pre(0)
        for ti in range(NT):
            t0 = ti * TT
            h = hh_[ti % 2]
            ht = "h%d" % (ti % 2)
            for ck in range(4):
                cg = ti * 4 + ck
                zb = zs[ck % 2]
                for nb in range(4):
                    zp = zps[nb % 2]
                    for kt in range(8):
                        mm(zp[:], h[:, kt, ck * 128:(ck + 1) * 128], win[:, kt, nb * 512:(nb + 1) * 512], kt == 0, kt == 7,
                           r=[gtok("win", wing, nb * 512), (ht, kt)], w=[("zps", nb % 2)])
                    act(zb[:, nb * 512:(nb + 1) * 512], zp[:], AF.Silu, r=[("zps", nb % 2)], w=[("zs", ck % 2)])
                S.dma("sp", ZS[cg * 128:(cg + 1) * 128, :], zb[:], r=[("zs", ck % 2)], w=[("zsd", cg)])
                for kt in range(8):
                    mm(dps[:, 0:32], h[:, kt, ck * 128:(ck + 1) * 128], win[:, kt, DI + CONVD:INP], kt == 0, kt == 7,
                       r=[("win", 4), (ht, kt)], w=["dps"])
                db = ck % 2
                tt("dve", dtt[db][:], dps[:, 0:32], bv[:, 0:32], ALU.add, r=["dps", "bv"], w=[("dtt", db)])
                act(dtt[db][:], dtt[db][:], AF.Exp, r=[("dtt", db)], w=[("dtt", db)])
                act(dta[db][:, 0:32], dtt[db][:], AF.Ln, r=[("dtt", db)], w=[("dta", db)], bias=onec[:])
                tt("dve", dta[db][:, 32:64], dta[db][:, 0:32], Ab[:], ALU.mult, r=[("dta", db), "Ab"], w=[("dta", db)])
                S.dma("sp", DTA[cg * 128:(cg + 1) * 128, :], dta[db][:], r=[("dta", db)], w=[("dtad", cg)])
            def finish_c(c):
                b = c % 2
                xs = xbs[c % 4]
                act(xs[:], acc[b][:], AF.Silu, r=[("acc", b)], w=[("xbs", c % 4)])
                S.dma("sp", XBC[c * 128:(c + 1) * 128, t0:t0 + TT], xs[:], r=[("xbs", c % 4)], w=[("xbcd", ti, c)])

            for c in range(24):
                b = c % 2
                col0 = DI + c * 128
                for kt in range(8):
                    mm(ups[b][:], win[:, kt, col0:col0 + 128], h[:, kt, :], kt == 0, kt == 7,
                       r=[gtok("win", wing, col0), (ht, kt)], w=[("ups", b)])
                u, ac = ub[b], acc[b]
                cp("pool", u[:, 0:3], halo[:, c, :], r=[("halo", c)], w=[("ubh", b)])
                act(u[:, 3:3 + TT], ups[b][:], AF.Copy, r=[("ups", b)], w=[("ub", b)])
                act(ac[:], ups[b][:], AF.Identity, r=[("ups", b)], w=[("acc", b)],
                    scale=vcol(("scw", j), 3 * 24 + c), bias=vcol(("scb", j), c))
                cp("pool", halo[:, c, :], u[:, TT:TT + 3], r=[("ub", b)], w=[("halo", c)])
                if c > 0:
                    finish_c(c - 1)
                if ti + 1 < NT:
                    nb_ = (ti + 1) % 2
                    if c == 4:
                        S.dma("sp", xa[:], xr_ap(xsrc, (ti + 1) * TT, TT), r=xr_tok((ti + 1) * TT, TT), w=["xa"])
                        prenorm_sq(xa, "xa", TT, sq)
                    elif c == 12:
                        prenorm_ss(TT, sq, ssps, rstd)
                    elif c == 18:
                        prenorm_h(xa, "xa", ("ng", l, 0), hh_[nb_], TT, rstd, htok="h%d" % nb_)
                for k in (2, 1, 0):
                    stt("dve", ac[:], u[:, k:k + TT], vcol(("scw", j), k * 24 + c), ac[:], ALU.mult, ALU.add,
                        r=[("ub", b), ("ubh", b), ("acc", b)], w=[("acc", b)])
            finish_c(23)
            if ti == NT - 1 and wout is not None:
                for c in range(16):
                    act(wout[:, c, :], wout[:, c, :], AF.Identity, r=[("wout", c)], w=[("wout", c)], scale=vcol(("sng", j), c))
        st.close()

    def stage_ssd_b(l, xsrc, wout_pre=None):
        j = l // 2
        TT = 256
        CPT = TT // 128
        NT = L // TT
        NCH = L // 128
        st = Stage()
        wout = wout_pre if wout_pre is not None else st.sb("wout", [128, 16, D], BF16)
        xbc = [st.sb("xbc%d" % b, [128, 24, TT], BF16) for b in range(2)]
        zsc = [st.sb("zsc%d" % b, [128, DI], BF16) for b in range(2)]
        dta = [st.sb("dtac%d" % b, [128, 64], F32) for b in range(2)]
        bv = st.sb("bv", [128, 96], F32)
        xtm = [st.sb("xtm%d" % b, [128, DI], BF16) for b in range(2)]
        xdt = [st.sb("xdt%d" % b, [128, DI], BF16) for b in range(2)]
        xdd = [st.sb("xdd%d" % b, [128, DI], BF16) for b in range(2)]
        xD = [st.sb("xD%d" % b, [128, DI], BF16) for b in range(2)]
        Btm = [st.sb("Btm%d" % b, [128, 512], BF16) for b in range(2)]
        cs_s = [st.sb("cs_s%d" % b, [128, 32], F32) for b in range(2)]
        ecs = [st.sb("ecs%d" % b, [128, 32], F32) for b in range(2)]
        dte = [st.sb("dte%d" % b, [128, 32], F32) for b in range(2)]
        dec = [st.sb("dec%d" % b, [128, 32], F32) for b in range(2)]
        CBm = [st.sb("CBm%d" % b, [128, 4, 128], BF16) for b in range(2)]
        R = [st.sb("R%d" % b, [128, 8, 128], F32) for b in range(2)]
        es_ = [[st.sb("es%d%d" % (a, b), [128, 512], BF16) for b in range(2)] for a in range(2)]
        M = [[st.sb("M%d%d" % (a, b), [128, 4, 128], BF16) for b in range(2)] for a in range(2)]
        stf = st.sb("stf", [128, 4, 512], F32)
        stb = st.sb("stb", [128, 4, 512], BF16)
        t1 = [st.sb("t1%d" % b, [128, 512], F32) for b in range(2)]
        ytm = st.sb("ytm", [128, DI], F32)
        ss1 = st.sb("ss1", [128, 1], F32)
        rs1 = st.sb("rs1", [128, 1], F32)
        yn = st.sb("yn", [128, DI], BF16)
        ynT = st.sb("ynT", [128, 16, TT], BF16)
        xb = st.sb("xb", [128, 8, TT], F32)
        fsb = st.sb("fsb", [128, 8, TT], F32)
        sq = st.sb("sq", [128, 8, TT], BF16)
        rstd = st.sb("rstd", [128, TT], F32)
        tp = st.ps("tp", [128, DI], BF16)
        smp = st.ps("smp", [128, 512])
        cbp = st.ps("cbp", [128, 512])
        sgp = [st.ps("sgp%d" % b, [128, 512]) for b in range(2)]
        ydp = st.ps("ydp", [128, 512])
        yop = st.ps("yop", [128, 512])
        if wout_pre is None:
            load_w(wout, W["ssm_out_w"][j], 16, "wout", per=4)
            for c in range(16):
                ts("dve" if c % 2 == 0 else "pool", wout[:, c, :], wout[:, c, :], vcol(("sng", j), c), None, ALU.mult, None,
                   r=[("wout", c)], w=[("wout", c)])
        S.dma("sp", bv[:], bvecs_d[:, j * 96:(j + 1) * 96], w=["bv"])
        mset("dve", stf[:], 0.0, w=[("stf", g) for g in range(NG)])
        mset("pool", stb[:], 0.0, w=[("stb", g) for g in range(NG)])
        hq = "p (h q) -> p h q"

        def load_tile(ti):
            t0 = ti * TT
            for c0 in range(0, 24, 8):
                S.dma("sp", xbc[ti % 2][:, c0:c0 + 8, :],
                      XBC[c0 * 128:(c0 + 8) * 128, t0:t0 + TT].rearrange("(c p) t -> p c t", p=128),
                      r=[("xbcd", (t0 // 512), c) for c in range(c0, c0 + 8)],
                      w=[("xbc", ti % 2, c) for c in range(c0, c0 + 8)])

        def front(c):
            cb, ti, ck = c % 2, c // CPT, c % CPT
            X = xbc[ti % 2]
            sl = slice(ck * 128, (ck + 1) * 128)
            S.dma("sp", zsc[cb][:], ZS[c * 128:(c + 1) * 128, :], r=[("zsd", c)], w=[("zsc", cb)])
            S.dma("sp", dta[cb][:], DTA[c * 128:(c + 1) * 128, :], r=[("dtad", c)], w=[("dtac", cb)])
            dtv = dta[cb][:, 0:32]
            av = dta[cb][:, 32:64]
            for g in range(NG):
                tr(tp[:, g * 128:(g + 1) * 128], X[:, 16 + g, sl], r=[("xbc", ti % 2, 16 + g)], w=["tp"])
            cp("dve", Btm[cb][:], tp[:, 0:512], r=["tp"], w=[("Btm", cb)])
            for ch in range(16):
                tr(tp[:, ch * 128:(ch + 1) * 128], X[:, ch, sl], r=[("xbc", ti % 2, ch)], w=["tp"])
            cp("dve", xtm[cb][:], tp[:], r=["tp"], w=[("xtm", cb)])
            tt("dve", xdt[cb][:].rearrange(hq, q=64), tp[:].rearrange(hq, q=64),
               dtv.unsqueeze(2).to_broadcast([128, 32, 64]), ALU.mult, r=["tp", ("dtac", cb)], w=[("xdt", cb)])
            tt("pool", xD[cb][:].rearrange(hq, q=64), xtm[cb][:].rearrange(hq, q=64),
               bv[:, 64:96].unsqueeze(2).to_broadcast([128, 32, 64]), ALU.mult, r=[("xtm", cb), "bv"], w=[("xD", cb)])
            mm(smp[:, 0:32], U_f, av, True, True, r=["cst", ("dtac", cb)], w=["smp"])
            mm(smp[:, 32:64], ones_f, av, True, True, r=["cst", ("dtac", cb)], w=["smp"])
            cp("dve", cs_s[cb][:], smp[:, 0:32], r=["smp"], w=[("cs_s", cb)])
            act(ecs[cb][:], smp[:, 0:32], AF.Exp, r=["smp"], w=[("ecs", cb)])
            tt("dve", dte[cb][:], smp[:, 32:64], cs_s[cb][:], ALU.subtract, r=["smp", ("cs_s", cb)], w=[("dte", cb)])
            act(dte[cb][:], dte[cb][:], AF.Exp, r=[("dte", cb)], w=[("dte", cb)])
            act(dec[cb][:], smp[:, 32:64], AF.Exp, r=["smp"], w=[("dec", cb)])
            tt("pool", xdd[cb][:].rearrange(hq, q=64), xdt[cb][:].rearrange(hq, q=64),
               dte[cb][:].unsqueeze(2).to_broadcast([128, 32, 64]), ALU.mult, r=[("xdt", cb), ("dte", cb)], w=[("xdd", cb)])
            for g in range(NG):
                mm(cbp[:, g * 128:(g + 1) * 128], X[:, 16 + g, sl], X[:, 20 + g, sl], True, True,
                   r=[("xbc", ti % 2, 16 + g), ("xbc", ti % 2, 20 + g)], w=["ssps"])
            tt("dve", CBm[cb][:], cbp[:].rearrange("p (g i) -> p g i", g=4), U_f.unsqueeze(1).to_broadcast([128, 4, 128]),
               ALU.mult, r=["ssps", "cst"], w=[("CBm", cb)])

        def build_R_c(c, g):
            cb = c % 2
            av = dta[cb][:, 32:64]
            tt("pool", R[g % 2][:], U_f.unsqueeze(1).to_broadcast([128, 8, 128]),
               av[:, g * 8:(g + 1) * 8].unsqueeze(2).to_broadcast([128, 8, 128]), ALU.mult,
               r=["cst", ("dtac", cb)], w=[("R", g % 2)])

        def decay_c(c, g, hh):
            cb = c % 2
            gb = g % 2
            mm(sgp[hh][:], Lm_f, R[gb][:, hh * 4:(hh + 1) * 4, :].rearrange("p h i -> p (h i)"), True, True,
               r=["cst", ("R", gb)], w=[("sgp", hh)])
            act(es_[gb][hh][:], sgp[hh][:], AF.Exp, r=[("sgp", hh)], w=[("es", gb, hh)])
            tt("dve", M[gb][hh][:], es_[gb][hh][:].rearrange("p (h i) -> p h i", h=4),
               CBm[cb][:, g, :].unsqueeze(1).to_broadcast([128, 4, 128]), ALU.mult,
               r=[("es", gb, hh), ("CBm", cb)], w=[("M", gb, hh)])

        def decay0(c):
            build_R_c(c, 0)
            decay_c(c, 0, 0)
            decay_c(c, 0, 1)

        def middle(c):
            cb, ti, ck = c % 2, c // CPT, c % CPT
            X = xbc[ti % 2]
            sl = slice(ck * 128, (ck + 1) * 128)
            av = dta[cb][:, 32:64]

            def build_R(g):
                build_R_c(c, g)

            def decay(g, hh):
                decay_c(c, g, hh)

            decay0(c)
            for g in range(NG):
                gb = g % 2
                ydp_g, ydtok = (ydp, "ydp") if gb == 0 else (smp, "smp")
                if g + 1 < NG:
                    build_R(g + 1)
                for hh in range(2):
                    for hl in range(4):
                        hg = g * 8 + hh * 4 + hl
                        o0 = (hh * 4 + hl) * 64
                        mm(ydp_g[:, o0:o0 + 64], M[gb][hh][:, hl, :], xdt[cb][:, hg * 64:(hg + 1) * 64], True, False,
                           r=[("M", gb, hh), ("xdt", cb)], w=[ydtok])
                        mm(ydp_g[:, o0:o0 + 64], ident_bf[:], xD[cb][:, hg * 64:(hg + 1) * 64], False, True,
                           r=["cbf", ("xD", cb)], w=[ydtok])
                    if g + 1 < NG:
                        decay(g + 1, hh)
                mm(yop[:], X[:, 20 + g, sl], stb[:, g, :], True, True, r=[("xbc", ti % 2, 20 + g), ("stb", g)], w=["yop"])
                mm(cbp[:], Btm[cb][:, g * 128:(g + 1) * 128], xdd[cb][:, g * 512:(g + 1) * 512], True, True,
                   r=[("Btm", cb), ("xdd", cb)], w=["ssps"])
                tt("dve", t1[gb][:].rearrange(hq, q=64), yop[:].rearrange(hq, q=64),
                   ecs[cb][:, g * 8:(g + 1) * 8].unsqueeze(2).to_broadcast([128, 8, 64]), ALU.mult,
                   r=["yop", ("ecs", cb)], w=[("t1", gb)])
                tt("dve", stf[:, g, :].rearrange(hq, q=64), stf[:, g, :].rearrange(hq, q=64),
                   dec[cb][:, g * 8:(g + 1) * 8].unsqueeze(2).to_broadcast([128, 8, 64]), ALU.mult,
                   r=[("stf", g), ("dec", cb)], w=[("stf", g)])
                tt("dve", ytm[:, g * 512:(g + 1) * 512], t1[gb][:], ydp_g[:], ALU.add, r=[("t1", gb), ydtok], w=[("ytm", g)])
                tt("dve", stf[:, g, :], stf[:, g, :], cbp[:], ALU.add, r=[("stf", g), "ssps"], w=[("stf", g)])
                act(stb[:, g, :], stf[:, g, :], AF.Copy, r=[("stf", g)], w=[("stb", g)])

        def back1(c):
            cb = c % 2
            ytoks = [("ytm", g) for g in range(NG)]
            tt("dve", ytm[:], ytm[:], zsc[cb][:], ALU.mult, r=ytoks + [("zsc", cb)], w=ytoks)
            mset("dve", ss1[:], 0.0, w=["ss1"])
            act(yn[:], ytm[:], AF.Square, r=ytoks + ["ss1"], w=["yn", "ss1"], accum=ss1[:])
            rsqrt_act(rs1[:], ss1[:], 1.0 / DI, r=["ss1"], w=["rs1"])
            act(yn[:], ytm[:], AF.Identity, r=ytoks + ["rs1"], w=["yn"], scale=rs1[:])

        def back2(c):
            ti, ck = c // CPT, c % CPT
            sl = slice(ck * 128, (ck + 1) * 128)
            for ch in range(16):
                tr(tp[:, ch * 128:(ch + 1) * 128], yn[:, ch * 128:(ch + 1) * 128], r=["yn"], w=["tp"])
            for ch in range(16):
                cp("dve", ynT[:, ch, sl], tp[:, ch * 128:(ch + 1) * 128], r=["tp"], w=[("ynT", ch)])
            if ck == CPT - 1:
                t0 = ti * TT
                for dt in range(8):
                    fp = sgp[dt % 2]
                    for ch in range(16):
                        mm(fp[:, :TT], wout[:, ch, dt * 128:(dt + 1) * 128], ynT[:, ch, :], ch == 0, ch == 15,
                           r=[("wout", ch), ("ynT", ch)], w=[("sgp", dt % 2)])
                    post_evac(dt, fp[:, :TT], ("sgp", dt % 2), TT, fsb, sq)
                pending.append(ti)

        pending = []

        def flush_post():
            while pending:
                ti = pending.pop(0)
                t0 = ti * TT
                post_finish(("ng", l, 1), xb, TT, t0, fsb, sq, cbp, rstd, add_eng="dve")
                if ti + 1 < NT:
                    S.dma("sp", xb[:], xr_ap(xsrc, t0 + TT, TT), r=xr_tok(t0 + TT, TT), w=["xb"])

        load_tile(0)
        S.dma("sp", xb[:], xr_ap(xsrc, 0, TT), r=xr_tok(0, TT), w=["xb"])
        front(0)
        for c in range(NCH):
            if c % CPT == 0 and c // CPT + 1 < NT:
                load_tile(c // CPT + 1)
            middle(c)
            back1(c)
            if c + 1 < NCH:
                front(c + 1)
            flush_post()
            back2(c)
        flush_post()
        st.close()

    first = True
    for (l, s) in stages:
        xsrc = xT if first else y
        if s == "mix":
            if l % 2 == 0:
                with nc.sbuf_tensor("wout_l%d" % l, [128, 16, D], BF16) as wout_t:
                    stage_ssd_a(l, xsrc, wout_t)
                    stage_ssd_b(l, xsrc, wout_t)
            else:
                stage_conf(l, xsrc)
        elif s == "ssda":
            stage_ssd_a(l, xsrc)
        elif s == "ssdb":
            stage_ssd_b(l, xsrc)
        elif s == "xa":
            stage_xattn(l, xsrc)
        else:
            stage_ffn(l, xsrc)
        first = False
    S.finish()
    return nc


_PROGRAM_CACHE = {}


def make_in_maps(inp, L=L):
    vecs, bv, consts = build_vecs(inp)
    x = np.asarray(inp["x"], np.float32)[:, :L]
    mem = np.asarray(inp["mem"], np.float32)
    shared = {"vecs": vecs, "bvecs": bv, "consts": consts}
    for k in WEIGHT_SHAPES:
        shared[k] = np.ascontiguousarray(np.asarray(inp[k], np.float32))
    maps = []
    for b in range(x.shape[0]):
        m = dict(shared)
        m["xT"] = np.ascontiguousarray(x[b].T)
        m["memT"] = np.ascontiguousarray(mem[b].T)
        maps.append(m)
    return maps


def kernel(**inputs):
    maps = make_in_maps(inputs)
    nc = build_program()
    res = run_bass_kernel_spmd(nc, maps, core_ids=list(range(8)))
    out = np.stack([np.asarray(r["y"], np.float32).T for r in res.results], axis=0)
    return np.ascontiguousarray(out)
```

```python
from contextlib import ExitStack
import numpy as np
import concourse.bass as bass
import concourse.mybir as mybir
from concourse.bass_utils import run_bass_kernel_spmd

F32 = mybir.dt.float32
BF16 = mybir.dt.bfloat16
AF = mybir.ActivationFunctionType
ALU = mybir.AluOpType

D = 1024
L = 4096
NL = 4
DI = 2048
NH = 32
NG = 4
CONVD = 3072
INP = 5152
DFF = 2816
NM = 256
CFK = 31
EPS = 1e-6
ENGS = ("pe", "act", "dve", "pool", "sp")
SYNC_SMALL_ONLY = False


class Sched:
    def __init__(self, nc, same_engine_sync=True, n_dma_sems=28, n_pool_dma_sems=12):
        self.nc = nc
        self.same = same_engine_sync
        self.prog = {e: [] for e in ENGS}
        self.cnt = {e: 0 for e in ENGS}
        self.known = {e: {} for e in ENGS}
        self.state = {}
        self.sems = {}
        self._ctx = []
        for e in ("pe", "act", "dve", "pool"):
            self.sems[e] = self._sem("s_" + e)
        self.dma_pool = {"sp": [], "pool": []}
        for i in range(n_dma_sems):
            k = ("dsp", i); self.sems[k] = self._sem("d_sp%d" % i); self.dma_pool["sp"].append(k)
        for i in range(n_pool_dma_sems):
            k = ("dpl", i); self.sems[k] = self._sem("d_pl%d" % i); self.dma_pool["pool"].append(k)
        self.dma_cnt = {k: 0 for q in self.dma_pool.values() for k in q}
        self.dma_rr = {"sp": 0, "pool": 0}

    def _sem(self, name):
        cm = self.nc.semaphore(name)
        h = cm.__enter__()
        self._ctx.append(cm)
        return h

    def _need(self, eng, needs, key, val, small=True):
        if key == eng and (eng == "pe" or not self.same or (SYNC_SMALL_ONLY and not small)):
            return
        if self.known[eng].get(key, 0) >= val:
            return
        if needs.get(key, 0) < val:
            needs[key] = val

    def _deps(self, eng, r, w):
        needs = {}
        for t in r:
            st = self.state.get(t)
            if st and st[0]:
                self._need(eng, needs, st[0][0], st[0][1], st[2])
        for t in w:
            st = self.state.get(t)
            if st:
                if st[0]:
                    self._need(eng, needs, st[0][0], st[0][1], st[2])
                for k, v in st[1].items():
                    self._need(eng, needs, k, v)
        return needs

    def _emit_waits(self, eng, needs):
        for k, v in needs.items():
            self.known[eng][k] = v
            sem = self.sems[k]
            self.prog[eng].append(lambda e, sem=sem, v=v: e.wait_ge(sem, v))

    def _commit(self, r, w, key, val, small=True):
        for t in r:
            st = self.state.setdefault(t, [None, {}, True])
            if st[1].get(key, 0) < val:
                st[1][key] = val
        for t in w:
            self.state[t] = [(key, val), {}, small]

    def op(self, eng, fn, r=(), w=(), small=False):
        needs = self._deps(eng, r, w)
        self._emit_waits(eng, needs)
        self.cnt[eng] += 1
        val = self.cnt[eng]
        sem = self.sems[eng]
        self.prog[eng].append(lambda e, fn=fn, sem=sem: fn(e).then_inc(sem, 1))
        self._commit(r, w, eng, val, small)

    def dma(self, q, out, in_, r=(), w=()):
        pool = self.dma_pool[q]
        k = pool[self.dma_rr[q] % len(pool)]
        self.dma_rr[q] += 1
        needs = self._deps(q, r, w)
        prev = 16 * self.dma_cnt[k]
        if prev:
            self._need(q, needs, k, prev)
        self._emit_waits(q, needs)
        self.dma_cnt[k] += 1
        val = 16 * self.dma_cnt[k]
        sem = self.sems[k]
        self.prog[q].append(lambda e, out=out, in_=in_, sem=sem: e.dma_start(out=out, in_=in_).then_inc(sem, 16))
        self._commit(r, w, k, val)

    def barrier(self):
        cur = {e: self.cnt[e] for e in ("pe", "act", "dve", "pool")}
        for k, c in self.dma_cnt.items():
            cur[k] = 16 * c
        for eng in ENGS:
            needs = {}
            for k, v in cur.items():
                if v and k != eng:
                    self._need(eng, needs, k, v)
            self._emit_waits(eng, needs)
        self.state = {}

    def finish(self):
        self.barrier()
        prog = self.prog
        with self.nc.Block() as block:
            @block.tensor
            def _(e):
                for f in prog["pe"]:
                    f(e)

            @block.scalar
            def _(e):
                for f in prog["act"]:
                    f(e)

            @block.vector
            def _(e):
                for f in prog["dve"]:
                    f(e)

            @block.gpsimd
            def _(e):
                for f in prog["pool"]:
                    f(e)

            @block.sync
            def _(e):
                for f in prog["sp"]:
                    f(e)
        for cm in reversed(self._ctx):
            cm.__exit__(None, None, None)


def vec_layout():
    off = {}
    n = 0

    def add(name, cols):
        nonlocal n
        off[name] = n
        n += cols

    for l in range(NL):
        for j in range(6):
            add(("ng", l, j), 8)
        add(("mg", l), 8)
        add(("fcw", l), 3 * 44)
        add(("fcb", l), 44)
    for j in range(2):
        add(("scw", j), 4 * 24)
        add(("scb", j), 24)
        add(("sng", j), 16)
        add(("pw1b", j), 16)
        add(("dww", j), CFK * 8)
        add(("dwb", j), 8)
        add(("lng", j), 8)
        add(("lnb", j), 8)
        add(("pw2b", j), 8)
    return off, n


def _cols(v):
    v = np.asarray(v, dtype=np.float32)
    if v.ndim == 1:
        return np.ascontiguousarray(v.reshape(-1, 128).T)
    k, c = v.shape
    return np.ascontiguousarray(v.reshape(k, c // 128, 128).transpose(2, 0, 1).reshape(128, k * (c // 128)))


def build_vecs(inp):
    off, n = vec_layout()
    vecs = np.zeros((128, n), np.float32)

    def put(name, v):
        c = _cols(v)
        vecs[:, off[name]:off[name] + c.shape[1]] = c

    for l in range(NL):
        for j in range(6):
            put(("ng", l, j), inp["norm_g"][l, j])
        put(("mg", l), inp["xa_mem_g"][l])
        put(("fcw", l), inp["ffn_conv_w"][l])
        put(("fcb", l), inp["ffn_conv_b"][l])
    for j in range(2):
        put(("scw", j), inp["ssm_conv_w"][j])
        put(("scb", j), inp["ssm_conv_b"][j])
        put(("sng", j), inp["ssm_norm_g"][j])
        put(("pw1b", j), inp["cf_pw1_b"][j])
        put(("dww", j), inp["cf_dw_w"][j])
        put(("dwb", j), inp["cf_dw_b"][j])
        put(("lng", j), inp["cf_ln_g"][j])
        put(("lnb", j), inp["cf_ln_b"][j])
        put(("pw2b", j), inp["cf_pw2_b"][j])
    bv = np.zeros((128, 2 * 96), np.float32)
    for j in range(2):
        bv[:, j * 96 + 0:j * 96 + 32] = np.asarray(inp["ssm_dt_bias"][j], np.float32)[None, :]
        bv[:, j * 96 + 32:j * 96 + 64] = np.asarray(inp["ssm_A_log"][j], np.float32)[None, :]
        bv[:, j * 96 + 64:j * 96 + 96] = np.asarray(inp["ssm_D"][j], np.float32)[None, :]
    consts = np.zeros((128, 512), np.float32)
    consts[:, 0:128] = np.eye(128, dtype=np.float32)
    consts[:, 128:256] = np.triu(np.ones((128, 128), np.float32))
    consts[:, 256:384] = np.tril(np.ones((128, 128), np.float32), -1)
    consts[:, 384:512] = 1.0
    return vecs, bv, consts


WEIGHT_SHAPES = {
    "ssm_in_w": [2, D, INP], "ssm_out_w": [2, DI, D],
    "cf_pw1_w": [2, D, 2 * D], "cf_pw2_w": [2, D, D],
    "xa_q_w": [NL, D, D], "xa_kv_w": [NL, D, 2 * D], "xa_o_w": [NL, D, D],
    "ffn_in_w": [NL, D, 2 * DFF], "ffn_out_w": [NL, DFF, D],
}

ALL_STAGES = [(l, s) for l in range(NL) for s in ("mix", "xa", "ffn")]


def build_program(stages=None, same_engine_sync=True, L=L):
    stages = ALL_STAGES if stages is None else stages
    nc = bass.Bass("TRN2", target_bir_lowering=False)
    S = Sched(nc, same_engine_sync=same_engine_sync)
    voff, NV = vec_layout()

    xT = nc.dram_tensor("xT", [D, L], F32, kind="ExternalInput").ap()
    memT = nc.dram_tensor("memT", [D, NM], F32, kind="ExternalInput").ap()
    vecs_d = nc.dram_tensor("vecs", [128, NV], F32, kind="ExternalInput").ap()
    bvecs_d = nc.dram_tensor("bvecs", [128, 192], F32, kind="ExternalInput").ap()
    consts_d = nc.dram_tensor("consts", [128, 512], F32, kind="ExternalInput").ap()
    W = {k: nc.dram_tensor(k, shp, F32, kind="ExternalInput").ap() for k, shp in WEIGHT_SHAPES.items()}
    y = nc.dram_tensor("y", [D, L], F32, kind="ExternalOutput").ap()
    ZS = nc.dram_tensor("zs_scr", [L, DI], BF16).ap()
    XBC = nc.dram_tensor("xbc_scr", [CONVD, L], BF16).ap()
    DTA = nc.dram_tensor("dta_scr", [L, 64], F32).ap()

    vecs = nc.alloc_sbuf_tensor("vecs_sb", [128, NV], F32)
    cst = nc.alloc_sbuf_tensor("cst_sb", [128, 512], F32)
    ident_bf = nc.alloc_sbuf_tensor("ident_bf", [128, 128], BF16)
    ones_bf = nc.alloc_sbuf_tensor("ones_bf", [128, 128], BF16)
    onesD_bf = nc.alloc_sbuf_tensor("onesD_bf", [128, 128], BF16)
    epsc = nc.alloc_sbuf_tensor("epsc", [128, 1], F32)
    onec = nc.alloc_sbuf_tensor("onec", [128, 1], F32)
    ident_f = cst[:, 0:128]
    U_f = cst[:, 128:256]
    Lm_f = cst[:, 256:384]
    ones_f = cst[:, 384:512]

    def _small(ap):
        try:
            return ap.free_size() < 128
        except Exception:
            return True

    def mm(out, lhsT, rhs, start, stop, r, w):
        S.op("pe", lambda e: e.matmul(out, lhsT=lhsT, rhs=rhs, start=start, stop=stop), r, w, small=_small(out))

    def tr(out, in_, r, w):
        S.op("pe", lambda e: e.transpose(out, in_, ident_bf[:]), r, w, small=_small(out))

    def act(out, in_, func, r, w, bias=None, scale=None, accum=None):
        kw = {}
        if bias is not None:
            kw["bias"] = bias
        if scale is not None:
            kw["scale"] = scale
        if accum is not None:
            kw["accum_out"] = accum
        S.op("act", lambda e: e.activation(out=out, in_=in_, func=func, **kw), r, w,
             small=(_small(out) or accum is not None))

    def tt(eng, out, in0, in1, op, r, w):
        S.op(eng, lambda e: e.tensor_tensor(out=out, in0=in0, in1=in1, op=op), r, w, small=_small(out))

    def ts(eng, out, in0, s1, s2, op0, op1, r, w):
        if op1 is None:
            S.op(eng, lambda e: e.tensor_scalar(out=out, in0=in0, scalar1=s1, scalar2=None, op0=op0), r, w, small=_small(out))
        else:
            S.op(eng, lambda e: e.tensor_scalar(out=out, in0=in0, scalar1=s1, scalar2=s2, op0=op0, op1=op1), r, w, small=_small(out))

    def stt(eng, out, in0, scalar, in1, op0, op1, r, w):
        S.op(eng, lambda e: e.scalar_tensor_tensor(out=out, in0=in0, scalar=scalar, in1=in1, op0=op0, op1=op1), r, w, small=_small(out))

    def rsqrt_act(out, in_, scale, r, w):
        act(out, in_, AF.Ln, r=list(r) + ["cbf"], w=w, bias=epsc[:], scale=scale)
        act(out, out, AF.Exp, r=w, w=w, scale=-0.5)

    def cp(eng, out, in_, r, w):
        S.op(eng, lambda e: e.tensor_copy(out=out, in_=in_), r, w, small=_small(out))

    def mset(eng, ap, val, w):
        S.op(eng, lambda e: e.memset(ap, val), (), w, small=_small(ap))

    def vcol(name, c):
        o = voff[name] + c
        return vecs[:, o:o + 1]

    def xr_ap(dram, t0, tl):
        return dram.rearrange("(kt p) l -> p kt l", p=128)[:, :, t0:t0 + tl]

    def xr_tok(t0, tl):
        return [("xr", i) for i in range(t0 // 256, (t0 + tl) // 256)]

    S.dma("sp", vecs[:], vecs_d, w=["vecs"])
    S.dma("sp", cst[:], consts_d, w=["cst"])
    cp("dve", ident_bf[:], ident_f, r=["cst"], w=["cbf"])
    cp("dve", ones_bf[:], ones_f, r=["cst"], w=["cbf"])
    ts("dve", onesD_bf[:], ones_f, 1.0 / D, None, ALU.mult, None, r=["cst"], w=["cbf"])
    mset("dve", epsc[:], EPS, w=["cbf"])
    mset("dve", onec[:], 1.0, w=["cbf"])
    S.barrier()

    class Stage:
        n = 0

        def __init__(self):
            self.es = ExitStack()
            Stage.n += 1
            self.sfx = "_s%d" % Stage.n

        def sb(self, name, shape, dt):
            return self.es.enter_context(nc.sbuf_tensor(name + self.sfx, shape, dt))

        def ps(self, name, shape, dt=F32):
            return self.es.enter_context(nc.psum_tensor(name + self.sfx, shape, dt))

        def close(self):
            S.barrier()
            self.es.close()

    def prenorm(xa, xtok, gname, h, TT, sq, ssps, rstd, htok="h"):
        for kt in range(8):
            act(sq[:, kt, :TT], xa[:, kt, :TT], AF.Square, r=[xtok], w=[("sq", kt)])
        for kt in range(8):
            mm(ssps[:, :TT], onesD_bf[:], sq[:, kt, :TT], kt == 0, kt == 7, r=[("sq", kt)], w=["ssps"])
        rsqrt_act(rstd[:, :TT], ssps[:, :TT], 1.0, r=["ssps"], w=["rstd"])
        for kt in range(8):
            stt("dve", h[:, kt, :TT], xa[:, kt, :TT], vcol(gname, kt), rstd[:, :TT],
                ALU.mult, ALU.mult, r=[xtok, "rstd"], w=[(htok, kt)])

    def prenorm_sq(xa, xtok, TT, sq):
        for kt in range(8):
            act(sq[:, kt, :TT], xa[:, kt, :TT], AF.Square, r=[xtok], w=[("sq", kt)])

    def prenorm_ss(TT, sq, ssps, rstd):
        for kt in range(8):
            mm(ssps[:, :TT], onesD_bf[:], sq[:, kt, :TT], kt == 0, kt == 7, r=[("sq", kt)], w=["ssps"])
        rsqrt_act(rstd[:, :TT], ssps[:, :TT], 1.0, r=["ssps"], w=["rstd"])

    def prenorm_h(xa, xtok, gname, h, TT, rstd, htok="h"):
        for kt in range(8):
            stt("dve", h[:, kt, :TT], xa[:, kt, :TT], vcol(gname, kt), rstd[:, :TT],
                ALU.mult, ALU.mult, r=[xtok, "rstd"], w=[(htok, kt)])

    def post_evac(dt, fps, ftok, TT, fsb, sq, bias=None, pp=""):
        act(fsb[:, dt, :TT], fps, AF.Identity, r=[ftok], w=[("fsb", dt)], bias=bias)
        act(sq[:, dt, :TT], fps, AF.Square, r=[ftok], w=[(pp + "sq", dt)], bias=bias)

    def post_finish(gname, xb, TT, t0, fsb, sq, ssps, rstd, pp="", sstok=None, add_eng="pool"):
        sstok = sstok or (pp + "ssps")
        for kt in range(8):
            mm(ssps[:, :TT], onesD_bf[:], sq[:, kt, :TT], kt == 0, kt == 7, r=[(pp + "sq", kt)], w=[sstok])
        rsqrt_act(rstd[:, :TT], ssps[:, :TT], 1.0, r=[sstok], w=[pp + "rstd"])
        for kt in range(8):
            stt("dve", fsb[:, kt, :TT], fsb[:, kt, :TT], vcol(gname, kt), rstd[:, :TT], ALU.mult, ALU.mult,
                r=[("fsb", kt), pp + "rstd"], w=[("fsb", kt)])
            if add_eng == "pool":
                tt("pool", fsb[:, kt, :TT], fsb[:, kt, :TT], xb[:, kt, :TT], ALU.add, r=[("fsb", kt), "xb"], w=[("fsb", kt)])
        if add_eng != "pool":
            for kt in range(8):
                tt(add_eng, fsb[:, kt, :TT], fsb[:, kt, :TT], xb[:, kt, :TT], ALU.add, r=[("fsb", kt), "xb"], w=[("fsb", kt)])
        S.dma("sp", xr_ap(y, t0, TT), fsb[:, :, :TT], r=[("fsb", kt) for kt in range(8)], w=xr_tok(t0, TT))

    def load_w(dst, src2d, nk, tokname, per=1):
        for k0 in range(0, nk, per):
            k1 = min(nk, k0 + per)
            S.dma("pool", dst[:, k0:k1, :], src2d[k0 * 128:k1 * 128, :].rearrange("(k p) f -> p k f", p=128),
                  w=[(tokname, k) for k in range(k0, k1)])

    def load_wc(dst, src2d, nk, tokname, groups):
        for gi, (c0, c1) in groups:
            S.dma("pool", dst[:, :, c0:c1], src2d[:, c0:c1].rearrange("(k p) f -> p k f", p=128), w=[(tokname, gi)])

    def gtok(tokname, groups, col):
        for gi, (c0, c1) in groups:
            if c0 <= col < c1:
                return (tokname, gi)
        raise ValueError(col)

    def stage_ffn(l, xsrc):
        TT = 256
        st = Stage()
        w1 = st.sb("w1", [128, 8, 2 * DFF], BF16)
        w2 = st.sb("w2", [128, 22, D], BF16)
        xa = st.sb("xa", [128, 8, TT], F32)
        xb = st.sb("xb", [128, 8, TT], F32)
        sq = st.sb("sq", [128, 8, TT], BF16)
        sq2 = st.sb("sq2", [128, 8, TT], BF16)
        h = st.sb("h", [128, 8, TT + 2], BF16)
        hv = h[:, :, 2:2 + TT]
        rstd = st.sb("rstd", [128, TT], F32)
        rstd2 = st.sb("rstd2", [128, TT], F32)
        ga = st.sb("ga", [128, 22, TT], BF16)
        fsb = st.sb("fsb", [128, 8, TT], F32)
        acc = [[st.sb("acc%d%d" % (a, b), [128, TT], F32) for b in range(2)] for a in range(2)]
        sg = [st.sb("sg%d" % b, [128, TT], F32) for b in range(2)]
        ups = [[st.ps("ups%d%d" % (a, b), [128, 512]) for b in range(2)] for a in range(2)]
        fps = [st.ps("fps%d" % b, [128, 512]) for b in range(2)]
        ssps = st.ps("ssps", [128, 512])
        ssps2 = st.ps("ssps2", [128, 512])
        w1g = []
        for (f0, f1) in ((0, 3), (3, 9), (9, 15), (15, 22)):
            for a in range(2):
                w1g.append((len(w1g), (a * DFF + f0 * 128, a * DFF + f1 * 128)))
        load_wc(w1, W["ffn_in_w"][l], 8, "w1", w1g)
        load_w(w2, W["ffn_out_w"][l], 22, "w2", per=11)
        NT = L // TT

        def gate(fc):
            b = fc % 2
            act(sg[b][:], acc[0][b][:], AF.Silu, r=[("acc", 0, b)], w=[("sg", b)])
            tt("pool", ga[:, fc, :], sg[b][:], acc[1][b][:], ALU.mult, r=[("sg", b), ("acc", 1, b)], w=[("ga", fc)])

        mset("pool", h[:, :, 0:2], 0.0, w=["hh"])
        S.dma("sp", xa[:], xr_ap(xsrc, 0, TT), r=xr_tok(0, TT), w=["xa"])
        prenorm(xa, "xa", ("ng", l, 4), hv, TT, sq, ssps, rstd)
        hall = [("h", kt) for kt in range(8)]
        for ti in range(NT):
            t0 = ti * TT
            S.dma("sp", xb[:], xr_ap(xsrc, t0, TT), r=xr_tok(t0, TT), w=["xb"])
            for fc in range(22):
                b = fc % 2
                for a in range(2):
                    col0 = a * DFF + fc * 128
                    cidx = a * 22 + fc
                    up, ac = ups[a][b], acc[a][b]
                    for kt in range(8):
                        mm(up[:, :TT + 2], w1[:, kt, col0:col0 + 128], h[:, kt, :], kt == 0, kt == 7,
                           r=[gtok("w1", w1g, col0), ("h", kt), "hh"], w=[("ups", a, b)])
                    act(ac[:], up[:, 2:2 + TT], AF.Identity, r=[("ups", a, b)], w=[("acc", a, b)],
                        scale=vcol(("fcw", l), 2 * 44 + cidx), bias=vcol(("fcb", l), cidx))
                if fc > 0:
                    gate(fc - 1)
                if ti + 1 < NT:
                    if fc == 12:
                        S.dma("sp", xa[:], xr_ap(xsrc, t0 + TT, TT), r=xr_tok(t0 + TT, TT), w=["xa"])
                        prenorm_sq(xa, "xa", TT, sq)
                    elif fc == 17:
                        prenorm_ss(TT, sq, ssps, rstd)
                for k in (1, 0):
                    for a in range(2):
                        cidx = a * 22 + fc
                        up, ac = ups[a][b], acc[a][b]
                        stt("dve", ac[:], up[:, k:k + TT], vcol(("fcw", l), k * 44 + cidx), ac[:], ALU.mult, ALU.add,
                            r=[("ups", a, b), ("acc", a, b)], w=[("acc", a, b)])
            gate(21)
            if ti + 1 < NT:
                cp("pool", h[:, :, 0:2], h[:, :, TT:TT + 2], r=hall, w=["hh"])
                prenorm_h(xa, "xa", ("ng", l, 4), hv, TT, rstd)
            for dt in range(8):
                fp = fps[dt % 2]
                for fc in range(22):
                    mm(fp[:, :TT], w2[:, fc, dt * 128:(dt + 1) * 128], ga[:, fc, :], fc == 0, fc == 21,
                       r=[("w2", fc), ("ga", fc)], w=[("fps", dt % 2)])
                post_evac(dt, fp[:, :TT], ("fps", dt % 2), TT, fsb, sq2, pp="p")
            post_finish(("ng", l, 5), xb, TT, t0, fsb, sq2, ssps2, rstd2, pp="p")
        st.close()

    def stage_xattn(l, xsrc):
        TT = 512
        st = Stage()
        wq = st.sb("wq", [128, 8, D], BF16)
        wkv = st.sb("wkv", [128, 8, 2 * D], BF16)
        wo = st.sb("wo", [128, 8, D], BF16)
        memf = st.sb("memf", [128, 8, NM], F32)
        memn = st.sb("memn", [128, 8, NM], BF16)
        KT = st.sb("KT", [128, 8, NM], BF16)
        V = st.sb("V", [128, 2, D], BF16)
        xa = st.sb("xa", [128, 8, TT], F32)
        xb = st.sb("xb", [128, 8, TT], F32)
        sq = st.sb("sq", [128, 8, TT], BF16)
        h = st.sb("h", [128, 8, TT], BF16)
        rstd = st.sb("rstd", [128, TT], F32)
        qT = st.sb("qT", [128, 8, TT], BF16)
        eT = [st.sb("eT%d" % b, [128, 2, TT], BF16) for b in range(2)]
        rden = [st.sb("rden%d" % b, [128, TT], F32) for b in range(2)]
        oT = st.sb("oT", [128, 8, TT], BF16)
        fsb = st.sb("fsb", [128, 8, TT], F32)
        ssps = st.ps("ssps", [128, 512])
        qps = [st.ps("qps%d" % b, [128, 512]) for b in range(2)]
        sps = [st.ps("sps%d" % b, [128, 512]) for b in range(2)]
        dps = st.ps("dps", [128, 512])
        ops_ = [st.ps("ops%d" % b, [128, 512]) for b in range(2)]
        g512 = lambda n: [(i, (i * 512, (i + 1) * 512)) for i in range(n)]
        wkvg, wqg, wog = g512(4), g512(2), g512(2)
        load_wc(wkv, W["xa_kv_w"][l], 8, "wkv", wkvg)
        load_wc(wq, W["xa_q_w"][l], 8, "wq", wqg)
        load_wc(wo, W["xa_o_w"][l], 8, "wo", wog)
        S.dma("sp", memf[:], memT.rearrange("(kt p) m -> p kt m", p=128), w=["memf"])
        prenorm(memf, "memf", ("mg", l), memn, NM, sq, ssps, rstd)
        for dt in range(8):
            qp = qps[dt % 2]
            for kt in range(8):
                mm(qp[:, :NM], wkv[:, kt, dt * 128:(dt + 1) * 128], memn[:, kt, :], kt == 0, kt == 7,
                   r=[gtok("wkv", wkvg, dt * 128), ("h", kt)], w=[("qps", dt % 2)])
            act(KT[:, dt, :], qp[:, :NM], AF.Copy, r=[("qps", dt % 2)], w=["KT"])
        for mt in range(2):
            for nb in range(2):
                qp = qps[nb]
                for kt in range(8):
                    mm(qp[:], memn[:, kt, mt * 128:(mt + 1) * 128], wkv[:, kt, D + nb * 512:D + (nb + 1) * 512],
                       kt == 0, kt == 7, r=[gtok("wkv", wkvg, D + nb * 512), ("h", kt)], w=[("qps", nb)])
                act(V[:, mt, nb * 512:(nb + 1) * 512], qp[:], AF.Copy, r=[("qps", nb)], w=["V"])
        sq2 = st.sb("sq2", [128, 8, TT], BF16)
        rstd2 = st.sb("rstd2", [128, TT], F32)
        NT = L // TT

        def pre(ti):
            S.dma("sp", xa[:], xr_ap(xsrc, ti * TT, TT), r=xr_tok(ti * TT, TT), w=["xa"])
            prenorm(xa, "xa", ("ng", l, 2), h, TT, sq, ssps, rstd)

        pre(0)
        S.dma("sp", xb[:], xr_ap(xsrc, 0, TT), r=xr_tok(0, TT), w=["xb"])
        for ti in range(NT):
            t0 = ti * TT
            for dt in range(8):
                qp = qps[dt % 2]
                for kt in range(8):
                    mm(qp[:], wq[:, kt, dt * 128:(dt + 1) * 128], h[:, kt, :], kt == 0, kt == 7,
                       r=[gtok("wq", wqg, dt * 128), ("h", kt)], w=[("qps", dt % 2)])
                act(qT[:, dt, :], qp[:], AF.Identity, r=[("qps", dt % 2)], w=[("qT", dt)], scale=1.0 / 16.0)
            for hd in range(4):
                b = hd % 2
                for mt in range(2):
                    for j in range(2):
                        mm(sps[mt][:], KT[:, 2 * hd + j, mt * 128:(mt + 1) * 128], qT[:, 2 * hd + j, :], j == 0, j == 1,
                           r=["KT", ("qT", 2 * hd + j)], w=[("sps", mt)])
                    act(eT[b][:, mt, :], sps[mt][:], AF.Exp, r=[("sps", mt)], w=[("eT", b, mt)])
                for mt in range(2):
                    mm(dps[:], ones_bf[:], eT[b][:, mt, :], mt == 0, mt == 1, r=[("eT", b, mt)], w=["dps"])
                act(rden[b][:], dps[:], AF.Ln, r=["dps"], w=[("rden", b)])
                act(rden[b][:], rden[b][:], AF.Exp, r=[("rden", b)], w=[("rden", b)], scale=-1.0)
                for j in range(2):
                    op_ = ops_[j]
                    for mt in range(2):
                        mm(op_[:], V[:, mt, (2 * hd + j) * 128:(2 * hd + j + 1) * 128], eT[b][:, mt, :], mt == 0, mt == 1,
                           r=["V", ("eT", b, mt)], w=[("ops", j)])
                    tt("dve", oT[:, 2 * hd + j, :], op_[:], rden[b][:], ALU.mult, r=[("ops", j), ("rden", b)],
                       w=[("oT", 2 * hd + j)])
            if ti + 1 < NT:
                pre(ti + 1)
            for dt in range(8):
                qp = qps[dt % 2]
                for kt in range(8):
                    mm(qp[:], wo[:, kt, dt * 128:(dt + 1) * 128], oT[:, kt, :], kt == 0, kt == 7,
                       r=[gtok("wo", wog, dt * 128), ("oT", kt)], w=[("qps", dt % 2)])
                post_evac(dt, qp[:], ("qps", dt % 2), TT, fsb, sq2, pp="p")
            post_finish(("ng", l, 3), xb, TT, t0, fsb, sq2, dps, rstd2, pp="p", sstok="dps")
            if ti + 1 < NT:
                S.dma("sp", xb[:], xr_ap(xsrc, t0 + TT, TT), r=xr_tok(t0 + TT, TT), w=["xb"])
        st.close()

    def stage_conf(l, xsrc):
        j = l // 2
        TT = 256
        HL = CFK - 1
        NT = L // TT
        st = Stage()
        pw1 = st.sb("pw1", [128, 8, 2 * D], BF16)
        pw2 = st.sb("pw2", [128, 8, D], BF16)
        dg = st.sb("dg", [128, CFK * 8, 128], BF16)
        xa = st.sb("xa", [128, 8, TT], F32)
        xb = st.sb("xb", [128, 8, TT], F32)
        sq = st.sb("sq", [128, 8, TT], BF16)
        sqc = st.sb("sqc", [128, 8, TT], BF16)
        sq2 = st.sb("sq2", [128, 8, TT], BF16)
        h = st.sb("h", [128, 8, TT], BF16)
        rstd = st.sb("rstd", [128, TT], F32)
        rstdc = st.sb("rstdc", [128, TT], F32)
        rstd2 = st.sb("rstd2", [128, TT], F32)
        sig = [st.sb("sig%d" % b, [128, TT], F32) for b in range(2)]
        glu = [st.sb("glu%d" % b, [128, 8, HL + TT], BF16) for b in range(2)]
        cs = st.sb("cs", [128, 8, TT], F32)
        cb = st.sb("cb", [128, 8, TT], BF16)
        accd = [st.sb("accd%d" % b, [128, TT], F32) for b in range(4)]
        mean = st.sb("mean", [128, TT], F32)
        var = st.sb("var", [128, TT], F32)
        t1 = [st.sb("t1%d" % b, [128, TT], F32) for b in range(2)]
        sn = st.sb("sn", [128, 8, TT], BF16)
        fsb = st.sb("fsb", [128, 8, TT], F32)
        ssps = st.ps("ssps", [128, 512])
        aps = [st.ps("aps%d" % b, [128, 512]) for b in range(2)]
        gps = [st.ps("gps%d" % b, [128, 512]) for b in range(2)]
        cps = [st.ps("cps%d" % b, [128, 512]) for b in range(2)]
        mps = st.ps("mps", [128, 512])
        pw1g = [(0, (0, 512)), (1, (D, D + 512)), (2, (512, D)), (3, (D + 512, 2 * D))]
        load_wc(pw1, W["cf_pw1_w"][j], 8, "pw1", pw1g)
        load_w(pw2, W["cf_pw2_w"][j], 8, "pw2", per=4)
        dwo = voff[("dww", j)]
        for c in range(8):
            tt("dve", dg[:, c * CFK:(c + 1) * CFK, :],
               ident_f.unsqueeze(1).to_broadcast([128, CFK, 128]),
               vecs[:, dwo + c:dwo + c + CFK * 8:8].unsqueeze(2).to_broadcast([128, CFK, 128]), ALU.mult,
               r=["cst"], w=[("dg", c)])
        for c in range(8):
            mset("pool", glu[1][:, c, TT:TT + HL], 0.0, w=[("glu", 1, c)])

        def pre(ti):
            t0 = ti * TT
            S.dma("sp", xa[:], xr_ap(xsrc, t0, TT), r=xr_tok(t0, TT), w=["xa"])
            prenorm(xa, "xa", ("ng", l, 0), h, TT, sq, ssps, rstd)

        def pw1glu(ti):
            gb = ti % 2
            G_, Gp = glu[gb], glu[1 - gb]
            for c in range(8):
                b = c % 2
                for kt in range(8):
                    mm(aps[b][:, :TT], pw1[:, kt, c * 128:(c + 1) * 128], h[:, kt, :], kt == 0, kt == 7,
                       r=[gtok("pw1", pw1g, c * 128), ("h", kt)], w=[("aps", b)])
                for kt in range(8):
                    mm(gps[b][:, :TT], pw1[:, kt, D + c * 128:D + (c + 1) * 128], h[:, kt, :], kt == 0, kt == 7,
                       r=[gtok("pw1", pw1g, D + c * 128), ("h", kt)], w=[("gps", b)])
                act(sig[b][:], gps[b][:, :TT], AF.Sigmoid, r=[("gps", b)], w=[("sig", b)], bias=vcol(("pw1b", j), 8 + c))
                cp("pool", G_[:, c, 0:HL], Gp[:, c, TT:TT + HL], r=[("glu", 1 - gb, c)], w=[("gluh", gb, c)])
                stt("dve", G_[:, c, HL:HL + TT], aps[b][:, :TT], vcol(("pw1b", j), c), sig[b][:], ALU.add, ALU.mult,
                    r=[("aps", b), ("sig", b)], w=[("glu", gb, c)])

        KD = 6

        def conv(ti):
            gb = ti % 2
            G_ = glu[gb]

            def taps(c0):
                for k in range(KD):
                    for c in (c0, c0 + 1):
                        gr = [("glu", gb, c), ("gluh", gb, c)]
                        ab = (c0 // 2) % 2 * 2 + c % 2
                        if k == 0:
                            ts("dve", accd[ab][:], G_[:, c, 0:TT], vcol(("dww", j), c), None, ALU.mult, None,
                               r=gr, w=[("accd", ab)])
                        else:
                            stt("dve", accd[ab][:], G_[:, c, k:k + TT], vcol(("dww", j), k * 8 + c), accd[ab][:],
                                ALU.mult, ALU.add, r=gr + [("accd", ab)], w=[("accd", ab)])

            taps(0)
            for c0 in range(0, 8, 2):
                for c in (c0, c0 + 1):
                    b = c % 2
                    for k in range(KD, CFK):
                        mm(cps[b][:, :TT], dg[:, c * CFK + k, :], G_[:, c, k:k + TT], k == KD, k == CFK - 1,
                           r=[("dg", c), ("glu", gb, c), ("gluh", gb, c)], w=[("cps", b)])
                if c0 + 2 < 8:
                    taps(c0 + 2)
                for c in (c0, c0 + 1):
                    b = c % 2
                    ab = (c0 // 2) % 2 * 2 + c % 2
                    stt("dve", cs[:, c, :], cps[b][:, :TT], vcol(("dwb", j), c), accd[ab][:], ALU.add, ALU.add,
                        r=[("cps", b), ("accd", ab)], w=[("cs", c)])
                    act(sqc[:, c, :], cs[:, c, :], AF.Square, r=[("cs", c)], w=[("sqc", c)])
                    act(cb[:, c, :], cs[:, c, :], AF.Copy, r=[("cs", c)], w=[("cb", c)])

        def lnorm(ti):
            for c in range(8):
                mm(mps[:, 0:TT], onesD_bf[:], cb[:, c, :], c == 0, c == 7, r=[("cb", c)], w=["mps"])
            for c in range(8):
                mm(mps[:, TT:2 * TT], onesD_bf[:], sqc[:, c, :], c == 0, c == 7, r=[("sqc", c)], w=["mps"])
            cp("dve", mean[:], mps[:, 0:TT], r=["mps"], w=["mean"])
            tt("dve", var[:], mean[:], mean[:], ALU.mult, r=["mean"], w=["var"])
            tt("dve", var[:], mps[:, TT:2 * TT], var[:], ALU.subtract, r=["mps", "var"], w=["var"])
            rsqrt_act(rstdc[:], var[:], 1.0, r=["var"], w=["rstdc"])
            for c in range(8):
                b = c % 2
                tt("pool", t1[b][:], cs[:, c, :], mean[:], ALU.subtract, r=[("cs", c), "mean"], w=[("t1", b)])
                stt("dve", t1[b][:], t1[b][:], vcol(("lng", j), c), rstdc[:], ALU.mult, ALU.mult,
                    r=[("t1", b), "rstdc"], w=[("t1", b)])
                act(sn[:, c, :], t1[b][:], AF.Silu, r=[("t1", b)], w=[("sn", c)], bias=vcol(("lnb", j), c))

        def pw2post(ti):
            t0 = ti * TT
            for dt in range(8):
                b = dt % 2
                for c in range(8):
                    mm(cps[b][:, :TT], pw2[:, c, dt * 128:(dt + 1) * 128], sn[:, c, :], c == 0, c == 7,
                       r=[("pw2", c), ("sn", c)], w=[("cps", b)])
                post_evac(dt, cps[b][:, :TT], ("cps", b), TT, fsb, sq2, bias=vcol(("pw2b", j), dt), pp="p")
            post_finish(("ng", l, 1), xb, TT, t0, fsb, sq2, ssps, rstd2, pp="p", sstok="ssps")

        pre(0)
        S.dma("sp", xb[:], xr_ap(xsrc, 0, TT), r=xr_tok(0, TT), w=["xb"])
        pw1glu(0)
        for ti in range(NT):
            conv(ti)
            if ti + 1 < NT:
                pre(ti + 1)
                pw1glu(ti + 1)
            lnorm(ti)
            pw2post(ti)
            if ti + 1 < NT:
                S.dma("sp", xb[:], xr_ap(xsrc, (ti + 1) * TT, TT), r=xr_tok((ti + 1) * TT, TT), w=["xb"])
        st.close()

    def stage_ssd_a(l, xsrc, wout=None):
        j = l // 2
        TT = 512
        st = Stage()
        win = st.sb("win", [128, 8, INP], BF16)
        xa = st.sb("xa", [128, 8, TT], F32)
        sq = st.sb("sq", [128, 8, TT], BF16)
        hh_ = [st.sb("h%d" % b, [128, 8, TT], BF16) for b in range(2)]
        rstd = st.sb("rstd", [128, TT], F32)
        zs = [st.sb("zs%d" % b, [128, DI], BF16) for b in range(2)]
        ub = [st.sb("ub%d" % b, [128, TT + 3], F32) for b in range(2)]
        acc = [st.sb("acc%d" % b, [128, TT], F32) for b in range(2)]
        xbs = [st.sb("xbs%d" % b, [128, TT], BF16) for b in range(4)]
        halo = st.sb("halo", [128, 24, 3], F32)
        bv = st.sb("bv", [128, 96], F32)
        Ab = st.sb("Ab", [128, 32], F32)
        dtt = [st.sb("dtt%d" % b, [128, 32], F32) for b in range(2)]
        dta = [st.sb("dta%d" % b, [128, 64], F32) for b in range(2)]
        ssps = st.ps("ssps", [128, 512])
        zps = [st.ps("zps%d" % b, [128, 512]) for b in range(2)]
        ups = [st.ps("ups%d" % b, [128, 512]) for b in range(2)]
        dps = st.ps("dps", [128, 512])
        wing = [(i, (i * 512, (i + 1) * 512)) for i in range(4)] + [(4, (DI + CONVD, INP))] + \
               [(5 + i, (DI + i * 512, DI + (i + 1) * 512)) for i in range(6)]
        load_wc(win, W["ssm_in_w"][j], 8, "win", wing)
        if wout is not None:
            load_w(wout, W["ssm_out_w"][j], 16, "wout", per=4)
        S.dma("sp", bv[:], bvecs_d[:, j * 96:(j + 1) * 96], w=["bv"])
        act(Ab[:], bv[:, 32:64], AF.Exp, r=["bv"], w=["Ab"])
        ts("dve", Ab[:], Ab[:], -1.0, None, ALU.mult, None, r=["Ab"], w=["Ab"])
        mset("pool", halo[:], 0.0, w=[("halo", c) for c in range(24)])
        NT = L // TT

        def pre(ti):
            S.dma("sp", xa[:], xr_ap(xsrc, ti * TT, TT), r=xr_tok(ti * TT, TT), w=["xa"])
            prenorm(xa, "xa", ("ng", l, 0), hh_[ti % 2], TT, sq, ssps, rstd, htok="h%d" % (ti % 2))

        pre(0)
        for ti in range(NT):
            t0 = ti * TT
            h = hh_[ti % 2]
            ht = "h%d" % (ti % 2)
            for ck in range(4):
                cg = ti * 4 + ck
                zb = zs[ck % 2]
                for nb in range(4):
                    zp = zps[nb % 2]
                    for kt in range(8):
                        mm(zp[:], h[:, kt, ck * 128:(ck + 1) * 128], win[:, kt, nb * 512:(nb + 1) * 512], kt == 0, kt == 7,
                           r=[gtok("win", wing, nb * 512), (ht, kt)], w=[("zps", nb % 2)])
                    act(zb[:, nb * 512:(nb + 1) * 512], zp[:], AF.Silu, r=[("zps", nb % 2)], w=[("zs", ck % 2)])
                S.dma("sp", ZS[cg * 128:(cg + 1) * 128, :], zb[:], r=[("zs", ck % 2)], w=[("zsd", cg)])
                for kt in range(8):
                    mm(dps[:, 0:32], h[:, kt, ck * 128:(ck + 1) * 128], win[:, kt, DI + CONVD:INP], kt == 0, kt == 7,
                       r=[("win", 4), (ht, kt)], w=["dps"])
                db = ck % 2
                tt("dve", dtt[db][:], dps[:, 0:32], bv[:, 0:32], ALU.add, r=["dps", "bv"], w=[("dtt", db)])
                act(dtt[db][:], dtt[db][:], AF.Exp, r=[("dtt", db)], w=[("dtt", db)])
                act(dta[db][:, 0:32], dtt[db][:], AF.Ln, r=[("dtt", db)], w=[("dta", db)], bias=onec[:])
                tt("dve", dta[db][:, 32:64], dta[db][:, 0:32], Ab[:], ALU.mult, r=[("dta", db), "Ab"], w=[("dta", db)])
                S.dma("sp", DTA[cg * 128:(cg + 1) * 128, :], dta[db][:], r=[("dta", db)], w=[("dtad", cg)])
            def finish_c(c):
                b = c % 2
                xs = xbs[c % 4]
                act(xs[:], acc[b][:], AF.Silu, r=[("acc", b)], w=[("xbs", c % 4)])
                S.dma("sp", XBC[c * 128:(c + 1) * 128, t0:t0 + TT], xs[:], r=[("xbs", c % 4)], w=[("xbcd", ti, c)])

            for c in range(24):
                b = c % 2
                col0 = DI + c * 128
                for kt in range(8):
                    mm(ups[b][:], win[:, kt, col0:col0 + 128], h[:, kt, :], kt == 0, kt == 7,
                       r=[gtok("win", wing, col0), (ht, kt)], w=[("ups", b)])
                u, ac = ub[b], acc[b]
                cp("pool", u[:, 0:3], halo[:, c, :], r=[("halo", c)], w=[("ubh", b)])
                act(u[:, 3:3 + TT], ups[b][:], AF.Copy, r=[("ups", b)], w=[("ub", b)])
                act(ac[:], ups[b][:], AF.Identity, r=[("ups", b)], w=[("acc", b)],
                    scale=vcol(("scw", j), 3 * 24 + c), bias=vcol(("scb", j), c))
                cp("pool", halo[:, c, :], u[:, TT:TT + 3], r=[("ub", b)], w=[("halo", c)])
                if c > 0:
                    finish_c(c - 1)
                if ti + 1 < NT:
                    nb_ = (ti + 1) % 2
                    if c == 4:
                        S.dma("sp", xa[:], xr_ap(xsrc, (ti + 1) * TT, TT), r=xr_tok((ti + 1) * TT, TT), w=["xa"])
                        prenorm_sq(xa, "xa", TT, sq)
                    elif c == 12:
                        prenorm_ss(TT, sq, ssps, rstd)
                    elif c == 18:
                        prenorm_h(xa, "xa", ("ng", l, 0), hh_[nb_], TT, rstd, htok="h%d" % nb_)
                for k in (2, 1, 0):
                    stt("dve", ac[:], u[:, k:k + TT], vcol(("scw", j), k * 24 + c), ac[:], ALU.mult, ALU.add,
                        r=[("ub", b), ("ubh", b), ("acc", b)], w=[("acc", b)])
            finish_c(23)
            if ti == NT - 1 and wout is not None:
                for c in range(16):
                    act(wout[:, c, :], wout[:, c, :], AF.Identity, r=[("wout", c)], w=[("wout", c)], scale=vcol(("sng", j), c))
        st.close()

    def stage_ssd_b(l, xsrc, wout_pre=None):
        j = l // 2
        TT = 256
        CPT = TT // 128
        NT = L // TT
        NCH = L // 128
        st = Stage()
        wout = wout_pre if wout_pre is not None else st.sb("wout", [128, 16, D], BF16)
        xbc = [st.sb("xbc%d" % b, [128, 24, TT], BF16) for b in range(2)]
        zsc = [st.sb("zsc%d" % b, [128, DI], BF16) for b in range(2)]
        dta = [st.sb("dtac%d" % b, [128, 64], F32) for b in range(2)]
        bv = st.sb("bv", [128, 96], F32)
        xtm = [st.sb("xtm%d" % b, [128, DI], BF16) for b in range(2)]
        xdt = [st.sb("xdt%d" % b, [128, DI], BF16) for b in range(2)]
        xdd = [st.sb("xdd%d" % b, [128, DI], BF16) for b in range(2)]
        xD = [st.sb("xD%d" % b, [128, DI], BF16) for b in range(2)]
        Btm = [st.sb("Btm%d" % b, [128, 512], BF16) for b in range(2)]
        cs_s = [st.sb("cs_s%d" % b, [128, 32], F32) for b in range(2)]
        ecs = [st.sb("ecs%d" % b, [128, 32], F32) for b in range(2)]
        dte = [st.sb("dte%d" % b, [128, 32], F32) for b in range(2)]
        dec = [st.sb("dec%d" % b, [128, 32], F32) for b in range(2)]
        CBm = [st.sb("CBm%d" % b, [128, 4, 128], BF16) for b in range(2)]
        R = [st.sb("R%d" % b, [128, 8, 128], F32) for b in range(2)]
        es_ = [[st.sb("es%d%d" % (a, b), [128, 512], BF16) for b in range(2)] for a in range(2)]
        M = [[st.sb("M%d%d" % (a, b), [128, 4, 128], BF16) for b in range(2)] for a in range(2)]
        stf = st.sb("stf", [128, 4, 512], F32)
        stb = st.sb("stb", [128, 4, 512], BF16)
        t1 = [st.sb("t1%d" % b, [128, 512], F32) for b in range(2)]
        ytm = st.sb("ytm", [128, DI], F32)
        ss1 = st.sb("ss1", [128, 1], F32)
        rs1 = st.sb("rs1", [128, 1], F32)
        yn = st.sb("yn", [128, DI], BF16)
        ynT = st.sb("ynT", [128, 16, TT], BF16)
        xb = st.sb("xb", [128, 8, TT], F32)
        fsb = st.sb("fsb", [128, 8, TT], F32)
        sq = st.sb("sq", [128, 8, TT], BF16)
        rstd = st.sb("rstd", [128, TT], F32)
        tp = st.ps("tp", [128, DI], BF16)
        smp = st.ps("smp", [128, 512])
        cbp = st.ps("cbp", [128, 512])
        sgp = [st.ps("sgp%d" % b, [128, 512]) for b in range(2)]
        ydp = st.ps("ydp", [128, 512])
        yop = st.ps("yop", [128, 512])
        if wout_pre is None:
            load_w(wout, W["ssm_out_w"][j], 16, "wout", per=4)
            for c in range(16):
                ts("dve" if c % 2 == 0 else "pool", wout[:, c, :], wout[:, c, :], vcol(("sng", j), c), None, ALU.mult, None,
                   r=[("wout", c)], w=[("wout", c)])
        S.dma("sp", bv[:], bvecs_d[:, j * 96:(j + 1) * 96], w=["bv"])
        mset("dve", stf[:], 0.0, w=[("stf", g) for g in range(NG)])
        mset("pool", stb[:], 0.0, w=[("stb", g) for g in range(NG)])
        hq = "p (h q) -> p h q"

        def load_tile(ti):
            t0 = ti * TT
            for c0 in range(0, 24, 8):
                S.dma("sp", xbc[ti % 2][:, c0:c0 + 8, :],
                      XBC[c0 * 128:(c0 + 8) * 128, t0:t0 + TT].rearrange("(c p) t -> p c t", p=128),
                      r=[("xbcd", (t0 // 512), c) for c in range(c0, c0 + 8)],
                      w=[("xbc", ti % 2, c) for c in range(c0, c0 + 8)])

        def front(c):
            cb, ti, ck = c % 2, c // CPT, c % CPT
            X = xbc[ti % 2]
            sl = slice(ck * 128, (ck + 1) * 128)
            S.dma("sp", zsc[cb][:], ZS[c * 128:(c + 1) * 128, :], r=[("zsd", c)], w=[("zsc", cb)])
            S.dma("sp", dta[cb][:], DTA[c * 128:(c + 1) * 128, :], r=[("dtad", c)], w=[("dtac", cb)])
            dtv = dta[cb][:, 0:32]
            av = dta[cb][:, 32:64]
            for g in range(NG):
                tr(tp[:, g * 128:(g + 1) * 128], X[:, 16 + g, sl], r=[("xbc", ti % 2, 16 + g)], w=["tp"])
            cp("dve", Btm[cb][:], tp[:, 0:512], r=["tp"], w=[("Btm", cb)])
            for ch in range(16):
                tr(tp[:, ch * 128:(ch + 1) * 128], X[:, ch, sl], r=[("xbc", ti % 2, ch)], w=["tp"])
            cp("dve", xtm[cb][:], tp[:], r=["tp"], w=[("xtm", cb)])
            tt("dve", xdt[cb][:].rearrange(hq, q=64), tp[:].rearrange(hq, q=64),
               dtv.unsqueeze(2).to_broadcast([128, 32, 64]), ALU.mult, r=["tp", ("dtac", cb)], w=[("xdt", cb)])
            tt("pool", xD[cb][:].rearrange(hq, q=64), xtm[cb][:].rearrange(hq, q=64),
               bv[:, 64:96].unsqueeze(2).to_broadcast([128, 32, 64]), ALU.mult, r=[("xtm", cb), "bv"], w=[("xD", cb)])
            mm(smp[:, 0:32], U_f, av, True, True, r=["cst", ("dtac", cb)], w=["smp"])
            mm(smp[:, 32:64], ones_f, av, True, True, r=["cst", ("dtac", cb)], w=["smp"])
            cp("dve", cs_s[cb][:], smp[:, 0:32], r=["smp"], w=[("cs_s", cb)])
            act(ecs[cb][:], smp[:, 0:32], AF.Exp, r=["smp"], w=[("ecs", cb)])
            tt("dve", dte[cb][:], smp[:, 32:64], cs_s[cb][:], ALU.subtract, r=["smp", ("cs_s", cb)], w=[("dte", cb)])
            act(dte[cb][:], dte[cb][:], AF.Exp, r=[("dte", cb)], w=[("dte", cb)])
            act(dec[cb][:], smp[:, 32:64], AF.Exp, r=["smp"], w=[("dec", cb)])
            tt("pool", xdd[cb][:].rearrange(hq, q=64), xdt[cb][:].rearrange(hq, q=64),
               dte[cb][:].unsqueeze(2).to_broadcast([128, 32, 64]), ALU.mult, r=[("xdt", cb), ("dte", cb)], w=[("xdd", cb)])
            for g in range(NG):
                mm(cbp[:, g * 128:(g + 1) * 128], X[:, 16 + g, sl], X[:, 20 + g, sl], True, True,
                   r=[("xbc", ti % 2, 16 + g), ("xbc", ti % 2, 20 + g)], w=["ssps"])
            tt("dve", CBm[cb][:], cbp[:].rearrange("p (g i) -> p g i", g=4), U_f.unsqueeze(1).to_broadcast([128, 4, 128]),
               ALU.mult, r=["ssps", "cst"], w=[("CBm", cb)])

        def build_R_c(c, g):
            cb = c % 2
            av = dta[cb][:, 32:64]
            tt("pool", R[g % 2][:], U_f.unsqueeze(1).to_broadcast([128, 8, 128]),
               av[:, g * 8:(g + 1) * 8].unsqueeze(2).to_broadcast([128, 8, 128]), ALU.mult,
               r=["cst", ("dtac", cb)], w=[("R", g % 2)])

        def decay_c(c, g, hh):
            cb = c % 2
            gb = g % 2
            mm(sgp[hh][:], Lm_f, R[gb][:, hh * 4:(hh + 1) * 4, :].rearrange("p h i -> p (h i)"), True, True,
               r=["cst", ("R", gb)], w=[("sgp", hh)])
            act(es_[gb][hh][:], sgp[hh][:], AF.Exp, r=[("sgp", hh)], w=[("es", gb, hh)])
            tt("dve", M[gb][hh][:], es_[gb][hh][:].rearrange("p (h i) -> p h i", h=4),
               CBm[cb][:, g, :].unsqueeze(1).to_broadcast([128, 4, 128]), ALU.mult,
               r=[("es", gb, hh), ("CBm", cb)], w=[("M", gb, hh)])

        def decay0(c):
            build_R_c(c, 0)
            decay_c(c, 0, 0)
            decay_c(c, 0, 1)

        def middle(c):
            cb, ti, ck = c % 2, c // CPT, c % CPT
            X = xbc[ti % 2]
            sl = slice(ck * 128, (ck + 1) * 128)
            av = dta[cb][:, 32:64]

            def build_R(g):
                build_R_c(c, g)

            def decay(g, hh):
                decay_c(c, g, hh)

            decay0(c)
            for g in range(NG):
                gb = g % 2
                ydp_g, ydtok = (ydp, "ydp") if gb == 0 else (smp, "smp")
                if g + 1 < NG:
                    build_R(g + 1)
                for hh in range(2):
                    for hl in range(4):
                        hg = g * 8 + hh * 4 + hl
                        o0 = (hh * 4 + hl) * 64
                        mm(ydp_g[:, o0:o0 + 64], M[gb][hh][:, hl, :], xdt[cb][:, hg * 64:(hg + 1) * 64], True, False,
                           r=[("M", gb, hh), ("xdt", cb)], w=[ydtok])
                        mm(ydp_g[:, o0:o0 + 64], ident_bf[:], xD[cb][:, hg * 64:(hg + 1) * 64], False, True,
                           r=["cbf", ("xD", cb)], w=[ydtok])
                    if g + 1 < NG:
                        decay(g + 1, hh)
                mm(yop[:], X[:, 20 + g, sl], stb[:, g, :], True, True, r=[("xbc", ti % 2, 20 + g), ("stb", g)], w=["yop"])
                mm(cbp[:], Btm[cb][:, g * 128:(g + 1) * 128], xdd[cb][:, g * 512:(g + 1) * 512], True, True,
                   r=[("Btm", cb), ("xdd", cb)], w=["ssps"])
                tt("dve", t1[gb][:].rearrange(hq, q=64), yop[:].rearrange(hq, q=64),
                   ecs[cb][:, g * 8:(g + 1) * 8].unsqueeze(2).to_broadcast([128, 8, 64]), ALU.mult,
                   r=["yop", ("ecs", cb)], w=[("t1", gb)])
                tt("dve", stf[:, g, :].rearrange(hq, q=64), stf[:, g, :].rearrange(hq, q=64),
                   dec[cb][:, g * 8:(g + 1) * 8].unsqueeze(2).to_broadcast([128, 8, 64]), ALU.mult,
                   r=[("stf", g), ("dec", cb)], w=[("stf", g)])
                tt("dve", ytm[:, g * 512:(g + 1) * 512], t1[gb][:], ydp_g[:], ALU.add, r=[("t1", gb), ydtok], w=[("ytm", g)])
                tt("dve", stf[:, g, :], stf[:, g, :], cbp[:], ALU.add, r=[("stf", g), "ssps"], w=[("stf", g)])
                act(stb[:, g, :], stf[:, g, :], AF.Copy, r=[("stf", g)], w=[("stb", g)])

        def back1(c):
            cb = c % 2
            ytoks = [("ytm", g) for g in range(NG)]
            tt("dve", ytm[:], ytm[:], zsc[cb][:], ALU.mult, r=ytoks + [("zsc", cb)], w=ytoks)
            mset("dve", ss1[:], 0.0, w=["ss1"])
            act(yn[:], ytm[:], AF.Square, r=ytoks + ["ss1"], w=["yn", "ss1"], accum=ss1[:])
            rsqrt_act(rs1[:], ss1[:], 1.0 / DI, r=["ss1"], w=["rs1"])
            act(yn[:], ytm[:], AF.Identity, r=ytoks + ["rs1"], w=["yn"], scale=rs1[:])

        def back2(c):
            ti, ck = c // CPT, c % CPT
            sl = slice(ck * 128, (ck + 1) * 128)
            for ch in range(16):
                tr(tp[:, ch * 128:(ch + 1) * 128], yn[:, ch * 128:(ch + 1) * 128], r=["yn"], w=["tp"])
            for ch in range(16):
                cp("dve", ynT[:, ch, sl], tp[:, ch * 128:(ch + 1) * 128], r=["tp"], w=[("ynT", ch)])
            if ck == CPT - 1:
                t0 = ti * TT
                for dt in range(8):
                    fp = sgp[dt % 2]
                    for ch in range(16):
                        mm(fp[:, :TT], wout[:, ch, dt * 128:(dt + 1) * 128], ynT[:, ch, :], ch == 0, ch == 15,
                           r=[("wout", ch), ("ynT", ch)], w=[("sgp", dt % 2)])
                    post_evac(dt, fp[:, :TT], ("sgp", dt % 2), TT, fsb, sq)
                pending.append(ti)

        pending = []

        def flush_post():
            while pending:
                ti = pending.pop(0)
                t0 = ti * TT
                post_finish(("ng", l, 1), xb, TT, t0, fsb, sq, cbp, rstd, add_eng="dve")
                if ti + 1 < NT:
                    S.dma("sp", xb[:], xr_ap(xsrc, t0 + TT, TT), r=xr_tok(t0 + TT, TT), w=["xb"])

        load_tile(0)
        S.dma("sp", xb[:], xr_ap(xsrc, 0, TT), r=xr_tok(0, TT), w=["xb"])
        front(0)
        for c in range(NCH):
            if c % CPT == 0 and c // CPT + 1 < NT:
                load_tile(c // CPT + 1)
            middle(c)
            back1(c)
            if c + 1 < NCH:
                front(c + 1)
            flush_post()
            back2(c)
        flush_post()
        st.close()

    first = True
    for (l, s) in stages:
        xsrc = xT if first else y
        if s == "mix":
            if l % 2 == 0:
                with nc.sbuf_tensor("wout_l%d" % l, [128, 16, D], BF16) as wout_t:
                    stage_ssd_a(l, xsrc, wout_t)
                    stage_ssd_b(l, xsrc, wout_t)
            else:
                stage_conf(l, xsrc)
        elif s == "ssda":
            stage_ssd_a(l, xsrc)
        elif s == "ssdb":
            stage_ssd_b(l, xsrc)
        elif s == "xa":
            stage_xattn(l, xsrc)
        else:
            stage_ffn(l, xsrc)
        first = False
    S.finish()
    return nc


_PROGRAM_CACHE = {}


def make_in_maps(inp, L=L):
    vecs, bv, consts = build_vecs(inp)
    x = np.asarray(inp["x"], np.float32)[:, :L]
    mem = np.asarray(inp["mem"], np.float32)
    shared = {"vecs": vecs, "bvecs": bv, "consts": consts}
    for k in WEIGHT_SHAPES:
        shared[k] = np.ascontiguousarray(np.asarray(inp[k], np.float32))
    maps = []
    for b in range(x.shape[0]):
        m = dict(shared)
        m["xT"] = np.ascontiguousarray(x[b].T)
        m["memT"] = np.ascontiguousarray(mem[b].T)
        maps.append(m)
    return maps


def kernel(**inputs):
    maps = make_in_maps(inputs)
    nc = build_program()
    res = run_bass_kernel_spmd(nc, maps, core_ids=list(range(8)))
    out = np.stack([np.asarray(r["y"], np.float32).T for r in res.results], axis=0)
    return np.ascontiguousarray(out)
```

```python
from contextlib import ExitStack
import numpy as np
import concourse.bass as bass
import concourse.mybir as mybir
from concourse.bass_utils import run_bass_kernel_spmd

F32 = mybir.dt.float32
BF16 = mybir.dt.bfloat16
AF = mybir.ActivationFunctionType
ALU = mybir.AluOpType

D = 1024
L = 4096
NL = 4
DI = 2048
NH = 32
NG = 4
CONVD = 3072
INP = 5152
DFF = 2816
NM = 256
CFK = 31
EPS = 1e-6
ENGS = ("pe", "act", "dve", "pool", "sp")
SYNC_SMALL_ONLY = False


class Sched:
    def __init__(self, nc, same_engine_sync=True, n_dma_sems=28, n_pool_dma_sems=12):
        self.nc = nc
        self.same = same_engine_sync
        self.prog = {e: [] for e in ENGS}
        self.cnt = {e: 0 for e in ENGS}
        self.known = {e: {} for e in ENGS}
        self.state = {}
        self.sems = {}
        self._ctx = []
        for e in ("pe", "act", "dve", "pool"):
            self.sems[e] = self._sem("s_" + e)
        self.dma_pool = {"sp": [], "pool": []}
        for i in range(n_dma_sems):
            k = ("dsp", i); self.sems[k] = self._sem("d_sp%d" % i); self.dma_pool["sp"].append(k)
        for i in range(n_pool_dma_sems):
            k = ("dpl", i); self.sems[k] = self._sem("d_pl%d" % i); self.dma_pool["pool"].append(k)
        self.dma_cnt = {k: 0 for q in self.dma_pool.values() for k in q}
        self.dma_rr = {"sp": 0, "pool": 0}

    def _sem(self, name):
        cm = self.nc.semaphore(name)
        h = cm.__enter__()
        self._ctx.append(cm)
        return h

    def _need(self, eng, needs, key, val, small=True):
        if key == eng and (eng == "pe" or not self.same or (SYNC_SMALL_ONLY and not small)):
            return
        if self.known[eng].get(key, 0) >= val:
            return
        if needs.get(key, 0) < val:
            needs[key] = val

    def _deps(self, eng, r, w):
        needs = {}
        for t in r:
            st = self.state.get(t)
            if st and st[0]:
                self._need(eng, needs, st[0][0], st[0][1], st[2])
        for t in w:
            st = self.state.get(t)
            if st:
                if st[0]:
                    self._need(eng, needs, st[0][0], st[0][1], st[2])
                for k, v in st[1].items():
                    self._need(eng, needs, k, v)
        return needs

    def _emit_waits(self, eng, needs):
        for k, v in needs.items():
            self.known[eng][k] = v
            sem = self.sems[k]
            self.prog[eng].append(lambda e, sem=sem, v=v: e.wait_ge(sem, v))

    def _commit(self, r, w, key, val, small=True):
        for t in r:
            st = self.state.setdefault(t, [None, {}, True])
            if st[1].get(key, 0) < val:
                st[1][key] = val
        for t in w:
            self.state[t] = [(key, val), {}, small]

    def op(self, eng, fn, r=(), w=(), small=False):
        needs = self._deps(eng, r, w)
        self._emit_waits(eng, needs)
        self.cnt[eng] += 1
        val = self.cnt[eng]
        sem = self.sems[eng]
        self.prog[eng].append(lambda e, fn=fn, sem=sem: fn(e).then_inc(sem, 1))
        self._commit(r, w, eng, val, small)

    def dma(self, q, out, in_, r=(), w=()):
        pool = self.dma_pool[q]
        k = pool[self.dma_rr[q] % len(pool)]
        self.dma_rr[q] += 1
        needs = self._deps(q, r, w)
        prev = 16 * self.dma_cnt[k]
        if prev:
            self._need(q, needs, k, prev)
        self._emit_waits(q, needs)
        self.dma_cnt[k] += 1
        val = 16 * self.dma_cnt[k]
        sem = self.sems[k]
        self.prog[q].append(lambda e, out=out, in_=in_, sem=sem: e.dma_start(out=out, in_=in_).then_inc(sem, 16))
        self._commit(r, w, k, val)

    def barrier(self):
        cur = {e: self.cnt[e] for e in ("pe", "act", "dve", "pool")}
        for k, c in self.dma_cnt.items():
            cur[k] = 16 * c
        for eng in ENGS:
            needs = {}
            for k, v in cur.items():
                if v and k != eng:
                    self._need(eng, needs, k, v)
            self._emit_waits(eng, needs)
        self.state = {}

    def finish(self):
        self.barrier()
        prog = self.prog
        with self.nc.Block() as block:
            @block.tensor
            def _(e):
                for f in prog["pe"]:
                    f(e)

            @block.scalar
            def _(e):
                for f in prog["act"]:
                    f(e)

            @block.vector
            def _(e):
                for f in prog["dve"]:
                    f(e)

            @block.gpsimd
            def _(e):
                for f in prog["pool"]:
                    f(e)

            @block.sync
            def _(e):
                for f in prog["sp"]:
                    f(e)
        for cm in reversed(self._ctx):
            cm.__exit__(None, None, None)


def vec_layout():
    off = {}
    n = 0

    def add(name, cols):
        nonlocal n
        off[name] = n
        n += cols

    for l in range(NL):
        for j in range(6):
            add(("ng", l, j), 8)
        add(("mg", l), 8)
        add(("fcw", l), 3 * 44)
        add(("fcb", l), 44)
    for j in range(2):
        add(("scw", j), 4 * 24)
        add(("scb", j), 24)
        add(("sng", j), 16)
        add(("pw1b", j), 16)
        add(("dww", j), CFK * 8)
        add(("dwb", j), 8)
        add(("lng", j), 8)
        add(("lnb", j), 8)
        add(("pw2b", j), 8)
    return off, n


def _cols(v):
    v = np.asarray(v, dtype=np.float32)
    if v.ndim == 1:
        return np.ascontiguousarray(v.reshape(-1, 128).T)
    k, c = v.shape
    return np.ascontiguousarray(v.reshape(k, c // 128, 128).transpose(2, 0, 1).reshape(128, k * (c // 128)))


def build_vecs(inp):
    off, n = vec_layout()
    vecs = np.zeros((128, n), np.float32)

    def put(name, v):
        c = _cols(v)
        vecs[:, off[name]:off[name] + c.shape[1]] = c

    for l in range(NL):
        for j in range(6):
            put(("ng", l, j), inp["norm_g"][l, j])
        put(("mg", l), inp["xa_mem_g"][l])
        put(("fcw", l), inp["ffn_conv_w"][l])
        put(("fcb", l), inp["ffn_conv_b"][l])
    for j in range(2):
        put(("scw", j), inp["ssm_conv_w"][j])
        put(("scb", j), inp["ssm_conv_b"][j])
        put(("sng", j), inp["ssm_norm_g"][j])
        put(("pw1b", j), inp["cf_pw1_b"][j])
        put(("dww", j), inp["cf_dw_w"][j])
        put(("dwb", j), inp["cf_dw_b"][j])
        put(("lng", j), inp["cf_ln_g"][j])
        put(("lnb", j), inp["cf_ln_b"][j])
        put(("pw2b", j), inp["cf_pw2_b"][j])
    bv = np.zeros((128, 2 * 96), np.float32)
    for j in range(2):
        bv[:, j * 96 + 0:j * 96 + 32] = np.asarray(inp["ssm_dt_bias"][j], np.float32)[None, :]
        bv[:, j * 96 + 32:j * 96 + 64] = np.asarray(inp["ssm_A_log"][j], np.float32)[None, :]
        bv[:, j * 96 + 64:j * 96 + 96] = np.asarray(inp["ssm_D"][j], np.float32)[None, :]
    consts = np.zeros((128, 512), np.float32)
    consts[:, 0:128] = np.eye(128, dtype=np.float32)
    consts[:, 128:256] = np.triu(np.ones((128, 128), np.float32))
    consts[:, 256:384] = np.tril(np.ones((128, 128), np.float32), -1)
    consts[:, 384:512] = 1.0
    return vecs, bv, consts


WEIGHT_SHAPES = {
    "ssm_in_w": [2, D, INP], "ssm_out_w": [2, DI, D],
    "cf_pw1_w": [2, D, 2 * D], "cf_pw2_w": [2, D, D],
    "xa_q_w": [NL, D, D], "xa_kv_w": [NL, D, 2 * D], "xa_o_w": [NL, D, D],
    "ffn_in_w": [NL, D, 2 * DFF], "ffn_out_w": [NL, DFF, D],
}

ALL_STAGES = [(l, s) for l in range(NL) for s in ("mix", "xa", "ffn")]


def build_program(stages=None, same_engine_sync=True, L=L):
    stages = ALL_STAGES if stages is None else stages
    nc = bass.Bass("TRN2", target_bir_lowering=False)
    S = Sched(nc, same_engine_sync=same_engine_sync)
    voff, NV = vec_layout()

    xT = nc.dram_tensor("xT", [D, L], F32, kind="ExternalInput").ap()
    memT = nc.dram_tensor("memT", [D, NM], F32, kind="ExternalInput").ap()
    vecs_d = nc.dram_tensor("vecs", [128, NV], F32, kind="ExternalInput").ap()
    bvecs_d = nc.dram_tensor("bvecs", [128, 192], F32, kind="ExternalInput").ap()
    consts_d = nc.dram_tensor("consts", [128, 512], F32, kind="ExternalInput").ap()
    W = {k: nc.dram_tensor(k, shp, F32, kind="ExternalInput").ap() for k, shp in WEIGHT_SHAPES.items()}
    y = nc.dram_tensor("y", [D, L], F32, kind="ExternalOutput").ap()
    ZS = nc.dram_tensor("zs_scr", [L, DI], BF16).ap()
    XBC = nc.dram_tensor("xbc_scr", [CONVD, L], BF16).ap()
    DTA = nc.dram_tensor("dta_scr", [L, 64], F32).ap()

    vecs = nc.alloc_sbuf_tensor("vecs_sb", [128, NV], F32)
    cst = nc.alloc_sbuf_tensor("cst_sb", [128, 512], F32)
    ident_bf = nc.alloc_sbuf_tensor("ident_bf", [128, 128], BF16)
    ones_bf = nc.alloc_sbuf_tensor("ones_bf", [128, 128], BF16)
    onesD_bf = nc.alloc_sbuf_tensor("onesD_bf", [128, 128], BF16)
    epsc = nc.alloc_sbuf_tensor("epsc", [128, 1], F32)
    onec = nc.alloc_sbuf_tensor("onec", [128, 1], F32)
    ident_f = cst[:, 0:128]
    U_f = cst[:, 128:256]
    Lm_f = cst[:, 256:384]
    ones_f = cst[:, 384:512]

    def _small(ap):
        try:
            return ap.free_size() < 128
        except Exception:
            return True

    def mm(out, lhsT, rhs, start, stop, r, w):
        S.op("pe", lambda e: e.matmul(out, lhsT=lhsT, rhs=rhs, start=start, stop=stop), r, w, small=_small(out))

    def tr(out, in_, r, w):
        S.op("pe", lambda e: e.transpose(out, in_, ident_bf[:]), r, w, small=_small(out))

    def act(out, in_, func, r, w, bias=None, scale=None, accum=None):
        kw = {}
        if bias is not None:
            kw["bias"] = bias
        if scale is not None:
            kw["scale"] = scale
        if accum is not None:
            kw["accum_out"] = accum
        S.op("act", lambda e: e.activation(out=out, in_=in_, func=func, **kw), r, w,
             small=(_small(out) or accum is not None))

    def tt(eng, out, in0, in1, op, r, w):
        S.op(eng, lambda e: e.tensor_tensor(out=out, in0=in0, in1=in1, op=op), r, w, small=_small(out))

    def ts(eng, out, in0, s1, s2, op0, op1, r, w):
        if op1 is None:
            S.op(eng, lambda e: e.tensor_scalar(out=out, in0=in0, scalar1=s1, scalar2=None, op0=op0), r, w, small=_small(out))
        else:
            S.op(eng, lambda e: e.tensor_scalar(out=out, in0=in0, scalar1=s1, scalar2=s2, op0=op0, op1=op1), r, w, small=_small(out))

    def stt(eng, out, in0, scalar, in1, op0, op1, r, w):
        S.op(eng, lambda e: e.scalar_tensor_tensor(out=out, in0=in0, scalar=scalar, in1=in1, op0=op0, op1=op1), r, w, small=_small(out))

    def rsqrt_act(out, in_, scale, r, w):
        act(out, in_, AF.Ln, r=list(r) + ["cbf"], w=w, bias=epsc[:], scale=scale)
        act(out, out, AF.Exp, r=w, w=w, scale=-0.5)

    def cp(eng, out, in_, r, w):
        S.op(eng, lambda e: e.tensor_copy(out=out, in_=in_), r, w, small=_small(out))

    def mset(eng, ap, val, w):
        S.op(eng, lambda e: e.memset(ap, val), (), w, small=_small(ap))

    def vcol(name, c):
        o = voff[name] + c
        return vecs[:, o:o + 1]

    def xr_ap(dram, t0, tl):
        return dram.rearrange("(kt p) l -> p kt l", p=128)[:, :, t0:t0 + tl]

    def xr_tok(t0, tl):
        return [("xr", i) for i in range(t0 // 256, (t0 + tl) // 256)]

    S.dma("sp", vecs[:], vecs_d, w=["vecs"])
    S.dma("sp", cst[:], consts_d, w=["cst"])
    cp("dve", ident_bf[:], ident_f, r=["cst"], w=["cbf"])
    cp("dve", ones_bf[:], ones_f, r=["cst"], w=["cbf"])
    ts("dve", onesD_bf[:], ones_f, 1.0 / D, None, ALU.mult, None, r=["cst"], w=["cbf"])
    mset("dve", epsc[:], EPS, w=["cbf"])
    mset("dve", onec[:], 1.0, w=["cbf"])
    S.barrier()

    class Stage:
        n = 0

        def __init__(self):
            self.es = ExitStack()
            Stage.n += 1
            self.sfx = "_s%d" % Stage.n

        def sb(self, name, shape, dt):
            return self.es.enter_context(nc.sbuf_tensor(name + self.sfx, shape, dt))

        def ps(self, name, shape, dt=F32):
            return self.es.enter_context(nc.psum_tensor(name + self.sfx, shape, dt))

        def close(self):
            S.barrier()
            self.es.close()

    def prenorm(xa, xtok, gname, h, TT, sq, ssps, rstd, htok="h"):
        for kt in range(8):
            act(sq[:, kt, :TT], xa[:, kt, :TT], AF.Square, r=[xtok], w=[("sq", kt)])
        for kt in range(8):
            mm(ssps[:, :TT], onesD_bf[:], sq[:, kt, :TT], kt == 0, kt == 7, r=[("sq", kt)], w=["ssps"])
        rsqrt_act(rstd[:, :TT], ssps[:, :TT], 1.0, r=["ssps"], w=["rstd"])
        for kt in range(8):
            stt("dve", h[:, kt, :TT], xa[:, kt, :TT], vcol(gname, kt), rstd[:, :TT],
                ALU.mult, ALU.mult, r=[xtok, "rstd"], w=[(htok, kt)])

    def prenorm_sq(xa, xtok, TT, sq):
        for kt in range(8):
            act(sq[:, kt, :TT], xa[:, kt, :TT], AF.Square, r=[xtok], w=[("sq", kt)])

    def prenorm_ss(TT, sq, ssps, rstd):
        for kt in range(8):
            mm(ssps[:, :TT], onesD_bf[:], sq[:, kt, :TT], kt == 0, kt == 7, r=[("sq", kt)], w=["ssps"])
        rsqrt_act(rstd[:, :TT], ssps[:, :TT], 1.0, r=["ssps"], w=["rstd"])

    def prenorm_h(xa, xtok, gname, h, TT, rstd, htok="h"):
        for kt in range(8):
            stt("dve", h[:, kt, :TT], xa[:, kt, :TT], vcol(gname, kt), rstd[:, :TT],
                ALU.mult, ALU.mult, r=[xtok, "rstd"], w=[(htok, kt)])

    def post_evac(dt, fps, ftok, TT, fsb, sq, bias=None, pp=""):
        act(fsb[:, dt, :TT], fps, AF.Identity, r=[ftok], w=[("fsb", dt)], bias=bias)
        act(sq[:, dt, :TT], fps, AF.Square, r=[ftok], w=[(pp + "sq", dt)], bias=bias)

    def post_finish(gname, xb, TT, t0, fsb, sq, ssps, rstd, pp="", sstok=None, add_eng="pool"):
        sstok = sstok or (pp + "ssps")
        for kt in range(8):
            mm(ssps[:, :TT], onesD_bf[:], sq[:, kt, :TT], kt == 0, kt == 7, r=[(pp + "sq", kt)], w=[sstok])
        rsqrt_act(rstd[:, :TT], ssps[:, :TT], 1.0, r=[sstok], w=[pp + "rstd"])
        for kt in range(8):
            stt("dve", fsb[:, kt, :TT], fsb[:, kt, :TT], vcol(gname, kt), rstd[:, :TT], ALU.mult, ALU.mult,
                r=[("fsb", kt), pp + "rstd"], w=[("fsb", kt)])
            if add_eng == "pool":
                tt("pool", fsb[:, kt, :TT], fsb[:, kt, :TT], xb[:, kt, :TT], ALU.add, r=[("fsb", kt), "xb"], w=[("fsb", kt)])
        if add_eng != "pool":
            for kt in range(8):
                tt(add_eng, fsb[:, kt, :TT], fsb[:, kt, :TT], xb[:, kt, :TT], ALU.add, r=[("fsb", kt), "xb"], w=[("fsb", kt)])
        S.dma("sp", xr_ap(y, t0, TT), fsb[:, :, :TT], r=[("fsb", kt) for kt in range(8)], w=xr_tok(t0, TT))

    def load_w(dst, src2d, nk, tokname, per=1):
        for k0 in range(0, nk, per):
            k1 = min(nk, k0 + per)
            S.dma("pool", dst[:, k0:k1, :], src2d[k0 * 128:k1 * 128, :].rearrange("(k p) f -> p k f", p=128),
                  w=[(tokname, k) for k in range(k0, k1)])

    def load_wc(dst, src2d, nk, tokname, groups):
        for gi, (c0, c1) in groups:
            S.dma("pool", dst[:, :, c0:c1], src2d[:, c0:c1].rearrange("(k p) f -> p k f", p=128), w=[(tokname, gi)])

    def gtok(tokname, groups, col):
        for gi, (c0, c1) in groups:
            if c0 <= col < c1:
                return (tokname, gi)
        raise ValueError(col)

    def stage_ffn(l, xsrc):
        TT = 256
        st = Stage()
        w1 = st.sb("w1", [128, 8, 2 * DFF], BF16)
        w2 = st.sb("w2", [128, 22, D], BF16)
        xa = st.sb("xa", [128, 8, TT], F32)
        xb = st.sb("xb", [128, 8, TT], F32)
        sq = st.sb("sq", [128, 8, TT], BF16)
        sq2 = st.sb("sq2", [128, 8, TT], BF16)
        h = st.sb("h", [128, 8, TT + 2], BF16)
        hv = h[:, :, 2:2 + TT]
        rstd = st.sb("rstd", [128, TT], F32)
        rstd2 = st.sb("rstd2", [128, TT], F32)
        ga = st.sb("ga", [128, 22, TT], BF16)
        fsb = st.sb("fsb", [128, 8, TT], F32)
        acc = [[st.sb("acc%d%d" % (a, b), [128, TT], F32) for b in range(2)] for a in range(2)]
        sg = [st.sb("sg%d" % b, [128, TT], F32) for b in range(2)]
        ups = [[st.ps("ups%d%d" % (a, b), [128, 512]) for b in range(2)] for a in range(2)]
        fps = [st.ps("fps%d" % b, [128, 512]) for b in range(2)]
        ssps = st.ps("ssps", [128, 512])
        ssps2 = st.ps("ssps2", [128, 512])
        w1g = []
        for (f0, f1) in ((0, 3), (3, 9), (9, 15), (15, 22)):
            for a in range(2):
                w1g.append((len(w1g), (a * DFF + f0 * 128, a * DFF + f1 * 128)))
        load_wc(w1, W["ffn_in_w"][l], 8, "w1", w1g)
        load_w(w2, W["ffn_out_w"][l], 22, "w2", per=11)
        NT = L // TT

        def gate(fc):
            b = fc % 2
            act(sg[b][:], acc[0][b][:], AF.Silu, r=[("acc", 0, b)], w=[("sg", b)])
            tt("pool", ga[:, fc, :], sg[b][:], acc[1][b][:], ALU.mult, r=[("sg", b), ("acc", 1, b)], w=[("ga", fc)])

        mset("pool", h[:, :, 0:2], 0.0, w=["hh"])
        S.dma("sp", xa[:], xr_ap(xsrc, 0, TT), r=xr_tok(0, TT), w=["xa"])
        prenorm(xa, "xa", ("ng", l, 4), hv, TT, sq, ssps, rstd)
        hall = [("h", kt) for kt in range(8)]
        for ti in range(NT):
            t0 = ti * TT
            S.dma("sp", xb[:], xr_ap(xsrc, t0, TT), r=xr_tok(t0, TT), w=["xb"])
            for fc in range(22):
                b = fc % 2
                for a in range(2):
                    col0 = a * DFF + fc * 128
                    cidx = a * 22 + fc
                    up, ac = ups[a][b], acc[a][b]
                    for kt in range(8):
                        mm(up[:, :TT + 2], w1[:, kt, col0:col0 + 128], h[:, kt, :], kt == 0, kt == 7,
                           r=[gtok("w1", w1g, col0), ("h", kt), "hh"], w=[("ups", a, b)])
                    act(ac[:], up[:, 2:2 + TT], AF.Identity, r=[("ups", a, b)], w=[("acc", a, b)],
                        scale=vcol(("fcw", l), 2 * 44 + cidx), bias=vcol(("fcb", l), cidx))
                if fc > 0:
                    gate(fc - 1)
                for k in (1, 0):
                    for a in range(2):
                        cidx = a * 22 + fc
                        up, ac = ups[a][b], acc[a][b]
                        stt("dve", ac[:], up[:, k:k + TT], vcol(("fcw", l), k * 44 + cidx), ac[:], ALU.mult, ALU.add,
                            r=[("ups", a, b), ("acc", a, b)], w=[("acc", a, b)])
            gate(21)
            if ti + 1 < NT:
                cp("pool", h[:, :, 0:2], h[:, :, TT:TT + 2], r=hall, w=["hh"])
                S.dma("sp", xa[:], xr_ap(xsrc, t0 + TT, TT), r=xr_tok(t0 + TT, TT), w=["xa"])
                prenorm(xa, "xa", ("ng", l, 4), hv, TT, sq, ssps, rstd)
            for dt in range(8):
                fp = fps[dt % 2]
                for fc in range(22):
                    mm(fp[:, :TT], w2[:, fc, dt * 128:(dt + 1) * 128], ga[:, fc, :], fc == 0, fc == 21,
                       r=[("w2", fc), ("ga", fc)], w=[("fps", dt % 2)])
                post_evac(dt, fp[:, :TT], ("fps", dt % 2), TT, fsb, sq2, pp="p")
            post_finish(("ng", l, 5), xb, TT, t0, fsb, sq2, ssps2, rstd2, pp="p")
        st.close()

    def stage_xattn(l, xsrc):
        TT = 512
        st = Stage()
        wq = st.sb("wq", [128, 8, D], BF16)
        wkv = st.sb("wkv", [128, 8, 2 * D], BF16)
        wo = st.sb("wo", [128, 8, D], BF16)
        memf = st.sb("memf", [128, 8, NM], F32)
        memn = st.sb("memn", [128, 8, NM], BF16)
        KT = st.sb("KT", [128, 8, NM], BF16)
        V = st.sb("V", [128, 2, D], BF16)
        xa = st.sb("xa", [128, 8, TT], F32)
        xb = st.sb("xb", [128, 8, TT], F32)
        sq = st.sb("sq", [128, 8, TT], BF16)
        h = st.sb("h", [128, 8, TT], BF16)
        rstd = st.sb("rstd", [128, TT], F32)
        qT = st.sb("qT", [128, 8, TT], BF16)
        eT = [st.sb("eT%d" % b, [128, 2, TT], BF16) for b in range(2)]
        rden = [st.sb("rden%d" % b, [128, TT], F32) for b in range(2)]
        oT = st.sb("oT", [128, 8, TT], BF16)
        fsb = st.sb("fsb", [128, 8, TT], F32)
        ssps = st.ps("ssps", [128, 512])
        qps = [st.ps("qps%d" % b, [128, 512]) for b in range(2)]
        sps = [st.ps("sps%d" % b, [128, 512]) for b in range(2)]
        dps = st.ps("dps", [128, 512])
        ops_ = [st.ps("ops%d" % b, [128, 512]) for b in range(2)]
        g512 = lambda n: [(i, (i * 512, (i + 1) * 512)) for i in range(n)]
        wkvg, wqg, wog = g512(4), g512(2), g512(2)
        load_wc(wkv, W["xa_kv_w"][l], 8, "wkv", wkvg)
        load_wc(wq, W["xa_q_w"][l], 8, "wq", wqg)
        load_wc(wo, W["xa_o_w"][l], 8, "wo", wog)
        S.dma("sp", memf[:], memT.rearrange("(kt p) m -> p kt m", p=128), w=["memf"])
        prenorm(memf, "memf", ("mg", l), memn, NM, sq, ssps, rstd)
        for dt in range(8):
            qp = qps[dt % 2]
            for kt in range(8):
                mm(qp[:, :NM], wkv[:, kt, dt * 128:(dt + 1) * 128], memn[:, kt, :], kt == 0, kt == 7,
                   r=[gtok("wkv", wkvg, dt * 128), ("h", kt)], w=[("qps", dt % 2)])
            act(KT[:, dt, :], qp[:, :NM], AF.Copy, r=[("qps", dt % 2)], w=["KT"])
        for mt in range(2):
            for nb in range(2):
                qp = qps[nb]
                for kt in range(8):
                    mm(qp[:], memn[:, kt, mt * 128:(mt + 1) * 128], wkv[:, kt, D + nb * 512:D + (nb + 1) * 512],
                       kt == 0, kt == 7, r=[gtok("wkv", wkvg, D + nb * 512), ("h", kt)], w=[("qps", nb)])
                act(V[:, mt, nb * 512:(nb + 1) * 512], qp[:], AF.Copy, r=[("qps", nb)], w=["V"])
        sq2 = st.sb("sq2", [128, 8, TT], BF16)
        rstd2 = st.sb("rstd2", [128, TT], F32)
        NT = L // TT

        def pre(ti):
            S.dma("sp", xa[:], xr_ap(xsrc, ti * TT, TT), r=xr_tok(ti * TT, TT), w=["xa"])
            prenorm(xa, "xa", ("ng", l, 2), h, TT, sq, ssps, rstd)

        pre(0)
        S.dma("sp", xb[:], xr_ap(xsrc, 0, TT), r=xr_tok(0, TT), w=["xb"])
        for ti in range(NT):
            t0 = ti * TT
            for dt in range(8):
                qp = qps[dt % 2]
                for kt in range(8):
                    mm(qp[:], wq[:, kt, dt * 128:(dt + 1) * 128], h[:, kt, :], kt == 0, kt == 7,
                       r=[gtok("wq", wqg, dt * 128), ("h", kt)], w=[("qps", dt % 2)])
                act(qT[:, dt, :], qp[:], AF.Identity, r=[("qps", dt % 2)], w=[("qT", dt)], scale=1.0 / 16.0)
            for hd in range(4):
                b = hd % 2
                for mt in range(2):
                    for j in range(2):
                        mm(sps[mt][:], KT[:, 2 * hd + j, mt * 128:(mt + 1) * 128], qT[:, 2 * hd + j, :], j == 0, j == 1,
                           r=["KT", ("qT", 2 * hd + j)], w=[("sps", mt)])
                    act(eT[b][:, mt, :], sps[mt][:], AF.Exp, r=[("sps", mt)], w=[("eT", b, mt)])
                for mt in range(2):
                    mm(dps[:], ones_bf[:], eT[b][:, mt, :], mt == 0, mt == 1, r=[("eT", b, mt)], w=["dps"])
                act(rden[b][:], dps[:], AF.Ln, r=["dps"], w=[("rden", b)])
                act(rden[b][:], rden[b][:], AF.Exp, r=[("rden", b)], w=[("rden", b)], scale=-1.0)
                for j in range(2):
                    op_ = ops_[j]
                    for mt in range(2):
                        mm(op_[:], V[:, mt, (2 * hd + j) * 128:(2 * hd + j + 1) * 128], eT[b][:, mt, :], mt == 0, mt == 1,
                           r=["V", ("eT", b, mt)], w=[("ops", j)])
                    tt("dve", oT[:, 2 * hd + j, :], op_[:], rden[b][:], ALU.mult, r=[("ops", j), ("rden", b)],
                       w=[("oT", 2 * hd + j)])
            if ti + 1 < NT:
                pre(ti + 1)
            for dt in range(8):
                qp = qps[dt % 2]
                for kt in range(8):
                    mm(qp[:], wo[:, kt, dt * 128:(dt + 1) * 128], oT[:, kt, :], kt == 0, kt == 7,
                       r=[gtok("wo", wog, dt * 128), ("oT", kt)], w=[("qps", dt % 2)])
                post_evac(dt, qp[:], ("qps", dt % 2), TT, fsb, sq2, pp="p")
            post_finish(("ng", l, 3), xb, TT, t0, fsb, sq2, dps, rstd2, pp="p", sstok="dps")
            if ti + 1 < NT:
                S.dma("sp", xb[:], xr_ap(xsrc, t0 + TT, TT), r=xr_tok(t0 + TT, TT), w=["xb"])
        st.close()

    def stage_conf(l, xsrc):
        j = l // 2
        TT = 256
        HL = CFK - 1
        NT = L // TT
        st = Stage()
        pw1 = st.sb("pw1", [128, 8, 2 * D], BF16)
        pw2 = st.sb("pw2", [128, 8, D], BF16)
        dg = st.sb("dg", [128, CFK * 8, 128], BF16)
        xa = st.sb("xa", [128, 8, TT], F32)
        xb = st.sb("xb", [128, 8, TT], F32)
        sq = st.sb("sq", [128, 8, TT], BF16)
        sqc = st.sb("sqc", [128, 8, TT], BF16)
        sq2 = st.sb("sq2", [128, 8, TT], BF16)
        h = st.sb("h", [128, 8, TT], BF16)
        rstd = st.sb("rstd", [128, TT], F32)
        rstdc = st.sb("rstdc", [128, TT], F32)
        rstd2 = st.sb("rstd2", [128, TT], F32)
        sig = [st.sb("sig%d" % b, [128, TT], F32) for b in range(2)]
        glu = [st.sb("glu%d" % b, [128, 8, HL + TT], BF16) for b in range(2)]
        cs = st.sb("cs", [128, 8, TT], F32)
        cb = st.sb("cb", [128, 8, TT], BF16)
        mean = st.sb("mean", [128, TT], F32)
        var = st.sb("var", [128, TT], F32)
        t1 = [st.sb("t1%d" % b, [128, TT], F32) for b in range(2)]
        sn = st.sb("sn", [128, 8, TT], BF16)
        fsb = st.sb("fsb", [128, 8, TT], F32)
        ssps = st.ps("ssps", [128, 512])
        aps = [st.ps("aps%d" % b, [128, 512]) for b in range(2)]
        gps = [st.ps("gps%d" % b, [128, 512]) for b in range(2)]
        cps = [st.ps("cps%d" % b, [128, 512]) for b in range(2)]
        mps = st.ps("mps", [128, 512])
        pw1g = [(0, (0, 512)), (1, (D, D + 512)), (2, (512, D)), (3, (D + 512, 2 * D))]
        load_wc(pw1, W["cf_pw1_w"][j], 8, "pw1", pw1g)
        load_w(pw2, W["cf_pw2_w"][j], 8, "pw2", per=4)
        dwo = voff[("dww", j)]
        for c in range(8):
            tt("dve", dg[:, c * CFK:(c + 1) * CFK, :],
               ident_f.unsqueeze(1).to_broadcast([128, CFK, 128]),
               vecs[:, dwo + c:dwo + c + CFK * 8:8].unsqueeze(2).to_broadcast([128, CFK, 128]), ALU.mult,
               r=["cst"], w=[("dg", c)])
        for c in range(8):
            mset("pool", glu[1][:, c, TT:TT + HL], 0.0, w=[("glu", 1, c)])

        def pre(ti):
            t0 = ti * TT
            S.dma("sp", xa[:], xr_ap(xsrc, t0, TT), r=xr_tok(t0, TT), w=["xa"])
            prenorm(xa, "xa", ("ng", l, 0), h, TT, sq, ssps, rstd)

        def pw1glu(ti):
            gb = ti % 2
            G_, Gp = glu[gb], glu[1 - gb]
            for c in range(8):
                b = c % 2
                for kt in range(8):
                    mm(aps[b][:, :TT], pw1[:, kt, c * 128:(c + 1) * 128], h[:, kt, :], kt == 0, kt == 7,
                       r=[gtok("pw1", pw1g, c * 128), ("h", kt)], w=[("aps", b)])
                for kt in range(8):
                    mm(gps[b][:, :TT], pw1[:, kt, D + c * 128:D + (c + 1) * 128], h[:, kt, :], kt == 0, kt == 7,
                       r=[gtok("pw1", pw1g, D + c * 128), ("h", kt)], w=[("gps", b)])
                act(sig[b][:], gps[b][:, :TT], AF.Sigmoid, r=[("gps", b)], w=[("sig", b)], bias=vcol(("pw1b", j), 8 + c))
                cp("pool", G_[:, c, 0:HL], Gp[:, c, TT:TT + HL], r=[("glu", 1 - gb, c)], w=[("gluh", gb, c)])
                stt("dve", G_[:, c, HL:HL + TT], aps[b][:, :TT], vcol(("pw1b", j), c), sig[b][:], ALU.add, ALU.mult,
                    r=[("aps", b), ("sig", b)], w=[("glu", gb, c)])

        def conv(ti):
            gb = ti % 2
            G_ = glu[gb]
            for c in range(8):
                b = c % 2
                for k in range(CFK):
                    mm(cps[b][:, :TT], dg[:, c * CFK + k, :], G_[:, c, k:k + TT], k == 0, k == CFK - 1,
                       r=[("dg", c), ("glu", gb, c), ("gluh", gb, c)], w=[("cps", b)])
                act(cs[:, c, :], cps[b][:, :TT], AF.Identity, r=[("cps", b)], w=[("cs", c)], bias=vcol(("dwb", j), c))
                act(sqc[:, c, :], cps[b][:, :TT], AF.Square, r=[("cps", b)], w=[("sqc", c)], bias=vcol(("dwb", j), c))
                act(cb[:, c, :], cps[b][:, :TT], AF.Identity, r=[("cps", b)], w=[("cb", c)], bias=vcol(("dwb", j), c))

        def lnorm(ti):
            for c in range(8):
                mm(mps[:, 0:TT], onesD_bf[:], cb[:, c, :], c == 0, c == 7, r=[("cb", c)], w=["mps"])
            for c in range(8):
                mm(mps[:, TT:2 * TT], onesD_bf[:], sqc[:, c, :], c == 0, c == 7, r=[("sqc", c)], w=["mps"])
            cp("dve", mean[:], mps[:, 0:TT], r=["mps"], w=["mean"])
            tt("dve", var[:], mean[:], mean[:], ALU.mult, r=["mean"], w=["var"])
            tt("dve", var[:], mps[:, TT:2 * TT], var[:], ALU.subtract, r=["mps", "var"], w=["var"])
            rsqrt_act(rstdc[:], var[:], 1.0, r=["var"], w=["rstdc"])
            for c in range(8):
                b = c % 2
                tt("pool", t1[b][:], cs[:, c, :], mean[:], ALU.subtract, r=[("cs", c), "mean"], w=[("t1", b)])
                stt("dve", t1[b][:], t1[b][:], vcol(("lng", j), c), rstdc[:], ALU.mult, ALU.mult,
                    r=[("t1", b), "rstdc"], w=[("t1", b)])
                act(sn[:, c, :], t1[b][:], AF.Silu, r=[("t1", b)], w=[("sn", c)], bias=vcol(("lnb", j), c))

        def pw2post(ti):
            t0 = ti * TT
            for dt in range(8):
                b = dt % 2
                for c in range(8):
                    mm(cps[b][:, :TT], pw2[:, c, dt * 128:(dt + 1) * 128], sn[:, c, :], c == 0, c == 7,
                       r=[("pw2", c), ("sn", c)], w=[("cps", b)])
                post_evac(dt, cps[b][:, :TT], ("cps", b), TT, fsb, sq2, bias=vcol(("pw2b", j), dt), pp="p")
            post_finish(("ng", l, 1), xb, TT, t0, fsb, sq2, ssps, rstd2, pp="p", sstok="ssps")

        pre(0)
        S.dma("sp", xb[:], xr_ap(xsrc, 0, TT), r=xr_tok(0, TT), w=["xb"])
        pw1glu(0)
        for ti in range(NT):
            conv(ti)
            if ti + 1 < NT:
                pre(ti + 1)
                pw1glu(ti + 1)
            lnorm(ti)
            pw2post(ti)
            if ti + 1 < NT:
                S.dma("sp", xb[:], xr_ap(xsrc, (ti + 1) * TT, TT), r=xr_tok((ti + 1) * TT, TT), w=["xb"])
        st.close()

    def stage_ssd_a(l, xsrc, wout=None):
        j = l // 2
        TT = 512
        st = Stage()
        win = st.sb("win", [128, 8, INP], BF16)
        xa = st.sb("xa", [128, 8, TT], F32)
        sq = st.sb("sq", [128, 8, TT], BF16)
        hh_ = [st.sb("h%d" % b, [128, 8, TT], BF16) for b in range(2)]
        rstd = st.sb("rstd", [128, TT], F32)
        zs = [st.sb("zs%d" % b, [128, DI], BF16) for b in range(2)]
        ub = [st.sb("ub%d" % b, [128, TT + 3], F32) for b in range(2)]
        acc = [st.sb("acc%d" % b, [128, TT], F32) for b in range(2)]
        xbs = [st.sb("xbs%d" % b, [128, TT], BF16) for b in range(4)]
        halo = st.sb("halo", [128, 24, 3], F32)
        bv = st.sb("bv", [128, 96], F32)
        Ab = st.sb("Ab", [128, 32], F32)
        dtt = [st.sb("dtt%d" % b, [128, 32], F32) for b in range(2)]
        dta = [st.sb("dta%d" % b, [128, 64], F32) for b in range(2)]
        ssps = st.ps("ssps", [128, 512])
        zps = [st.ps("zps%d" % b, [128, 512]) for b in range(2)]
        ups = [st.ps("ups%d" % b, [128, 512]) for b in range(2)]
        dps = st.ps("dps", [128, 512])
        wing = [(i, (i * 512, (i + 1) * 512)) for i in range(4)] + [(4, (DI + CONVD, INP))] + \
               [(5 + i, (DI + i * 512, DI + (i + 1) * 512)) for i in range(6)]
        load_wc(win, W["ssm_in_w"][j], 8, "win", wing)
        if wout is not None:
            load_w(wout, W["ssm_out_w"][j], 16, "wout", per=4)
        S.dma("sp", bv[:], bvecs_d[:, j * 96:(j + 1) * 96], w=["bv"])
        act(Ab[:], bv[:, 32:64], AF.Exp, r=["bv"], w=["Ab"])
        ts("dve", Ab[:], Ab[:], -1.0, None, ALU.mult, None, r=["Ab"], w=["Ab"])
        mset("pool", halo[:], 0.0, w=[("halo", c) for c in range(24)])
        NT = L // TT

        def pre(ti):
            S.dma("sp", xa[:], xr_ap(xsrc, ti * TT, TT), r=xr_tok(ti * TT, TT), w=["xa"])
            prenorm(xa, "xa", ("ng", l, 0), hh_[ti % 2], TT, sq, ssps, rstd, htok="h%d" % (ti % 2))

        pre(0)
        for ti in range(NT):
            t0 = ti * TT
            h = hh_[ti % 2]
            ht = "h%d" % (ti % 2)
            for ck in range(4):
                cg = ti * 4 + ck
                zb = zs[ck % 2]
                for nb in range(4):
                    zp = zps[nb % 2]
                    for kt in range(8):
                        mm(zp[:], h[:, kt, ck * 128:(ck + 1) * 128], win[:, kt, nb * 512:(nb + 1) * 512], kt == 0, kt == 7,
                           r=[gtok("win", wing, nb * 512), (ht, kt)], w=[("zps", nb % 2)])
                    act(zb[:, nb * 512:(nb + 1) * 512], zp[:], AF.Silu, r=[("zps", nb % 2)], w=[("zs", ck % 2)])
                S.dma("sp", ZS[cg * 128:(cg + 1) * 128, :], zb[:], r=[("zs", ck % 2)], w=[("zsd", cg)])
                for kt in range(8):
                    mm(dps[:, 0:32], h[:, kt, ck * 128:(ck + 1) * 128], win[:, kt, DI + CONVD:INP], kt == 0, kt == 7,
                       r=[("win", 4), (ht, kt)], w=["dps"])
                db = ck % 2
                tt("dve", dtt[db][:], dps[:, 0:32], bv[:, 0:32], ALU.add, r=["dps", "bv"], w=[("dtt", db)])
                act(dtt[db][:], dtt[db][:], AF.Exp, r=[("dtt", db)], w=[("dtt", db)])
                act(dta[db][:, 0:32], dtt[db][:], AF.Ln, r=[("dtt", db)], w=[("dta", db)], bias=onec[:])
                tt("dve", dta[db][:, 32:64], dta[db][:, 0:32], Ab[:], ALU.mult, r=[("dta", db), "Ab"], w=[("dta", db)])
                S.dma("sp", DTA[cg * 128:(cg + 1) * 128, :], dta[db][:], r=[("dta", db)], w=[("dtad", cg)])
            def finish_c(c):
                b = c % 2
                xs = xbs[c % 4]
                act(xs[:], acc[b][:], AF.Silu, r=[("acc", b)], w=[("xbs", c % 4)])
                S.dma("sp", XBC[c * 128:(c + 1) * 128, t0:t0 + TT], xs[:], r=[("xbs", c % 4)], w=[("xbcd", ti, c)])

            for c in range(24):
                b = c % 2
                col0 = DI + c * 128
                for kt in range(8):
                    mm(ups[b][:], win[:, kt, col0:col0 + 128], h[:, kt, :], kt == 0, kt == 7,
                       r=[gtok("win", wing, col0), (ht, kt)], w=[("ups", b)])
                u, ac = ub[b], acc[b]
                cp("pool", u[:, 0:3], halo[:, c, :], r=[("halo", c)], w=[("ubh", b)])
                act(u[:, 3:3 + TT], ups[b][:], AF.Copy, r=[("ups", b)], w=[("ub", b)])
                act(ac[:], ups[b][:], AF.Identity, r=[("ups", b)], w=[("acc", b)],
                    scale=vcol(("scw", j), 3 * 24 + c), bias=vcol(("scb", j), c))
                cp("pool", halo[:, c, :], u[:, TT:TT + 3], r=[("ub", b)], w=[("halo", c)])
                if c > 0:
                    finish_c(c - 1)
                if ti + 1 < NT:
                    nb_ = (ti + 1) % 2
                    if c == 4:
                        S.dma("sp", xa[:], xr_ap(xsrc, (ti + 1) * TT, TT), r=xr_tok((ti + 1) * TT, TT), w=["xa"])
                        prenorm_sq(xa, "xa", TT, sq)
                    elif c == 12:
                        prenorm_ss(TT, sq, ssps, rstd)
                    elif c == 18:
                        prenorm_h(xa, "xa", ("ng", l, 0), hh_[nb_], TT, rstd, htok="h%d" % nb_)
                for k in (2, 1, 0):
                    stt("dve", ac[:], u[:, k:k + TT], vcol(("scw", j), k * 24 + c), ac[:], ALU.mult, ALU.add,
                        r=[("ub", b), ("ubh", b), ("acc", b)], w=[("acc", b)])
            finish_c(23)
            if ti == NT - 1 and wout is not None:
                for c in range(16):
                    act(wout[:, c, :], wout[:, c, :], AF.Identity, r=[("wout", c)], w=[("wout", c)], scale=vcol(("sng", j), c))
        st.close()

    def stage_ssd_b(l, xsrc, wout_pre=None):
        j = l // 2
        TT = 256
        CPT = TT // 128
        NT = L // TT
        NCH = L // 128
        st = Stage()
        wout = wout_pre if wout_pre is not None else st.sb("wout", [128, 16, D], BF16)
        xbc = [st.sb("xbc%d" % b, [128, 24, TT], BF16) for b in range(2)]
        zsc = [st.sb("zsc%d" % b, [128, DI], BF16) for b in range(2)]
        dta = [st.sb("dtac%d" % b, [128, 64], F32) for b in range(2)]
        bv = st.sb("bv", [128, 96], F32)
        xtm = [st.sb("xtm%d" % b, [128, DI], BF16) for b in range(2)]
        xdt = [st.sb("xdt%d" % b, [128, DI], BF16) for b in range(2)]
        xdd = [st.sb("xdd%d" % b, [128, DI], BF16) for b in range(2)]
        xD = [st.sb("xD%d" % b, [128, DI], BF16) for b in range(2)]
        Btm = [st.sb("Btm%d" % b, [128, 512], BF16) for b in range(2)]
        cs_s = [st.sb("cs_s%d" % b, [128, 32], F32) for b in range(2)]
        ecs = [st.sb("ecs%d" % b, [128, 32], F32) for b in range(2)]
        dte = [st.sb("dte%d" % b, [128, 32], F32) for b in range(2)]
        dec = [st.sb("dec%d" % b, [128, 32], F32) for b in range(2)]
        CBm = [st.sb("CBm%d" % b, [128, 4, 128], BF16) for b in range(2)]
        R = [st.sb("R%d" % b, [128, 8, 128], F32) for b in range(2)]
        es_ = [[st.sb("es%d%d" % (a, b), [128, 512], BF16) for b in range(2)] for a in range(2)]
        M = [[st.sb("M%d%d" % (a, b), [128, 4, 128], BF16) for b in range(2)] for a in range(2)]
        stf = st.sb("stf", [128, 4, 512], F32)
        stb = st.sb("stb", [128, 4, 512], BF16)
        t1 = [st.sb("t1%d" % b, [128, 512], F32) for b in range(2)]
        ytm = st.sb("ytm", [128, DI], F32)
        ss1 = st.sb("ss1", [128, 1], F32)
        rs1 = st.sb("rs1", [128, 1], F32)
        yn = st.sb("yn", [128, DI], BF16)
        ynT = st.sb("ynT", [128, 16, TT], BF16)
        xb = st.sb("xb", [128, 8, TT], F32)
        fsb = st.sb("fsb", [128, 8, TT], F32)
        sq = st.sb("sq", [128, 8, TT], BF16)
        rstd = st.sb("rstd", [128, TT], F32)
        tp = st.ps("tp", [128, DI], BF16)
        smp = st.ps("smp", [128, 512])
        cbp = st.ps("cbp", [128, 512])
        sgp = [st.ps("sgp%d" % b, [128, 512]) for b in range(2)]
        ydp = st.ps("ydp", [128, 512])
        yop = st.ps("yop", [128, 512])
        if wout_pre is None:
            load_w(wout, W["ssm_out_w"][j], 16, "wout", per=4)
            for c in range(16):
                ts("dve" if c % 2 == 0 else "pool", wout[:, c, :], wout[:, c, :], vcol(("sng", j), c), None, ALU.mult, None,
                   r=[("wout", c)], w=[("wout", c)])
        S.dma("sp", bv[:], bvecs_d[:, j * 96:(j + 1) * 96], w=["bv"])
        mset("dve", stf[:], 0.0, w=[("stf", g) for g in range(NG)])
        mset("pool", stb[:], 0.0, w=[("stb", g) for g in range(NG)])
        hq = "p (h q) -> p h q"

        def load_tile(ti):
            t0 = ti * TT
            for c0 in range(0, 24, 8):
                S.dma("sp", xbc[ti % 2][:, c0:c0 + 8, :],
                      XBC[c0 * 128:(c0 + 8) * 128, t0:t0 + TT].rearrange("(c p) t -> p c t", p=128),
                      r=[("xbcd", (t0 // 512), c) for c in range(c0, c0 + 8)],
                      w=[("xbc", ti % 2, c) for c in range(c0, c0 + 8)])

        def front(c):
            cb, ti, ck = c % 2, c // CPT, c % CPT
            X = xbc[ti % 2]
            sl = slice(ck * 128, (ck + 1) * 128)
            S.dma("sp", zsc[cb][:], ZS[c * 128:(c + 1) * 128, :], r=[("zsd", c)], w=[("zsc", cb)])
            S.dma("sp", dta[cb][:], DTA[c * 128:(c + 1) * 128, :], r=[("dtad", c)], w=[("dtac", cb)])
            dtv = dta[cb][:, 0:32]
            av = dta[cb][:, 32:64]
            for g in range(NG):
                tr(tp[:, g * 128:(g + 1) * 128], X[:, 16 + g, sl], r=[("xbc", ti % 2, 16 + g)], w=["tp"])
            cp("dve", Btm[cb][:], tp[:, 0:512], r=["tp"], w=[("Btm", cb)])
            for ch in range(16):
                tr(tp[:, ch * 128:(ch + 1) * 128], X[:, ch, sl], r=[("xbc", ti % 2, ch)], w=["tp"])
            cp("dve", xtm[cb][:], tp[:], r=["tp"], w=[("xtm", cb)])
            tt("dve", xdt[cb][:].rearrange(hq, q=64), tp[:].rearrange(hq, q=64),
               dtv.unsqueeze(2).to_broadcast([128, 32, 64]), ALU.mult, r=["tp", ("dtac", cb)], w=[("xdt", cb)])
            tt("pool", xD[cb][:].rearrange(hq, q=64), xtm[cb][:].rearrange(hq, q=64),
               bv[:, 64:96].unsqueeze(2).to_broadcast([128, 32, 64]), ALU.mult, r=[("xtm", cb), "bv"], w=[("xD", cb)])
            mm(smp[:, 0:32], U_f, av, True, True, r=["cst", ("dtac", cb)], w=["smp"])
            mm(smp[:, 32:64], ones_f, av, True, True, r=["cst", ("dtac", cb)], w=["smp"])
            cp("dve", cs_s[cb][:], smp[:, 0:32], r=["smp"], w=[("cs_s", cb)])
            act(ecs[cb][:], smp[:, 0:32], AF.Exp, r=["smp"], w=[("ecs", cb)])
            tt("dve", dte[cb][:], smp[:, 32:64], cs_s[cb][:], ALU.subtract, r=["smp", ("cs_s", cb)], w=[("dte", cb)])
            act(dte[cb][:], dte[cb][:], AF.Exp, r=[("dte", cb)], w=[("dte", cb)])
            act(dec[cb][:], smp[:, 32:64], AF.Exp, r=["smp"], w=[("dec", cb)])
            tt("pool", xdd[cb][:].rearrange(hq, q=64), xdt[cb][:].rearrange(hq, q=64),
               dte[cb][:].unsqueeze(2).to_broadcast([128, 32, 64]), ALU.mult, r=[("xdt", cb), ("dte", cb)], w=[("xdd", cb)])
            for g in range(NG):
                mm(cbp[:, g * 128:(g + 1) * 128], X[:, 16 + g, sl], X[:, 20 + g, sl], True, True,
                   r=[("xbc", ti % 2, 16 + g), ("xbc", ti % 2, 20 + g)], w=["ssps"])
            tt("dve", CBm[cb][:], cbp[:].rearrange("p (g i) -> p g i", g=4), U_f.unsqueeze(1).to_broadcast([128, 4, 128]),
               ALU.mult, r=["ssps", "cst"], w=[("CBm", cb)])

        def build_R_c(c, g):
            cb = c % 2
            av = dta[cb][:, 32:64]
            tt("pool", R[g % 2][:], U_f.unsqueeze(1).to_broadcast([128, 8, 128]),
               av[:, g * 8:(g + 1) * 8].unsqueeze(2).to_broadcast([128, 8, 128]), ALU.mult,
               r=["cst", ("dtac", cb)], w=[("R", g % 2)])

        def decay_c(c, g, hh):
            cb = c % 2
            gb = g % 2
            mm(sgp[hh][:], Lm_f, R[gb][:, hh * 4:(hh + 1) * 4, :].rearrange("p h i -> p (h i)"), True, True,
               r=["cst", ("R", gb)], w=[("sgp", hh)])
            act(es_[gb][hh][:], sgp[hh][:], AF.Exp, r=[("sgp", hh)], w=[("es", gb, hh)])
            tt("dve", M[gb][hh][:], es_[gb][hh][:].rearrange("p (h i) -> p h i", h=4),
               CBm[cb][:, g, :].unsqueeze(1).to_broadcast([128, 4, 128]), ALU.mult,
               r=[("es", gb, hh), ("CBm", cb)], w=[("M", gb, hh)])

        def decay0(c):
            build_R_c(c, 0)
            decay_c(c, 0, 0)
            decay_c(c, 0, 1)

        def middle(c):
            cb, ti, ck = c % 2, c // CPT, c % CPT
            X = xbc[ti % 2]
            sl = slice(ck * 128, (ck + 1) * 128)
            av = dta[cb][:, 32:64]

            def build_R(g):
                build_R_c(c, g)

            def decay(g, hh):
                decay_c(c, g, hh)

            decay0(c)
            for g in range(NG):
                gb = g % 2
                ydp_g, ydtok = (ydp, "ydp") if gb == 0 else (smp, "smp")
                if g + 1 < NG:
                    build_R(g + 1)
                for hh in range(2):
                    for hl in range(4):
                        hg = g * 8 + hh * 4 + hl
                        o0 = (hh * 4 + hl) * 64
                        mm(ydp_g[:, o0:o0 + 64], M[gb][hh][:, hl, :], xdt[cb][:, hg * 64:(hg + 1) * 64], True, False,
                           r=[("M", gb, hh), ("xdt", cb)], w=[ydtok])
                        mm(ydp_g[:, o0:o0 + 64], ident_bf[:], xD[cb][:, hg * 64:(hg + 1) * 64], False, True,
                           r=["cbf", ("xD", cb)], w=[ydtok])
                    if g + 1 < NG:
                        decay(g + 1, hh)
                mm(yop[:], X[:, 20 + g, sl], stb[:, g, :], True, True, r=[("xbc", ti % 2, 20 + g), ("stb", g)], w=["yop"])
                mm(cbp[:], Btm[cb][:, g * 128:(g + 1) * 128], xdd[cb][:, g * 512:(g + 1) * 512], True, True,
                   r=[("Btm", cb), ("xdd", cb)], w=["ssps"])
                tt("dve", t1[gb][:].rearrange(hq, q=64), yop[:].rearrange(hq, q=64),
                   ecs[cb][:, g * 8:(g + 1) * 8].unsqueeze(2).to_broadcast([128, 8, 64]), ALU.mult,
                   r=["yop", ("ecs", cb)], w=[("t1", gb)])
                tt("dve", stf[:, g, :].rearrange(hq, q=64), stf[:, g, :].rearrange(hq, q=64),
                   dec[cb][:, g * 8:(g + 1) * 8].unsqueeze(2).to_broadcast([128, 8, 64]), ALU.mult,
                   r=[("stf", g), ("dec", cb)], w=[("stf", g)])
                tt("dve", ytm[:, g * 512:(g + 1) * 512], t1[gb][:], ydp_g[:], ALU.add, r=[("t1", gb), ydtok], w=[("ytm", g)])
                tt("dve", stf[:, g, :], stf[:, g, :], cbp[:], ALU.add, r=[("stf", g), "ssps"], w=[("stf", g)])
                act(stb[:, g, :], stf[:, g, :], AF.Copy, r=[("stf", g)], w=[("stb", g)])

        def back1(c):
            cb = c % 2
            ytoks = [("ytm", g) for g in range(NG)]
            tt("dve", ytm[:], ytm[:], zsc[cb][:], ALU.mult, r=ytoks + [("zsc", cb)], w=ytoks)
            mset("dve", ss1[:], 0.0, w=["ss1"])
            act(yn[:], ytm[:], AF.Square, r=ytoks + ["ss1"], w=["yn", "ss1"], accum=ss1[:])
            rsqrt_act(rs1[:], ss1[:], 1.0 / DI, r=["ss1"], w=["rs1"])
            act(yn[:], ytm[:], AF.Identity, r=ytoks + ["rs1"], w=["yn"], scale=rs1[:])

        def back2(c):
            ti, ck = c // CPT, c % CPT
            sl = slice(ck * 128, (ck + 1) * 128)
            for ch in range(16):
                tr(tp[:, ch * 128:(ch + 1) * 128], yn[:, ch * 128:(ch + 1) * 128], r=["yn"], w=["tp"])
            for ch in range(16):
                cp("dve", ynT[:, ch, sl], tp[:, ch * 128:(ch + 1) * 128], r=["tp"], w=[("ynT", ch)])
            if ck == CPT - 1:
                t0 = ti * TT
                for dt in range(8):
                    fp = sgp[dt % 2]
                    for ch in range(16):
                        mm(fp[:, :TT], wout[:, ch, dt * 128:(dt + 1) * 128], ynT[:, ch, :], ch == 0, ch == 15,
                           r=[("wout", ch), ("ynT", ch)], w=[("sgp", dt % 2)])
                    post_evac(dt, fp[:, :TT], ("sgp", dt % 2), TT, fsb, sq)
                pending.append(ti)

        pending = []

        def flush_post():
            while pending:
                ti = pending.pop(0)
                t0 = ti * TT
                post_finish(("ng", l, 1), xb, TT, t0, fsb, sq, cbp, rstd, add_eng="dve")
                if ti + 1 < NT:
                    S.dma("sp", xb[:], xr_ap(xsrc, t0 + TT, TT), r=xr_tok(t0 + TT, TT), w=["xb"])

        load_tile(0)
        S.dma("sp", xb[:], xr_ap(xsrc, 0, TT), r=xr_tok(0, TT), w=["xb"])
        front(0)
        for c in range(NCH):
            if c % CPT == 0 and c // CPT + 1 < NT:
                load_tile(c // CPT + 1)
            middle(c)
            back1(c)
            if c + 1 < NCH:
                front(c + 1)
            flush_post()
            back2(c)
        flush_post()
        st.close()

    first = True
    for (l, s) in stages:
        xsrc = xT if first else y
        if s == "mix":
            if l % 2 == 0:
                with nc.sbuf_tensor("wout_l%d" % l, [128, 16, D], BF16) as wout_t:
                    stage_ssd_a(l, xsrc, wout_t)
                    stage_ssd_b(l, xsrc, wout_t)
            else:
                stage_conf(l, xsrc)
        elif s == "ssda":
            stage_ssd_a(l, xsrc)
        elif s == "ssdb":
            stage_ssd_b(l, xsrc)
        elif s == "xa":
            stage_xattn(l, xsrc)
        else:
            stage_ffn(l, xsrc)
        first = False
    S.finish()
    return nc


_PROGRAM_CACHE = {}


def make_in_maps(inp, L=L):
    vecs, bv, consts = build_vecs(inp)
    x = np.asarray(inp["x"], np.float32)[:, :L]
    mem = np.asarray(inp["mem"], np.float32)
    shared = {"vecs": vecs, "bvecs": bv, "consts": consts}
    for k in WEIGHT_SHAPES:
        shared[k] = np.ascontiguousarray(np.asarray(inp[k], np.float32))
    maps = []
    for b in range(x.shape[0]):
        m = dict(shared)
        m["xT"] = np.ascontiguousarray(x[b].T)
        m["memT"] = np.ascontiguousarray(mem[b].T)
        maps.append(m)
    return maps


def kernel(**inputs):
    maps = make_in_maps(inputs)
    nc = build_program()
    res = run_bass_kernel_spmd(nc, maps, core_ids=list(range(8)))
    out = np.stack([np.asarray(r["y"], np.float32).T for r in res.results], axis=0)
    return np.ascontiguousarray(out)
```

```python
from contextlib import ExitStack
import numpy as np
import concourse.bass as bass
import concourse.mybir as mybir
from concourse.bass_utils import run_bass_kernel_spmd

F32 = mybir.dt.float32
BF16 = mybir.dt.bfloat16
AF = mybir.ActivationFunctionType
ALU = mybir.AluOpType

D = 1024
L = 4096
NL = 4
DI = 2048
NH = 32
NG = 4
CONVD = 3072
INP = 5152
DFF = 2816
NM = 256
CFK = 31
EPS = 1e-6
ENGS = ("pe", "act", "dve", "pool", "sp")
SYNC_SMALL_ONLY = False


class Sched:
    def __init__(self, nc, same_engine_sync=True, n_dma_sems=28, n_pool_dma_sems=12):
        self.nc = nc
        self.same = same_engine_sync
        self.prog = {e: [] for e in ENGS}
        self.cnt = {e: 0 for e in ENGS}
        self.known = {e: {} for e in ENGS}
        self.state = {}
        self.sems = {}
        self._ctx = []
        for e in ("pe", "act", "dve", "pool"):
            self.sems[e] = self._sem("s_" + e)
        self.dma_pool = {"sp": [], "pool": []}
        for i in range(n_dma_sems):
            k = ("dsp", i); self.sems[k] = self._sem("d_sp%d" % i); self.dma_pool["sp"].append(k)
        for i in range(n_pool_dma_sems):
            k = ("dpl", i); self.sems[k] = self._sem("d_pl%d" % i); self.dma_pool["pool"].append(k)
        self.dma_cnt = {k: 0 for q in self.dma_pool.values() for k in q}
        self.dma_rr = {"sp": 0, "pool": 0}

    def _sem(self, name):
        cm = self.nc.semaphore(name)
        h = cm.__enter__()
        self._ctx.append(cm)
        return h

    def _need(self, eng, needs, key, val, small=True):
        if key == eng and (eng == "pe" or not self.same or (SYNC_SMALL_ONLY and not small)):
            return
        if self.known[eng].get(key, 0) >= val:
            return
        if needs.get(key, 0) < val:
            needs[key] = val

    def _deps(self, eng, r, w):
        needs = {}
        for t in r:
            st = self.state.get(t)
            if st and st[0]:
                self._need(eng, needs, st[0][0], st[0][1], st[2])
        for t in w:
            st = self.state.get(t)
            if st:
                if st[0]:
                    self._need(eng, needs, st[0][0], st[0][1], st[2])
                for k, v in st[1].items():
                    self._need(eng, needs, k, v)
        return needs

    def _emit_waits(self, eng, needs):
        for k, v in needs.items():
            self.known[eng][k] = v
            sem = self.sems[k]
            self.prog[eng].append(lambda e, sem=sem, v=v: e.wait_ge(sem, v))

    def _commit(self, r, w, key, val, small=True):
        for t in r:
            st = self.state.setdefault(t, [None, {}, True])
            if st[1].get(key, 0) < val:
                st[1][key] = val
        for t in w:
            self.state[t] = [(key, val), {}, small]

    def op(self, eng, fn, r=(), w=(), small=False):
        needs = self._deps(eng, r, w)
        self._emit_waits(eng, needs)
        self.cnt[eng] += 1
        val = self.cnt[eng]
        sem = self.sems[eng]
        self.prog[eng].append(lambda e, fn=fn, sem=sem: fn(e).then_inc(sem, 1))
        self._commit(r, w, eng, val, small)

    def dma(self, q, out, in_, r=(), w=()):
        pool = self.dma_pool[q]
        k = pool[self.dma_rr[q] % len(pool)]
        self.dma_rr[q] += 1
        needs = self._deps(q, r, w)
        prev = 16 * self.dma_cnt[k]
        if prev:
            self._need(q, needs, k, prev)
        self._emit_waits(q, needs)
        self.dma_cnt[k] += 1
        val = 16 * self.dma_cnt[k]
        sem = self.sems[k]
        self.prog[q].append(lambda e, out=out, in_=in_, sem=sem: e.dma_start(out=out, in_=in_).then_inc(sem, 16))
        self._commit(r, w, k, val)

    def barrier(self):
        cur = {e: self.cnt[e] for e in ("pe", "act", "dve", "pool")}
        for k, c in self.dma_cnt.items():
            cur[k] = 16 * c
        for eng in ENGS:
            needs = {}
            for k, v in cur.items():
                if v and k != eng:
                    self._need(eng, needs, k, v)
            self._emit_waits(eng, needs)
        self.state = {}

    def finish(self):
        self.barrier()
        prog = self.prog
        with self.nc.Block() as block:
            @block.tensor
            def _(e):
                for f in prog["pe"]:
                    f(e)

            @block.scalar
            def _(e):
                for f in prog["act"]:
                    f(e)

            @block.vector
            def _(e):
                for f in prog["dve"]:
                    f(e)

            @block.gpsimd
            def _(e):
                for f in prog["pool"]:
                    f(e)

            @block.sync
            def _(e):
                for f in prog["sp"]:
                    f(e)
        for cm in reversed(self._ctx):
            cm.__exit__(None, None, None)


def vec_layout():
    off = {}
    n = 0

    def add(name, cols):
        nonlocal n
        off[name] = n
        n += cols

    for l in range(NL):
        for j in range(6):
            add(("ng", l, j), 8)
        add(("mg", l), 8)
        add(("fcw", l), 3 * 44)
        add(("fcb", l), 44)
    for j in range(2):
        add(("scw", j), 4 * 24)
        add(("scb", j), 24)
        add(("sng", j), 16)
        add(("pw1b", j), 16)
        add(("dww", j), CFK * 8)
        add(("dwb", j), 8)
        add(("lng", j), 8)
        add(("lnb", j), 8)
        add(("pw2b", j), 8)
    return off, n


def _cols(v):
    v = np.asarray(v, dtype=np.float32)
    if v.ndim == 1:
        return np.ascontiguousarray(v.reshape(-1, 128).T)
    k, c = v.shape
    return np.ascontiguousarray(v.reshape(k, c // 128, 128).transpose(2, 0, 1).reshape(128, k * (c // 128)))


def build_vecs(inp):
    off, n = vec_layout()
    vecs = np.zeros((128, n), np.float32)

    def put(name, v):
        c = _cols(v)
        vecs[:, off[name]:off[name] + c.shape[1]] = c

    for l in range(NL):
        for j in range(6):
            put(("ng", l, j), inp["norm_g"][l, j])
        put(("mg", l), inp["xa_mem_g"][l])
        put(("fcw", l), inp["ffn_conv_w"][l])
        put(("fcb", l), inp["ffn_conv_b"][l])
    for j in range(2):
        put(("scw", j), inp["ssm_conv_w"][j])
        put(("scb", j), inp["ssm_conv_b"][j])
        put(("sng", j), inp["ssm_norm_g"][j])
        put(("pw1b", j), inp["cf_pw1_b"][j])
        put(("dww", j), inp["cf_dw_w"][j])
        put(("dwb", j), inp["cf_dw_b"][j])
        put(("lng", j), inp["cf_ln_g"][j])
        put(("lnb", j), inp["cf_ln_b"][j])
        put(("pw2b", j), inp["cf_pw2_b"][j])
    bv = np.zeros((128, 2 * 96), np.float32)
    for j in range(2):
        bv[:, j * 96 + 0:j * 96 + 32] = np.asarray(inp["ssm_dt_bias"][j], np.float32)[None, :]
        bv[:, j * 96 + 32:j * 96 + 64] = np.asarray(inp["ssm_A_log"][j], np.float32)[None, :]
        bv[:, j * 96 + 64:j * 96 + 96] = np.asarray(inp["ssm_D"][j], np.float32)[None, :]
    consts = np.zeros((128, 512), np.float32)
    consts[:, 0:128] = np.eye(128, dtype=np.float32)
    consts[:, 128:256] = np.triu(np.ones((128, 128), np.float32))
    consts[:, 256:384] = np.tril(np.ones((128, 128), np.float32), -1)
    consts[:, 384:512] = 1.0
    return vecs, bv, consts


WEIGHT_SHAPES = {
    "ssm_in_w": [2, D, INP], "ssm_out_w": [2, DI, D],
    "cf_pw1_w": [2, D, 2 * D], "cf_pw2_w": [2, D, D],
    "xa_q_w": [NL, D, D], "xa_kv_w": [NL, D, 2 * D], "xa_o_w": [NL, D, D],
    "ffn_in_w": [NL, D, 2 * DFF], "ffn_out_w": [NL, DFF, D],
}

ALL_STAGES = [(l, s) for l in range(NL) for s in ("mix", "xa", "ffn")]


def build_program(stages=None, same_engine_sync=True, L=L):
    stages = ALL_STAGES if stages is None else stages
    nc = bass.Bass("TRN2", target_bir_lowering=False)
    S = Sched(nc, same_engine_sync=same_engine_sync)
    voff, NV = vec_layout()

    xT = nc.dram_tensor("xT", [D, L], F32, kind="ExternalInput").ap()
    memT = nc.dram_tensor("memT", [D, NM], F32, kind="ExternalInput").ap()
    vecs_d = nc.dram_tensor("vecs", [128, NV], F32, kind="ExternalInput").ap()
    bvecs_d = nc.dram_tensor("bvecs", [128, 192], F32, kind="ExternalInput").ap()
    consts_d = nc.dram_tensor("consts", [128, 512], F32, kind="ExternalInput").ap()
    W = {k: nc.dram_tensor(k, shp, F32, kind="ExternalInput").ap() for k, shp in WEIGHT_SHAPES.items()}
    y = nc.dram_tensor("y", [D, L], F32, kind="ExternalOutput").ap()
    ZS = nc.dram_tensor("zs_scr", [L, DI], BF16).ap()
    XBC = nc.dram_tensor("xbc_scr", [CONVD, L], BF16).ap()
    DTA = nc.dram_tensor("dta_scr", [L, 64], F32).ap()

    vecs = nc.alloc_sbuf_tensor("vecs_sb", [128, NV], F32)
    cst = nc.alloc_sbuf_tensor("cst_sb", [128, 512], F32)
    ident_bf = nc.alloc_sbuf_tensor("ident_bf", [128, 128], BF16)
    ones_bf = nc.alloc_sbuf_tensor("ones_bf", [128, 128], BF16)
    onesD_bf = nc.alloc_sbuf_tensor("onesD_bf", [128, 128], BF16)
    epsc = nc.alloc_sbuf_tensor("epsc", [128, 1], F32)
    onec = nc.alloc_sbuf_tensor("onec", [128, 1], F32)
    ident_f = cst[:, 0:128]
    U_f = cst[:, 128:256]
    Lm_f = cst[:, 256:384]
    ones_f = cst[:, 384:512]

    def _small(ap):
        try:
            return ap.free_size() < 128
        except Exception:
            return True

    def mm(out, lhsT, rhs, start, stop, r, w):
        S.op("pe", lambda e: e.matmul(out, lhsT=lhsT, rhs=rhs, start=start, stop=stop), r, w, small=_small(out))

    def tr(out, in_, r, w):
        S.op("pe", lambda e: e.transpose(out, in_, ident_bf[:]), r, w, small=_small(out))

    def act(out, in_, func, r, w, bias=None, scale=None, accum=None):
        kw = {}
        if bias is not None:
            kw["bias"] = bias
        if scale is not None:
            kw["scale"] = scale
        if accum is not None:
            kw["accum_out"] = accum
        S.op("act", lambda e: e.activation(out=out, in_=in_, func=func, **kw), r, w,
             small=(_small(out) or accum is not None))

    def tt(eng, out, in0, in1, op, r, w):
        S.op(eng, lambda e: e.tensor_tensor(out=out, in0=in0, in1=in1, op=op), r, w, small=_small(out))

    def ts(eng, out, in0, s1, s2, op0, op1, r, w):
        if op1 is None:
            S.op(eng, lambda e: e.tensor_scalar(out=out, in0=in0, scalar1=s1, scalar2=None, op0=op0), r, w, small=_small(out))
        else:
            S.op(eng, lambda e: e.tensor_scalar(out=out, in0=in0, scalar1=s1, scalar2=s2, op0=op0, op1=op1), r, w, small=_small(out))

    def stt(eng, out, in0, scalar, in1, op0, op1, r, w):
        S.op(eng, lambda e: e.scalar_tensor_tensor(out=out, in0=in0, scalar=scalar, in1=in1, op0=op0, op1=op1), r, w, small=_small(out))

    def rsqrt_act(out, in_, scale, r, w):
        act(out, in_, AF.Ln, r=list(r) + ["cbf"], w=w, bias=epsc[:], scale=scale)
        act(out, out, AF.Exp, r=w, w=w, scale=-0.5)

    def cp(eng, out, in_, r, w):
        S.op(eng, lambda e: e.tensor_copy(out=out, in_=in_), r, w, small=_small(out))

    def mset(eng, ap, val, w):
        S.op(eng, lambda e: e.memset(ap, val), (), w, small=_small(ap))

    def vcol(name, c):
        o = voff[name] + c
        return vecs[:, o:o + 1]

    def xr_ap(dram, t0, tl):
        return dram.rearrange("(kt p) l -> p kt l", p=128)[:, :, t0:t0 + tl]

    def xr_tok(t0, tl):
        return [("xr", i) for i in range(t0 // 256, (t0 + tl) // 256)]

    S.dma("sp", vecs[:], vecs_d, w=["vecs"])
    S.dma("sp", cst[:], consts_d, w=["cst"])
    cp("dve", ident_bf[:], ident_f, r=["cst"], w=["cbf"])
    cp("dve", ones_bf[:], ones_f, r=["cst"], w=["cbf"])
    ts("dve", onesD_bf[:], ones_f, 1.0 / D, None, ALU.mult, None, r=["cst"], w=["cbf"])
    mset("dve", epsc[:], EPS, w=["cbf"])
    mset("dve", onec[:], 1.0, w=["cbf"])
    S.barrier()

    class Stage:
        n = 0

        def __init__(self):
            self.es = ExitStack()
            Stage.n += 1
            self.sfx = "_s%d" % Stage.n

        def sb(self, name, shape, dt):
            return self.es.enter_context(nc.sbuf_tensor(name + self.sfx, shape, dt))

        def ps(self, name, shape, dt=F32):
            return self.es.enter_context(nc.psum_tensor(name + self.sfx, shape, dt))

        def close(self):
            S.barrier()
            self.es.close()

    def prenorm(xa, xtok, gname, h, TT, sq, ssps, rstd, htok="h"):
        for kt in range(8):
            act(sq[:, kt, :TT], xa[:, kt, :TT], AF.Square, r=[xtok], w=[("sq", kt)])
        for kt in range(8):
            mm(ssps[:, :TT], onesD_bf[:], sq[:, kt, :TT], kt == 0, kt == 7, r=[("sq", kt)], w=["ssps"])
        rsqrt_act(rstd[:, :TT], ssps[:, :TT], 1.0, r=["ssps"], w=["rstd"])
        for kt in range(8):
            stt("dve", h[:, kt, :TT], xa[:, kt, :TT], vcol(gname, kt), rstd[:, :TT],
                ALU.mult, ALU.mult, r=[xtok, "rstd"], w=[(htok, kt)])

    def prenorm_sq(xa, xtok, TT, sq):
        for kt in range(8):
            act(sq[:, kt, :TT], xa[:, kt, :TT], AF.Square, r=[xtok], w=[("sq", kt)])

    def prenorm_ss(TT, sq, ssps, rstd):
        for kt in range(8):
            mm(ssps[:, :TT], onesD_bf[:], sq[:, kt, :TT], kt == 0, kt == 7, r=[("sq", kt)], w=["ssps"])
        rsqrt_act(rstd[:, :TT], ssps[:, :TT], 1.0, r=["ssps"], w=["rstd"])

    def prenorm_h(xa, xtok, gname, h, TT, rstd, htok="h"):
        for kt in range(8):
            stt("dve", h[:, kt, :TT], xa[:, kt, :TT], vcol(gname, kt), rstd[:, :TT],
                ALU.mult, ALU.mult, r=[xtok, "rstd"], w=[(htok, kt)])

    def post_evac(dt, fps, ftok, TT, fsb, sq, bias=None, pp=""):
        act(fsb[:, dt, :TT], fps, AF.Identity, r=[ftok], w=[("fsb", dt)], bias=bias)
        act(sq[:, dt, :TT], fps, AF.Square, r=[ftok], w=[(pp + "sq", dt)], bias=bias)

    def post_finish(gname, xb, TT, t0, fsb, sq, ssps, rstd, pp="", sstok=None, add_eng="pool"):
        sstok = sstok or (pp + "ssps")
        for kt in range(8):
            mm(ssps[:, :TT], onesD_bf[:], sq[:, kt, :TT], kt == 0, kt == 7, r=[(pp + "sq", kt)], w=[sstok])
        rsqrt_act(rstd[:, :TT], ssps[:, :TT], 1.0, r=[sstok], w=[pp + "rstd"])
        for kt in range(8):
            stt("dve", fsb[:, kt, :TT], fsb[:, kt, :TT], vcol(gname, kt), rstd[:, :TT], ALU.mult, ALU.mult,
                r=[("fsb", kt), pp + "rstd"], w=[("fsb", kt)])
            if add_eng == "pool":
                tt("pool", fsb[:, kt, :TT], fsb[:, kt, :TT], xb[:, kt, :TT], ALU.add, r=[("fsb", kt), "xb"], w=[("fsb", kt)])
        if add_eng != "pool":
            for kt in range(8):
                tt(add_eng, fsb[:, kt, :TT], fsb[:, kt, :TT], xb[:, kt, :TT], ALU.add, r=[("fsb", kt), "xb"], w=[("fsb", kt)])
        S.dma("sp", xr_ap(y, t0, TT), fsb[:, :, :TT], r=[("fsb", kt) for kt in range(8)], w=xr_tok(t0, TT))

    def load_w(dst, src2d, nk, tokname, per=1):
        for k0 in range(0, nk, per):
            k1 = min(nk, k0 + per)
            S.dma("pool", dst[:, k0:k1, :], src2d[k0 * 128:k1 * 128, :].rearrange("(k p) f -> p k f", p=128),
                  w=[(tokname, k) for k in range(k0, k1)])

    def load_wc(dst, src2d, nk, tokname, groups):
        for gi, (c0, c1) in groups:
            S.dma("pool", dst[:, :, c0:c1], src2d[:, c0:c1].rearrange("(k p) f -> p k f", p=128), w=[(tokname, gi)])

    def gtok(tokname, groups, col):
        for gi, (c0, c1) in groups:
            if c0 <= col < c1:
                return (tokname, gi)
        raise ValueError(col)

    def stage_ffn(l, xsrc):
        TT = 256
        st = Stage()
        w1 = st.sb("w1", [128, 8, 2 * DFF], BF16)
        w2 = st.sb("w2", [128, 22, D], BF16)
        xa = st.sb("xa", [128, 8, TT], F32)
        xb = st.sb("xb", [128, 8, TT], F32)
        sq = st.sb("sq", [128, 8, TT], BF16)
        sq2 = st.sb("sq2", [128, 8, TT], BF16)
        h = st.sb("h", [128, 8, TT + 2], BF16)
        hv = h[:, :, 2:2 + TT]
        rstd = st.sb("rstd", [128, TT], F32)
        rstd2 = st.sb("rstd2", [128, TT], F32)
        ga = st.sb("ga", [128, 22, TT], BF16)
        fsb = st.sb("fsb", [128, 8, TT], F32)
        acc = [[st.sb("acc%d%d" % (a, b), [128, TT], F32) for b in range(2)] for a in range(2)]
        sg = [st.sb("sg%d" % b, [128, TT], F32) for b in range(2)]
        ups = [[st.ps("ups%d%d" % (a, b), [128, 512]) for b in range(2)] for a in range(2)]
        fps = [st.ps("fps%d" % b, [128, 512]) for b in range(2)]
        ssps = st.ps("ssps", [128, 512])
        ssps2 = st.ps("ssps2", [128, 512])
        w1g = []
        for (f0, f1) in ((0, 3), (3, 9), (9, 15), (15, 22)):
            for a in range(2):
                w1g.append((len(w1g), (a * DFF + f0 * 128, a * DFF + f1 * 128)))
        load_wc(w1, W["ffn_in_w"][l], 8, "w1", w1g)
        load_w(w2, W["ffn_out_w"][l], 22, "w2", per=11)
        NT = L // TT

        def gate(fc):
            b = fc % 2
            act(sg[b][:], acc[0][b][:], AF.Silu, r=[("acc", 0, b)], w=[("sg", b)])
            tt("pool", ga[:, fc, :], sg[b][:], acc[1][b][:], ALU.mult, r=[("sg", b), ("acc", 1, b)], w=[("ga", fc)])

        mset("pool", h[:, :, 0:2], 0.0, w=["hh"])
        S.dma("sp", xa[:], xr_ap(xsrc, 0, TT), r=xr_tok(0, TT), w=["xa"])
        prenorm(xa, "xa", ("ng", l, 4), hv, TT, sq, ssps, rstd)
        hall = [("h", kt) for kt in range(8)]
        for ti in range(NT):
            t0 = ti * TT
            S.dma("sp", xb[:], xr_ap(xsrc, t0, TT), r=xr_tok(t0, TT), w=["xb"])
            for fc in range(22):
                b = fc % 2
                for a in range(2):
                    col0 = a * DFF + fc * 128
                    cidx = a * 22 + fc
                    up, ac = ups[a][b], acc[a][b]
                    for kt in range(8):
                        mm(up[:, :TT + 2], w1[:, kt, col0:col0 + 128], h[:, kt, :], kt == 0, kt == 7,
                           r=[gtok("w1", w1g, col0), ("h", kt), "hh"], w=[("ups", a, b)])
                    act(ac[:], up[:, 2:2 + TT], AF.Identity, r=[("ups", a, b)], w=[("acc", a, b)],
                        scale=vcol(("fcw", l), 2 * 44 + cidx), bias=vcol(("fcb", l), cidx))
                if fc > 0:
                    gate(fc - 1)
                for k in (1, 0):
                    for a in range(2):
                        cidx = a * 22 + fc
                        up, ac = ups[a][b], acc[a][b]
                        stt("dve", ac[:], up[:, k:k + TT], vcol(("fcw", l), k * 44 + cidx), ac[:], ALU.mult, ALU.add,
                            r=[("ups", a, b), ("acc", a, b)], w=[("acc", a, b)])
            gate(21)
            if ti + 1 < NT:
                cp("pool", h[:, :, 0:2], h[:, :, TT:TT + 2], r=hall, w=["hh"])
                S.dma("sp", xa[:], xr_ap(xsrc, t0 + TT, TT), r=xr_tok(t0 + TT, TT), w=["xa"])
                prenorm(xa, "xa", ("ng", l, 4), hv, TT, sq, ssps, rstd)
            for dt in range(8):
                fp = fps[dt % 2]
                for fc in range(22):
                    mm(fp[:, :TT], w2[:, fc, dt * 128:(dt + 1) * 128], ga[:, fc, :], fc == 0, fc == 21,
                       r=[("w2", fc), ("ga", fc)], w=[("fps", dt % 2)])
                post_evac(dt, fp[:, :TT], ("fps", dt % 2), TT, fsb, sq2, pp="p")
            post_finish(("ng", l, 5), xb, TT, t0, fsb, sq2, ssps2, rstd2, pp="p")
        st.close()

    def stage_xattn(l, xsrc):
        TT = 512
        st = Stage()
        wq = st.sb("wq", [128, 8, D], BF16)
        wkv = st.sb("wkv", [128, 8, 2 * D], BF16)
        wo = st.sb("wo", [128, 8, D], BF16)
        memf = st.sb("memf", [128, 8, NM], F32)
        memn = st.sb("memn", [128, 8, NM], BF16)
        KT = st.sb("KT", [128, 8, NM], BF16)
        V = st.sb("V", [128, 2, D], BF16)
        xa = st.sb("xa", [128, 8, TT], F32)
        xb = st.sb("xb", [128, 8, TT], F32)
        sq = st.sb("sq", [128, 8, TT], BF16)
        h = st.sb("h", [128, 8, TT], BF16)
        rstd = st.sb("rstd", [128, TT], F32)
        qT = st.sb("qT", [128, 8, TT], BF16)
        eT = [st.sb("eT%d" % b, [128, 2, TT], BF16) for b in range(2)]
        rden = [st.sb("rden%d" % b, [128, TT], F32) for b in range(2)]
        oT = st.sb("oT", [128, 8, TT], BF16)
        fsb = st.sb("fsb", [128, 8, TT], F32)
        ssps = st.ps("ssps", [128, 512])
        qps = [st.ps("qps%d" % b, [128, 512]) for b in range(2)]
        sps = [st.ps("sps%d" % b, [128, 512]) for b in range(2)]
        dps = st.ps("dps", [128, 512])
        ops_ = [st.ps("ops%d" % b, [128, 512]) for b in range(2)]
        g512 = lambda n: [(i, (i * 512, (i + 1) * 512)) for i in range(n)]
        wkvg, wqg, wog = g512(4), g512(2), g512(2)
        load_wc(wkv, W["xa_kv_w"][l], 8, "wkv", wkvg)
        load_wc(wq, W["xa_q_w"][l], 8, "wq", wqg)
        load_wc(wo, W["xa_o_w"][l], 8, "wo", wog)
        S.dma("sp", memf[:], memT.rearrange("(kt p) m -> p kt m", p=128), w=["memf"])
        prenorm(memf, "memf", ("mg", l), memn, NM, sq, ssps, rstd)
        for dt in range(8):
            qp = qps[dt % 2]
            for kt in range(8):
                mm(qp[:, :NM], wkv[:, kt, dt * 128:(dt + 1) * 128], memn[:, kt, :], kt == 0, kt == 7,
                   r=[gtok("wkv", wkvg, dt * 128), ("h", kt)], w=[("qps", dt % 2)])
            act(KT[:, dt, :], qp[:, :NM], AF.Copy, r=[("qps", dt % 2)], w=["KT"])
        for mt in range(2):
            for nb in range(2):
                qp = qps[nb]
                for kt in range(8):
                    mm(qp[:], memn[:, kt, mt * 128:(mt + 1) * 128], wkv[:, kt, D + nb * 512:D + (nb + 1) * 512],
                       kt == 0, kt == 7, r=[gtok("wkv", wkvg, D + nb * 512), ("h", kt)], w=[("qps", nb)])
                act(V[:, mt, nb * 512:(nb + 1) * 512], qp[:], AF.Copy, r=[("qps", nb)], w=["V"])
        sq2 = st.sb("sq2", [128, 8, TT], BF16)
        rstd2 = st.sb("rstd2", [128, TT], F32)
        NT = L // TT

        def pre(ti):
            S.dma("sp", xa[:], xr_ap(xsrc, ti * TT, TT), r=xr_tok(ti * TT, TT), w=["xa"])
            prenorm(xa, "xa", ("ng", l, 2), h, TT, sq, ssps, rstd)

        pre(0)
        S.dma("sp", xb[:], xr_ap(xsrc, 0, TT), r=xr_tok(0, TT), w=["xb"])
        for ti in range(NT):
            t0 = ti * TT
            for dt in range(8):
                qp = qps[dt % 2]
                for kt in range(8):
                    mm(qp[:], wq[:, kt, dt * 128:(dt + 1) * 128], h[:, kt, :], kt == 0, kt == 7,
                       r=[gtok("wq", wqg, dt * 128), ("h", kt)], w=[("qps", dt % 2)])
                act(qT[:, dt, :], qp[:], AF.Identity, r=[("qps", dt % 2)], w=[("qT", dt)], scale=1.0 / 16.0)
            for hd in range(4):
                b = hd % 2
                for mt in range(2):
                    for j in range(2):
                        mm(sps[mt][:], KT[:, 2 * hd + j, mt * 128:(mt + 1) * 128], qT[:, 2 * hd + j, :], j == 0, j == 1,
                           r=["KT", ("qT", 2 * hd + j)], w=[("sps", mt)])
                    act(eT[b][:, mt, :], sps[mt][:], AF.Exp, r=[("sps", mt)], w=[("eT", b, mt)])
                for mt in range(2):
                    mm(dps[:], ones_bf[:], eT[b][:, mt, :], mt == 0, mt == 1, r=[("eT", b, mt)], w=["dps"])
                act(rden[b][:], dps[:], AF.Ln, r=["dps"], w=[("rden", b)])
                act(rden[b][:], rden[b][:], AF.Exp, r=[("rden", b)], w=[("rden", b)], scale=-1.0)
                for j in range(2):
                    op_ = ops_[j]
                    for mt in range(2):
                        mm(op_[:], V[:, mt, (2 * hd + j) * 128:(2 * hd + j + 1) * 128], eT[b][:, mt, :], mt == 0, mt == 1,
                           r=["V", ("eT", b, mt)], w=[("ops", j)])
                    tt("dve", oT[:, 2 * hd + j, :], op_[:], rden[b][:], ALU.mult, r=[("ops", j), ("rden", b)],
                       w=[("oT", 2 * hd + j)])
            if ti + 1 < NT:
                pre(ti + 1)
            for dt in range(8):
                qp = qps[dt % 2]
                for kt in range(8):
                    mm(qp[:], wo[:, kt, dt * 128:(dt + 1) * 128], oT[:, kt, :], kt == 0, kt == 7,
                       r=[gtok("wo", wog, dt * 128), ("oT", kt)], w=[("qps", dt % 2)])
                post_evac(dt, qp[:], ("qps", dt % 2), TT, fsb, sq2, pp="p")
            post_finish(("ng", l, 3), xb, TT, t0, fsb, sq2, dps, rstd2, pp="p", sstok="dps")
            if ti + 1 < NT:
                S.dma("sp", xb[:], xr_ap(xsrc, t0 + TT, TT), r=xr_tok(t0 + TT, TT), w=["xb"])
        st.close()

    def stage_conf(l, xsrc):
        j = l // 2
        TT = 256
        HL = CFK - 1
        NT = L // TT
        st = Stage()
        pw1 = st.sb("pw1", [128, 8, 2 * D], BF16)
        pw2 = st.sb("pw2", [128, 8, D], BF16)
        dg = st.sb("dg", [128, CFK * 8, 128], BF16)
        xa = st.sb("xa", [128, 8, TT], F32)
        xb = st.sb("xb", [128, 8, TT], F32)
        sq = st.sb("sq", [128, 8, TT], BF16)
        sqc = st.sb("sqc", [128, 8, TT], BF16)
        sq2 = st.sb("sq2", [128, 8, TT], BF16)
        h = st.sb("h", [128, 8, TT], BF16)
        rstd = st.sb("rstd", [128, TT], F32)
        rstdc = st.sb("rstdc", [128, TT], F32)
        rstd2 = st.sb("rstd2", [128, TT], F32)
        sig = [st.sb("sig%d" % b, [128, TT], F32) for b in range(2)]
        glu = [st.sb("glu%d" % b, [128, 8, HL + TT], BF16) for b in range(2)]
        cs = st.sb("cs", [128, 8, TT], F32)
        cb = st.sb("cb", [128, 8, TT], BF16)
        mean = st.sb("mean", [128, TT], F32)
        var = st.sb("var", [128, TT], F32)
        t1 = [st.sb("t1%d" % b, [128, TT], F32) for b in range(2)]
        sn = st.sb("sn", [128, 8, TT], BF16)
        fsb = st.sb("fsb", [128, 8, TT], F32)
        ssps = st.ps("ssps", [128, 512])
        aps = [st.ps("aps%d" % b, [128, 512]) for b in range(2)]
        gps = [st.ps("gps%d" % b, [128, 512]) for b in range(2)]
        cps = [st.ps("cps%d" % b, [128, 512]) for b in range(2)]
        mps = st.ps("mps", [128, 512])
        pw1g = [(0, (0, 512)), (1, (D, D + 512)), (2, (512, D)), (3, (D + 512, 2 * D))]
        load_wc(pw1, W["cf_pw1_w"][j], 8, "pw1", pw1g)
        load_w(pw2, W["cf_pw2_w"][j], 8, "pw2", per=4)
        dwo = voff[("dww", j)]
        for c in range(8):
            tt("dve", dg[:, c * CFK:(c + 1) * CFK, :],
               ident_f.unsqueeze(1).to_broadcast([128, CFK, 128]),
               vecs[:, dwo + c:dwo + c + CFK * 8:8].unsqueeze(2).to_broadcast([128, CFK, 128]), ALU.mult,
               r=["cst"], w=[("dg", c)])
        for c in range(8):
            mset("pool", glu[1][:, c, TT:TT + HL], 0.0, w=[("glu", 1, c)])

        def pre(ti):
            t0 = ti * TT
            S.dma("sp", xa[:], xr_ap(xsrc, t0, TT), r=xr_tok(t0, TT), w=["xa"])
            prenorm(xa, "xa", ("ng", l, 0), h, TT, sq, ssps, rstd)

        def pw1glu(ti):
            gb = ti % 2
            G_, Gp = glu[gb], glu[1 - gb]
            for c in range(8):
                b = c % 2
                for kt in range(8):
                    mm(aps[b][:, :TT], pw1[:, kt, c * 128:(c + 1) * 128], h[:, kt, :], kt == 0, kt == 7,
                       r=[gtok("pw1", pw1g, c * 128), ("h", kt)], w=[("aps", b)])
                for kt in range(8):
                    mm(gps[b][:, :TT], pw1[:, kt, D + c * 128:D + (c + 1) * 128], h[:, kt, :], kt == 0, kt == 7,
                       r=[gtok("pw1", pw1g, D + c * 128), ("h", kt)], w=[("gps", b)])
                act(sig[b][:], gps[b][:, :TT], AF.Sigmoid, r=[("gps", b)], w=[("sig", b)], bias=vcol(("pw1b", j), 8 + c))
                cp("pool", G_[:, c, 0:HL], Gp[:, c, TT:TT + HL], r=[("glu", 1 - gb, c)], w=[("gluh", gb, c)])
                stt("dve", G_[:, c, HL:HL + TT], aps[b][:, :TT], vcol(("pw1b", j), c), sig[b][:], ALU.add, ALU.mult,
                    r=[("aps", b), ("sig", b)], w=[("glu", gb, c)])

        def conv(ti):
            gb = ti % 2
            G_ = glu[gb]
            for c in range(8):
                b = c % 2
                for k in range(CFK):
                    mm(cps[b][:, :TT], dg[:, c * CFK + k, :], G_[:, c, k:k + TT], k == 0, k == CFK - 1,
                       r=[("dg", c), ("glu", gb, c), ("gluh", gb, c)], w=[("cps", b)])
                act(cs[:, c, :], cps[b][:, :TT], AF.Identity, r=[("cps", b)], w=[("cs", c)], bias=vcol(("dwb", j), c))
                act(sqc[:, c, :], cps[b][:, :TT], AF.Square, r=[("cps", b)], w=[("sqc", c)], bias=vcol(("dwb", j), c))
                act(cb[:, c, :], cps[b][:, :TT], AF.Identity, r=[("cps", b)], w=[("cb", c)], bias=vcol(("dwb", j), c))

        def lnorm(ti):
            for c in range(8):
                mm(mps[:, 0:TT], onesD_bf[:], cb[:, c, :], c == 0, c == 7, r=[("cb", c)], w=["mps"])
            for c in range(8):
                mm(mps[:, TT:2 * TT], onesD_bf[:], sqc[:, c, :], c == 0, c == 7, r=[("sqc", c)], w=["mps"])
            cp("dve", mean[:], mps[:, 0:TT], r=["mps"], w=["mean"])
            tt("dve", var[:], mean[:], mean[:], ALU.mult, r=["mean"], w=["var"])
            tt("dve", var[:], mps[:, TT:2 * TT], var[:], ALU.subtract, r=["mps", "var"], w=["var"])
            rsqrt_act(rstdc[:], var[:], 1.0, r=["var"], w=["rstdc"])
            for c in range(8):
                b = c % 2
                tt("pool", t1[b][:], cs[:, c, :], mean[:], ALU.subtract, r=[("cs", c), "mean"], w=[("t1", b)])
                stt("dve", t1[b][:], t1[b][:], vcol(("lng", j), c), rstdc[:], ALU.mult, ALU.mult,
                    r=[("t1", b), "rstdc"], w=[("t1", b)])
                act(sn[:, c, :], t1[b][:], AF.Silu, r=[("t1", b)], w=[("sn", c)], bias=vcol(("lnb", j), c))

        def pw2post(ti):
            t0 = ti * TT
            for dt in range(8):
                b = dt % 2
                for c in range(8):
                    mm(cps[b][:, :TT], pw2[:, c, dt * 128:(dt + 1) * 128], sn[:, c, :], c == 0, c == 7,
                       r=[("pw2", c), ("sn", c)], w=[("cps", b)])
                post_evac(dt, cps[b][:, :TT], ("cps", b), TT, fsb, sq2, bias=vcol(("pw2b", j), dt), pp="p")
            post_finish(("ng", l, 1), xb, TT, t0, fsb, sq2, ssps, rstd2, pp="p", sstok="ssps")

        pre(0)
        S.dma("sp", xb[:], xr_ap(xsrc, 0, TT), r=xr_tok(0, TT), w=["xb"])
        pw1glu(0)
        for ti in range(NT):
            conv(ti)
            if ti + 1 < NT:
                pre(ti + 1)
                pw1glu(ti + 1)
            lnorm(ti)
            pw2post(ti)
            if ti + 1 < NT:
                S.dma("sp", xb[:], xr_ap(xsrc, (ti + 1) * TT, TT), r=xr_tok((ti + 1) * TT, TT), w=["xb"])
        st.close()

    def stage_ssd_a(l, xsrc, wout=None):
        j = l // 2
        TT = 512
        st = Stage()
        win = st.sb("win", [128, 8, INP], BF16)
        xa = st.sb("xa", [128, 8, TT], F32)
        sq = st.sb("sq", [128, 8, TT], BF16)
        hh_ = [st.sb("h%d" % b, [128, 8, TT], BF16) for b in range(2)]
        rstd = st.sb("rstd", [128, TT], F32)
        zs = [st.sb("zs%d" % b, [128, DI], BF16) for b in range(2)]
        ub = [st.sb("ub%d" % b, [128, TT + 3], F32) for b in range(2)]
        acc = [st.sb("acc%d" % b, [128, TT], F32) for b in range(2)]
        xbs = [st.sb("xbs%d" % b, [128, TT], BF16) for b in range(4)]
        halo = st.sb("halo", [128, 24, 3], F32)
        bv = st.sb("bv", [128, 96], F32)
        Ab = st.sb("Ab", [128, 32], F32)
        dtt = [st.sb("dtt%d" % b, [128, 32], F32) for b in range(2)]
        dta = [st.sb("dta%d" % b, [128, 64], F32) for b in range(2)]
        ssps = st.ps("ssps", [128, 512])
        zps = [st.ps("zps%d" % b, [128, 512]) for b in range(2)]
        ups = [st.ps("ups%d" % b, [128, 512]) for b in range(2)]
        dps = st.ps("dps", [128, 512])
        wing = [(i, (i * 512, (i + 1) * 512)) for i in range(4)] + [(4, (DI + CONVD, INP))] + \
               [(5 + i, (DI + i * 512, DI + (i + 1) * 512)) for i in range(6)]
        load_wc(win, W["ssm_in_w"][j], 8, "win", wing)
        if wout is not None:
            load_w(wout, W["ssm_out_w"][j], 16, "wout", per=4)
        S.dma("sp", bv[:], bvecs_d[:, j * 96:(j + 1) * 96], w=["bv"])
        act(Ab[:], bv[:, 32:64], AF.Exp, r=["bv"], w=["Ab"])
        ts("dve", Ab[:], Ab[:], -1.0, None, ALU.mult, None, r=["Ab"], w=["Ab"])
        mset("pool", halo[:], 0.0, w=[("halo", c) for c in range(24)])
        NT = L // TT

        def pre(ti):
            S.dma("sp", xa[:], xr_ap(xsrc, ti * TT, TT), r=xr_tok(ti * TT, TT), w=["xa"])
            prenorm(xa, "xa", ("ng", l, 0), hh_[ti % 2], TT, sq, ssps, rstd, htok="h%d" % (ti % 2))

        pre(0)
        for ti in range(NT):
            t0 = ti * TT
            h = hh_[ti % 2]
            ht = "h%d" % (ti % 2)
            for ck in range(4):
                cg = ti * 4 + ck
                zb = zs[ck % 2]
                for nb in range(4):
                    zp = zps[nb % 2]
                    for kt in range(8):
                        mm(zp[:], h[:, kt, ck * 128:(ck + 1) * 128], win[:, kt, nb * 512:(nb + 1) * 512], kt == 0, kt == 7,
                           r=[gtok("win", wing, nb * 512), (ht, kt)], w=[("zps", nb % 2)])
                    act(zb[:, nb * 512:(nb + 1) * 512], zp[:], AF.Silu, r=[("zps", nb % 2)], w=[("zs", ck % 2)])
                S.dma("sp", ZS[cg * 128:(cg + 1) * 128, :], zb[:], r=[("zs", ck % 2)], w=[("zsd", cg)])
                for kt in range(8):
                    mm(dps[:, 0:32], h[:, kt, ck * 128:(ck + 1) * 128], win[:, kt, DI + CONVD:INP], kt == 0, kt == 7,
                       r=[("win", 4), (ht, kt)], w=["dps"])
                db = ck % 2
                tt("dve", dtt[db][:], dps[:, 0:32], bv[:, 0:32], ALU.add, r=["dps", "bv"], w=[("dtt", db)])
                act(dtt[db][:], dtt[db][:], AF.Exp, r=[("dtt", db)], w=[("dtt", db)])
                act(dta[db][:, 0:32], dtt[db][:], AF.Ln, r=[("dtt", db)], w=[("dta", db)], bias=onec[:])
                tt("dve", dta[db][:, 32:64], dta[db][:, 0:32], Ab[:], ALU.mult, r=[("dta", db), "Ab"], w=[("dta", db)])
                S.dma("sp", DTA[cg * 128:(cg + 1) * 128, :], dta[db][:], r=[("dta", db)], w=[("dtad", cg)])
            if ti + 1 < NT:
                pre(ti + 1)
            def finish_c(c):
                b = c % 2
                xs = xbs[c % 4]
                act(xs[:], acc[b][:], AF.Silu, r=[("acc", b)], w=[("xbs", c % 4)])
                S.dma("sp", XBC[c * 128:(c + 1) * 128, t0:t0 + TT], xs[:], r=[("xbs", c % 4)], w=[("xbcd", ti, c)])

            for c in range(24):
                b = c % 2
                col0 = DI + c * 128
                for kt in range(8):
                    mm(ups[b][:], win[:, kt, col0:col0 + 128], h[:, kt, :], kt == 0, kt == 7,
                       r=[gtok("win", wing, col0), (ht, kt)], w=[("ups", b)])
                u, ac = ub[b], acc[b]
                cp("pool", u[:, 0:3], halo[:, c, :], r=[("halo", c)], w=[("ubh", b)])
                act(u[:, 3:3 + TT], ups[b][:], AF.Copy, r=[("ups", b)], w=[("ub", b)])
                act(ac[:], ups[b][:], AF.Identity, r=[("ups", b)], w=[("acc", b)],
                    scale=vcol(("scw", j), 3 * 24 + c), bias=vcol(("scb", j), c))
                cp("pool", halo[:, c, :], u[:, TT:TT + 3], r=[("ub", b)], w=[("halo", c)])
                if c > 0:
                    finish_c(c - 1)
                for k in (2, 1, 0):
                    stt("dve", ac[:], u[:, k:k + TT], vcol(("scw", j), k * 24 + c), ac[:], ALU.mult, ALU.add,
                        r=[("ub", b), ("ubh", b), ("acc", b)], w=[("acc", b)])
            finish_c(23)
            if ti == NT - 1 and wout is not None:
                for c in range(16):
                    act(wout[:, c, :], wout[:, c, :], AF.Identity, r=[("wout", c)], w=[("wout", c)], scale=vcol(("sng", j), c))
        st.close()

    def stage_ssd_b(l, xsrc, wout_pre=None):
        j = l // 2
        TT = 256
        CPT = TT // 128
        NT = L // TT
        NCH = L // 128
        st = Stage()
        wout = wout_pre if wout_pre is not None else st.sb("wout", [128, 16, D], BF16)
        xbc = [st.sb("xbc%d" % b, [128, 24, TT], BF16) for b in range(2)]
        zsc = [st.sb("zsc%d" % b, [128, DI], BF16) for b in range(2)]
        dta = [st.sb("dtac%d" % b, [128, 64], F32) for b in range(2)]
        bv = st.sb("bv", [128, 96], F32)
        xtm = [st.sb("xtm%d" % b, [128, DI], BF16) for b in range(2)]
        xdt = [st.sb("xdt%d" % b, [128, DI], BF16) for b in range(2)]
        xdd = [st.sb("xdd%d" % b, [128, DI], BF16) for b in range(2)]
        xD = [st.sb("xD%d" % b, [128, DI], BF16) for b in range(2)]
        Btm = [st.sb("Btm%d" % b, [128, 512], BF16) for b in range(2)]
        cs_s = [st.sb("cs_s%d" % b, [128, 32], F32) for b in range(2)]
        ecs = [st.sb("ecs%d" % b, [128, 32], F32) for b in range(2)]
        dte = [st.sb("dte%d" % b, [128, 32], F32) for b in range(2)]
        dec = [st.sb("dec%d" % b, [128, 32], F32) for b in range(2)]
        CBm = [st.sb("CBm%d" % b, [128, 4, 128], BF16) for b in range(2)]
        R = [st.sb("R%d" % b, [128, 8, 128], F32) for b in range(2)]
        es_ = [[st.sb("es%d%d" % (a, b), [128, 512], BF16) for b in range(2)] for a in range(2)]
        M = [[st.sb("M%d%d" % (a, b), [128, 4, 128], BF16) for b in range(2)] for a in range(2)]
        stf = st.sb("stf", [128, 4, 512], F32)
        stb = st.sb("stb", [128, 4, 512], BF16)
        t1 = [st.sb("t1%d" % b, [128, 512], F32) for b in range(2)]
        ytm = st.sb("ytm", [128, DI], F32)
        ss1 = st.sb("ss1", [128, 1], F32)
        rs1 = st.sb("rs1", [128, 1], F32)
        yn = st.sb("yn", [128, DI], BF16)
        ynT = st.sb("ynT", [128, 16, TT], BF16)
        xb = st.sb("xb", [128, 8, TT], F32)
        fsb = st.sb("fsb", [128, 8, TT], F32)
        sq = st.sb("sq", [128, 8, TT], BF16)
        rstd = st.sb("rstd", [128, TT], F32)
        tp = st.ps("tp", [128, DI], BF16)
        smp = st.ps("smp", [128, 512])
        cbp = st.ps("cbp", [128, 512])
        sgp = [st.ps("sgp%d" % b, [128, 512]) for b in range(2)]
        ydp = st.ps("ydp", [128, 512])
        yop = st.ps("yop", [128, 512])
        if wout_pre is None:
            load_w(wout, W["ssm_out_w"][j], 16, "wout", per=4)
            for c in range(16):
                ts("dve" if c % 2 == 0 else "pool", wout[:, c, :], wout[:, c, :], vcol(("sng", j), c), None, ALU.mult, None,
                   r=[("wout", c)], w=[("wout", c)])
        S.dma("sp", bv[:], bvecs_d[:, j * 96:(j + 1) * 96], w=["bv"])
        mset("dve", stf[:], 0.0, w=[("stf", g) for g in range(NG)])
        mset("pool", stb[:], 0.0, w=[("stb", g) for g in range(NG)])
        hq = "p (h q) -> p h q"

        def load_tile(ti):
            t0 = ti * TT
            for c0 in range(0, 24, 8):
                S.dma("sp", xbc[ti % 2][:, c0:c0 + 8, :],
                      XBC[c0 * 128:(c0 + 8) * 128, t0:t0 + TT].rearrange("(c p) t -> p c t", p=128),
                      r=[("xbcd", (t0 // 512), c) for c in range(c0, c0 + 8)],
                      w=[("xbc", ti % 2, c) for c in range(c0, c0 + 8)])

        def front(c):
            cb, ti, ck = c % 2, c // CPT, c % CPT
            X = xbc[ti % 2]
            sl = slice(ck * 128, (ck + 1) * 128)
            S.dma("sp", zsc[cb][:], ZS[c * 128:(c + 1) * 128, :], r=[("zsd", c)], w=[("zsc", cb)])
            S.dma("sp", dta[cb][:], DTA[c * 128:(c + 1) * 128, :], r=[("dtad", c)], w=[("dtac", cb)])
            dtv = dta[cb][:, 0:32]
            av = dta[cb][:, 32:64]
            for g in range(NG):
                tr(tp[:, g * 128:(g + 1) * 128], X[:, 16 + g, sl], r=[("xbc", ti % 2, 16 + g)], w=["tp"])
            cp("dve", Btm[cb][:], tp[:, 0:512], r=["tp"], w=[("Btm", cb)])
            for ch in range(16):
                tr(tp[:, ch * 128:(ch + 1) * 128], X[:, ch, sl], r=[("xbc", ti % 2, ch)], w=["tp"])
            cp("dve", xtm[cb][:], tp[:], r=["tp"], w=[("xtm", cb)])
            tt("dve", xdt[cb][:].rearrange(hq, q=64), tp[:].rearrange(hq, q=64),
               dtv.unsqueeze(2).to_broadcast([128, 32, 64]), ALU.mult, r=["tp", ("dtac", cb)], w=[("xdt", cb)])
            tt("pool", xD[cb][:].rearrange(hq, q=64), xtm[cb][:].rearrange(hq, q=64),
               bv[:, 64:96].unsqueeze(2).to_broadcast([128, 32, 64]), ALU.mult, r=[("xtm", cb), "bv"], w=[("xD", cb)])
            mm(smp[:, 0:32], U_f, av, True, True, r=["cst", ("dtac", cb)], w=["smp"])
            mm(smp[:, 32:64], ones_f, av, True, True, r=["cst", ("dtac", cb)], w=["smp"])
            cp("dve", cs_s[cb][:], smp[:, 0:32], r=["smp"], w=[("cs_s", cb)])
            act(ecs[cb][:], smp[:, 0:32], AF.Exp, r=["smp"], w=[("ecs", cb)])
            tt("dve", dte[cb][:], smp[:, 32:64], cs_s[cb][:], ALU.subtract, r=["smp", ("cs_s", cb)], w=[("dte", cb)])
            act(dte[cb][:], dte[cb][:], AF.Exp, r=[("dte", cb)], w=[("dte", cb)])
            act(dec[cb][:], smp[:, 32:64], AF.Exp, r=["smp"], w=[("dec", cb)])
            tt("pool", xdd[cb][:].rearrange(hq, q=64), xdt[cb][:].rearrange(hq, q=64),
               dte[cb][:].unsqueeze(2).to_broadcast([128, 32, 64]), ALU.mult, r=[("xdt", cb), ("dte", cb)], w=[("xdd", cb)])
            for g in range(NG):
                mm(cbp[:, g * 128:(g + 1) * 128], X[:, 16 + g, sl], X[:, 20 + g, sl], True, True,
                   r=[("xbc", ti % 2, 16 + g), ("xbc", ti % 2, 20 + g)], w=["ssps"])
            tt("dve", CBm[cb][:], cbp[:].rearrange("p (g i) -> p g i", g=4), U_f.unsqueeze(1).to_broadcast([128, 4, 128]),
               ALU.mult, r=["ssps", "cst"], w=[("CBm", cb)])

        def build_R_c(c, g):
            cb = c % 2
            av = dta[cb][:, 32:64]
            tt("pool", R[g % 2][:], U_f.unsqueeze(1).to_broadcast([128, 8, 128]),
               av[:, g * 8:(g + 1) * 8].unsqueeze(2).to_broadcast([128, 8, 128]), ALU.mult,
               r=["cst", ("dtac", cb)], w=[("R", g % 2)])

        def decay_c(c, g, hh):
            cb = c % 2
            gb = g % 2
            mm(sgp[hh][:], Lm_f, R[gb][:, hh * 4:(hh + 1) * 4, :].rearrange("p h i -> p (h i)"), True, True,
               r=["cst", ("R", gb)], w=[("sgp", hh)])
            act(es_[gb][hh][:], sgp[hh][:], AF.Exp, r=[("sgp", hh)], w=[("es", gb, hh)])
            tt("dve", M[gb][hh][:], es_[gb][hh][:].rearrange("p (h i) -> p h i", h=4),
               CBm[cb][:, g, :].unsqueeze(1).to_broadcast([128, 4, 128]), ALU.mult,
               r=[("es", gb, hh), ("CBm", cb)], w=[("M", gb, hh)])

        def decay0(c):
            build_R_c(c, 0)
            decay_c(c, 0, 0)
            decay_c(c, 0, 1)

        def middle(c):
            cb, ti, ck = c % 2, c // CPT, c % CPT
            X = xbc[ti % 2]
            sl = slice(ck * 128, (ck + 1) * 128)
            av = dta[cb][:, 32:64]

            def build_R(g):
                build_R_c(c, g)

            def decay(g, hh):
                decay_c(c, g, hh)

            decay0(c)
            for g in range(NG):
                gb = g % 2
                ydp_g, ydtok = (ydp, "ydp") if gb == 0 else (smp, "smp")
                if g + 1 < NG:
                    build_R(g + 1)
                for hh in range(2):
                    for hl in range(4):
                        hg = g * 8 + hh * 4 + hl
                        o0 = (hh * 4 + hl) * 64
                        mm(ydp_g[:, o0:o0 + 64], M[gb][hh][:, hl, :], xdt[cb][:, hg * 64:(hg + 1) * 64], True, False,
                           r=[("M", gb, hh), ("xdt", cb)], w=[ydtok])
                        mm(ydp_g[:, o0:o0 + 64], ident_bf[:], xD[cb][:, hg * 64:(hg + 1) * 64], False, True,
                           r=["cbf", ("xD", cb)], w=[ydtok])
                    if g + 1 < NG:
                        decay(g + 1, hh)
                mm(yop[:], X[:, 20 + g, sl], stb[:, g, :], True, True, r=[("xbc", ti % 2, 20 + g), ("stb", g)], w=["yop"])
                mm(cbp[:], Btm[cb][:, g * 128:(g + 1) * 128], xdd[cb][:, g * 512:(g + 1) * 512], True, True,
                   r=[("Btm", cb), ("xdd", cb)], w=["ssps"])
                tt("dve", t1[gb][:].rearrange(hq, q=64), yop[:].rearrange(hq, q=64),
                   ecs[cb][:, g * 8:(g + 1) * 8].unsqueeze(2).to_broadcast([128, 8, 64]), ALU.mult,
                   r=["yop", ("ecs", cb)], w=[("t1", gb)])
                tt("dve", stf[:, g, :].rearrange(hq, q=64), stf[:, g, :].rearrange(hq, q=64),
                   dec[cb][:, g * 8:(g + 1) * 8].unsqueeze(2).to_broadcast([128, 8, 64]), ALU.mult,
                   r=[("stf", g), ("dec", cb)], w=[("stf", g)])
                tt("dve", ytm[:, g * 512:(g + 1) * 512], t1[gb][:], ydp_g[:], ALU.add, r=[("t1", gb), ydtok], w=[("ytm", g)])
                tt("dve", stf[:, g, :], stf[:, g, :], cbp[:], ALU.add, r=[("stf", g), "ssps"], w=[("stf", g)])
                act(stb[:, g, :], stf[:, g, :], AF.Copy, r=[("stf", g)], w=[("stb", g)])

        def back1(c):
            cb = c % 2
            ytoks = [("ytm", g) for g in range(NG)]
            tt("dve", ytm[:], ytm[:], zsc[cb][:], ALU.mult, r=ytoks + [("zsc", cb)], w=ytoks)
            mset("dve", ss1[:], 0.0, w=["ss1"])
            act(yn[:], ytm[:], AF.Square, r=ytoks + ["ss1"], w=["yn", "ss1"], accum=ss1[:])
            rsqrt_act(rs1[:], ss1[:], 1.0 / DI, r=["ss1"], w=["rs1"])
            act(yn[:], ytm[:], AF.Identity, r=ytoks + ["rs1"], w=["yn"], scale=rs1[:])

        def back2(c):
            ti, ck = c // CPT, c % CPT
            sl = slice(ck * 128, (ck + 1) * 128)
            for ch in range(16):
                tr(tp[:, ch * 128:(ch + 1) * 128], yn[:, ch * 128:(ch + 1) * 128], r=["yn"], w=["tp"])
            for ch in range(16):
                cp("dve", ynT[:, ch, sl], tp[:, ch * 128:(ch + 1) * 128], r=["tp"], w=[("ynT", ch)])
            if ck == CPT - 1:
                t0 = ti * TT
                for dt in range(8):
                    fp = sgp[dt % 2]
                    for ch in range(16):
                        mm(fp[:, :TT], wout[:, ch, dt * 128:(dt + 1) * 128], ynT[:, ch, :], ch == 0, ch == 15,
                           r=[("wout", ch), ("ynT", ch)], w=[("sgp", dt % 2)])
                    post_evac(dt, fp[:, :TT], ("sgp", dt % 2), TT, fsb, sq)
                pending.append(ti)

        pending = []

        def flush_post():
            while pending:
                ti = pending.pop(0)
                t0 = ti * TT
                post_finish(("ng", l, 1), xb, TT, t0, fsb, sq, cbp, rstd, add_eng="dve")
                if ti + 1 < NT:
                    S.dma("sp", xb[:], xr_ap(xsrc, t0 + TT, TT), r=xr_tok(t0 + TT, TT), w=["xb"])

        load_tile(0)
        S.dma("sp", xb[:], xr_ap(xsrc, 0, TT), r=xr_tok(0, TT), w=["xb"])
        front(0)
        for c in range(NCH):
            if c % CPT == 0 and c // CPT + 1 < NT:
                load_tile(c // CPT + 1)
            middle(c)
            back1(c)
            if c + 1 < NCH:
                front(c + 1)
            flush_post()
            back2(c)
        flush_post()
        st.close()

    first = True
    for (l, s) in stages:
        xsrc = xT if first else y
        if s == "mix":
            if l % 2 == 0:
                with nc.sbuf_tensor("wout_l%d" % l, [128, 16, D], BF16) as wout_t:
                    stage_ssd_a(l, xsrc, wout_t)
                    stage_ssd_b(l, xsrc, wout_t)
            else:
                stage_conf(l, xsrc)
        elif s == "ssda":
            stage_ssd_a(l, xsrc)
        elif s == "ssdb":
            stage_ssd_b(l, xsrc)
        elif s == "xa":
            stage_xattn(l, xsrc)
        else:
            stage_ffn(l, xsrc)
        first = False
    S.finish()
    return nc


_PROGRAM_CACHE = {}


def make_in_maps(inp, L=L):
    vecs, bv, consts = build_vecs(inp)
    x = np.asarray(inp["x"], np.float32)[:, :L]
    mem = np.asarray(inp["mem"], np.float32)
    shared = {"vecs": vecs, "bvecs": bv, "consts": consts}
    for k in WEIGHT_SHAPES:
        shared[k] = np.ascontiguousarray(np.asarray(inp[k], np.float32))
    maps = []
    for b in range(x.shape[0]):
        m = dict(shared)
        m["xT"] = np.ascontiguousarray(x[b].T)
        m["memT"] = np.ascontiguousarray(mem[b].T)
        maps.append(m)
    return maps


def kernel(**inputs):
    maps = make_in_maps(inputs)
    nc = build_program()
    res = run_bass_kernel_spmd(nc, maps, core_ids=list(range(8)))
    out = np.stack([np.asarray(r["y"], np.float32).T for r in res.results], axis=0)
    return np.ascontiguousarray(out)
```
